# Optimizing a Trainium2 kernel written in Bass

```python
import math
import jax, jax.numpy as jnp
from jax import lax
import numpy as np

D_MODEL = 1024
BATCH = 4
SEQ = 8192
DEPTH = 2

N_META = 16
ROPE_THETA = 500000.0
BLOCK_Q = 128
TOPK_MAX = 256
NEG = -1e30
A_HEADS = 8
A_KV_HEADS = 2
A_HEAD_DIM = 64
A_ROT = A_HEAD_DIM // 4
IDX_HEADS = 8
IDX_DIM = 64
IDX_ROT = IDX_DIM // 4
B_HEADS = 8
B_NOPE = 64
B_ROPE = 32
B_V = 64
B_Q_LORA = 256
B_KV_LORA = 128
D_FF = 2816
DN_ALPHA = (2 * DEPTH) ** 0.25
DN_BETA = (8 * DEPTH) ** -0.25

IN_SPLITS = (
    A_HEADS * A_HEAD_DIM,
    A_KV_HEADS * A_HEAD_DIM,
    A_KV_HEADS * A_HEAD_DIM,
    IDX_HEADS * IDX_DIM,
    IDX_DIM,
    IDX_HEADS,
    B_Q_LORA,
    B_KV_LORA,
    B_ROPE,
    D_MODEL,
    D_MODEL,
)
IN_WIDTH = sum(IN_SPLITS)
IN_OFFSETS = tuple(sum(IN_SPLITS[: i + 1]) for i in range(len(IN_SPLITS) - 1))

kernel_name = "hybrid_dsa_mla_macaron_deepnorm"


def layer_norm(x, g, b, eps=1e-5):
    xf = x.astype(jnp.float32)
    mu = jnp.mean(xf, axis=-1, keepdims=True)
    var = jnp.mean(jnp.square(xf - mu), axis=-1, keepdims=True)
    y = (xf - mu) * lax.rsqrt(var + eps) * g.astype(jnp.float32) + b.astype(jnp.float32)
    return y.astype(x.dtype)


def rms_norm(x, g, eps=1e-6):
    xf = x.astype(jnp.float32)
    y = xf * lax.rsqrt(jnp.mean(jnp.square(xf), axis=-1, keepdims=True) + eps) * g.astype(jnp.float32)
    return y.astype(x.dtype)


def rope(x, pos):
    r = x.shape[-1]
    half = r // 2
    inv = ROPE_THETA ** (-(jnp.arange(half, dtype=jnp.float32) * 2.0 / r))
    ang = pos[:, None] * inv[None, :]
    cos = jnp.cos(ang)[:, None, :]
    sin = jnp.sin(ang)[:, None, :]
    xf = x.astype(jnp.float32)
    x1, x2 = xf[..., :half], xf[..., half:]
    out = jnp.concatenate([x1 * cos - x2 * sin, x2 * cos + x1 * sin], axis=-1)
    return out.astype(x.dtype)


def partial_rope(x, pos, rot):
    return jnp.concatenate([rope(x[..., :rot], pos), x[..., rot:]], axis=-1)


def swiglu(x, w13, w2):
    gte, up = jnp.split(x @ w13, 2, axis=-1)
    return (jax.nn.silu(gte) * up) @ w2


def to_blocks(t):
    b, lp = t.shape[:2]
    t = t.reshape(b, lp // BLOCK_Q, BLOCK_Q, *t.shape[2:])
    return jnp.moveaxis(t, 1, 0)


def from_blocks(t):
    t = jnp.moveaxis(t, 0, 1)
    return t.reshape(t.shape[0], -1, *t.shape[3:])


def dsa_sparse_attention(q, k, v, q_idx, k_idx, w_idx, topk):
    b, lp = q.shape[:2]
    grp = A_HEADS // A_KV_HEADS
    key_pos = jnp.arange(lp)
    k_idx_f = k_idx.astype(jnp.float32)

    def block(args):
        qb, qib, wb, start = args
        qpos = start + jnp.arange(BLOCK_Q)
        causal = key_pos[None, :] <= qpos[:, None]
        rel = jax.nn.relu(jnp.einsum('bqhd,bkd->bqhk', qib.astype(jnp.float32), k_idx_f) * (IDX_DIM ** -0.5))
        score = jnp.einsum('bqh,bqhk->bqk', wb.astype(jnp.float32) * (IDX_HEADS ** -0.5), rel)
        score = jnp.where(causal[None], score, NEG)
        _, sel = lax.top_k(score, topk)
        valid = sel <= qpos[None, :, None]
        k_sel = jax.vmap(lambda kb, ib: kb[ib])(k, sel)
        v_sel = jax.vmap(lambda vb, ib: vb[ib])(v, sel)
        qg = qb.reshape(b, BLOCK_Q, A_KV_HEADS, grp, A_HEAD_DIM)
        s = jnp.einsum('bqhgd,bqkhd->bqhgk', qg, k_sel).astype(jnp.float32) * (A_HEAD_DIM ** -0.5)
        s = jnp.where(valid[:, :, None, None, :], s, NEG)
        p = jax.nn.softmax(s, axis=-1).astype(v.dtype)
        o = jnp.einsum('bqhgk,bqkhd->bqhgd', p, v_sel)
        return o.reshape(b, BLOCK_Q, A_HEADS * A_HEAD_DIM)

    starts = jnp.arange(lp // BLOCK_Q, dtype=jnp.int32) * BLOCK_Q
    out = lax.map(block, (to_blocks(q), to_blocks(q_idx), to_blocks(w_idx), starts))
    return from_blocks(out)


def mla_attention(q_nope, q_rope, k_nope, k_rope, v):
    b, lp = q_nope.shape[:2]
    key_pos = jnp.arange(lp)
    scale = (B_NOPE + B_ROPE) ** -0.5

    def block(args):
        qn, qr, start = args
        qpos = start + jnp.arange(BLOCK_Q)
        causal = key_pos[None, :] <= qpos[:, None]
        s = (jnp.einsum('bqhd,bkhd->bhqk', qn, k_nope)
             + jnp.einsum('bqhd,bkd->bhqk', qr, k_rope)).astype(jnp.float32) * scale
        s = jnp.where(causal[None, None], s, NEG)
        p = jax.nn.softmax(s, axis=-1).astype(v.dtype)
        o = jnp.einsum('bhqk,bkhd->bqhd', p, v)
        return o.reshape(b, BLOCK_Q, B_HEADS * B_V)

    starts = jnp.arange(lp // BLOCK_Q, dtype=jnp.int32) * BLOCK_Q
    out = lax.map(block, (to_blocks(q_nope), to_blocks(q_rope), starts))
    return from_blocks(out)


def hybrid_layer(h, pos, topk, ln_g, ln_b, ffn1_w13, ffn1_w2, w_in, mla_q_norm, mla_kv_norm,
                 mla_w_uq, mla_w_ukv, w_branch_a, w_branch_b, w_out, ffn2_w13, ffn2_w2):
    b, lp, _ = h.shape
    h = layer_norm(DN_ALPHA * h + 0.5 * swiglu(h, ffn1_w13, ffn1_w2), ln_g[0], ln_b[0])

    z = h @ w_in
    (a_q, a_k, a_v, i_q, i_k, i_w, b_cq, b_ckv, b_kr, g_a, g_b) = jnp.split(z, IN_OFFSETS, axis=-1)

    a_q = partial_rope(a_q.reshape(b, lp, A_HEADS, A_HEAD_DIM), pos, A_ROT)
    a_k = partial_rope(a_k.reshape(b, lp, A_KV_HEADS, A_HEAD_DIM), pos, A_ROT)
    a_v = a_v.reshape(b, lp, A_KV_HEADS, A_HEAD_DIM)
    i_q = partial_rope(i_q.reshape(b, lp, IDX_HEADS, IDX_DIM), pos, IDX_ROT)
    i_k = partial_rope(i_k.reshape(b, lp, 1, IDX_DIM), pos, IDX_ROT)[:, :, 0]
    o_a = dsa_sparse_attention(a_q, a_k, a_v, i_q, i_k, i_w, topk)

    c_q = rms_norm(b_cq, mla_q_norm)
    q = (c_q @ mla_w_uq).reshape(b, lp, B_HEADS, B_NOPE + B_ROPE)
    q_nope, q_rope = q[..., :B_NOPE], rope(q[..., B_NOPE:], pos)
    c_kv = rms_norm(b_ckv, mla_kv_norm)
    kv = (c_kv @ mla_w_ukv).reshape(b, lp, B_HEADS, B_NOPE + B_V)
    k_nope, v = kv[..., :B_NOPE], kv[..., B_NOPE:]
    k_rope = rope(b_kr[:, :, None, :], pos)[:, :, 0]
    o_b = mla_attention(q_nope, q_rope, k_nope, k_rope, v)

    mixed = jax.nn.sigmoid(g_a) * (o_a @ w_branch_a) + jax.nn.sigmoid(g_b) * (o_b @ w_branch_b)
    h = layer_norm(DN_ALPHA * h + mixed @ w_out, ln_g[1], ln_b[1])

    h = layer_norm(DN_ALPHA * h + 0.5 * swiglu(h, ffn2_w13, ffn2_w2), ln_g[2], ln_b[2])
    return h


def setup_inputs(seed: int = 0) -> dict:
    key = jax.random.key(seed)
    ks = jax.random.split(key, 16)
    f32 = jnp.float32

    def nrm(k, shape, fan_in, scale=1.0):
        return jax.random.normal(k, shape, f32) * (fan_in ** -0.5) * scale

    return {
        "x": jax.random.normal(ks[0], (BATCH, SEQ, D_MODEL), f32),
        "meta_tokens": jax.random.normal(ks[1], (N_META, D_MODEL), f32),
        "ln_g": 1.0 + 0.02 * jax.random.normal(ks[2], (DEPTH, 3, D_MODEL), f32),
        "ln_b": 0.02 * jax.random.normal(ks[3], (DEPTH, 3, D_MODEL), f32),
        "ffn1_w13": nrm(ks[4], (DEPTH, D_MODEL, 2 * D_FF), D_MODEL),
        "ffn1_w2": nrm(ks[5], (DEPTH, D_FF, D_MODEL), D_FF, DN_BETA),
        "w_in": nrm(ks[6], (DEPTH, D_MODEL, IN_WIDTH), D_MODEL),
        "mla_q_norm": 1.0 + 0.02 * jax.random.normal(ks[7], (DEPTH, B_Q_LORA), f32),
        "mla_kv_norm": 1.0 + 0.02 * jax.random.normal(ks[8], (DEPTH, B_KV_LORA), f32),
        "mla_w_uq": nrm(ks[9], (DEPTH, B_Q_LORA, B_HEADS * (B_NOPE + B_ROPE)), B_Q_LORA),
        "mla_w_ukv": nrm(ks[10], (DEPTH, B_KV_LORA, B_HEADS * (B_NOPE + B_V)), B_KV_LORA),
        "w_branch_a": nrm(ks[11], (DEPTH, A_HEADS * A_HEAD_DIM, D_MODEL), A_HEADS * A_HEAD_DIM),
        "w_branch_b": nrm(ks[12], (DEPTH, B_HEADS * B_V, D_MODEL), B_HEADS * B_V),
        "w_out": nrm(ks[13], (DEPTH, D_MODEL, D_MODEL), D_MODEL, DN_BETA),
        "ffn2_w13": nrm(ks[14], (DEPTH, D_MODEL, 2 * D_FF), D_MODEL),
        "ffn2_w2": nrm(ks[15], (DEPTH, D_FF, D_MODEL), D_FF, DN_BETA),
    }


def reference(x, meta_tokens, ln_g, ln_b, ffn1_w13, ffn1_w2, w_in, mla_q_norm, mla_kv_norm,
              mla_w_uq, mla_w_ukv, w_branch_a, w_branch_b, w_out, ffn2_w13, ffn2_w2):
    b, s, d = x.shape
    total = s + N_META
    lp = -(-total // BLOCK_Q) * BLOCK_Q
    meta = jnp.broadcast_to(meta_tokens[None].astype(x.dtype), (b, N_META, d))
    h = jnp.concatenate([meta, x, jnp.zeros((b, lp - total, d), x.dtype)], axis=1)
    pos = jnp.arange(lp, dtype=jnp.float32)
    topk = min(TOPK_MAX, s // 4)
    for layer in range(DEPTH):
        h = hybrid_layer(h, pos, topk, ln_g[layer], ln_b[layer], ffn1_w13[layer], ffn1_w2[layer],
                         w_in[layer], mla_q_norm[layer], mla_kv_norm[layer], mla_w_uq[layer],
                         mla_w_ukv[layer], w_branch_a[layer], w_branch_b[layer], w_out[layer],
                         ffn2_w13[layer], ffn2_w2[layer])
    return h[:, N_META:N_META + s]
```

```python
import numpy as np
import ml_dtypes
from contextlib import ExitStack
import concourse.bass as bass
import concourse.mybir as mybir
from concourse.bass_utils import run_bass_kernel_spmd

F32 = mybir.dt.float32
BF16 = mybir.dt.bfloat16
U8 = mybir.dt.uint8
FP8 = mybir.dt.float8e4
AF = mybir.ActivationFunctionType
ALU = mybir.AluOpType

P = 128
T = 384
NTL = 11
NBK = 33
NT = NBK * 128
NBLK = 66
D = 1024
DFF = 2816
N_META = 16
SEQ = 8192
BATCH = 4
DEPTH = 2
ALPHA = float((2 * DEPTH) ** 0.25)
THETA = 500000.0
NEGM = -1.0e4
NA = 3272
IDX_SCALE = float(64 ** -0.5 * 8 ** -0.5)
BIS_R = 16.0
BIS_IT = 17


class Buf:
    __slots__ = ("w", "r", "name")

    def __init__(self, name=""):
        self.w = {}
        self.r = {}
        self.name = name


class TL:
    __slots__ = ("ap", "b")

    def __init__(self, ap, name=""):
        self.ap = ap
        self.b = Buf(name)


class Em:
    def __init__(self, nc):
        self.nc = nc
        self.engs = {"pe": nc.tensor, "act": nc.scalar, "dve": nc.vector, "pool": nc.gpsimd, "sp": nc.sync}
        self.sems = {}
        self.cnt = {}
        self.seen = {k: {} for k in self.engs}
        self.grp = {}
        for k in self.engs:
            self.sems[k] = nc.alloc_semaphore("s_" + k)
            self.cnt[k] = 0
        self.nwait = 0
        self.nins = 0

    def _wait(self, eng, k, c):
        if self.seen[eng].get(k, 0) < c:
            self.engs[eng].wait_ge(self.sems[k], c)
            self.seen[eng][k] = c
            self.nwait += 1

    def _deps(self, eng, reads, writes):
        need = {}
        for b in reads:
            for k, c in b.w.items():
                if need.get(k, 0) < c:
                    need[k] = c
        for b in writes:
            for d in (b.w, b.r):
                for k, c in d.items():
                    if need.get(k, 0) < c:
                        need[k] = c
        for k, c in need.items():
            if k == eng and eng == "pe":
                continue
            self._wait(eng, k, c)

    def op(self, eng, fn, reads=(), writes=()):
        self._deps(eng, reads, writes)
        ins = fn(self.engs[eng])
        self.cnt[eng] += 1
        self.nins += 1
        ins.then_inc(self.sems[eng], 1)
        c = self.cnt[eng]
        for b in writes:
            b.w[eng] = c
            b.r = {}
        for b in reads:
            if b.r.get(eng, 0) < c:
                b.r[eng] = c
        return ins

    def dma(self, q, grp, out, in_, reads=(), writes=(), nslots=2):
        st = self.grp.setdefault(grp, [0, nslots])
        slot = st[0] % st[1]
        st[0] += 1
        key = (grp, slot)
        if key not in self.sems:
            self.sems[key] = self.nc.alloc_semaphore("d_%s_%d" % (grp, slot))
            self.cnt[key] = 0
        if self.cnt[key] > 0:
            self._wait(q, key, self.cnt[key])
        self._deps(q, reads, writes)
        ins = self.engs[q].dma_start(out=out, in_=in_)
        self.cnt[key] += 16
        self.nins += 1
        ins.then_inc(self.sems[key], 16)
        c = self.cnt[key]
        for b in writes:
            b.w[key] = c
            b.r = {}
        for b in reads:
            if b.r.get(key, 0) < c:
                b.r[key] = c
        return ins

    def coll(self, in_ap, out_ap, reads=(), writes=()):
        q, key = "pool", "cc"
        if key not in self.sems:
            self.sems[key] = self.nc.alloc_semaphore("s_cc")
            self.cnt[key] = 0
        if self.cnt[key] > 0:
            self._wait(q, key, self.cnt[key])
        self._deps(q, reads, writes)
        ins = self.engs[q].collective_compute("AllGather", ALU.bypass, replica_groups=[[0, 1], [2, 3], [4, 5], [6, 7]],
                                              ins=[in_ap], outs=[out_ap])
        self.cnt[key] += 1
        self.nins += 1
        ins.then_inc(self.sems[key])
        c = self.cnt[key]
        for b in writes:
            b.w[key] = c
            b.r = {}
        for b in reads:
            if b.r.get(key, 0) < c:
                b.r[key] = c
        return ins

    def barrier(self, engines=None):
        for e in (engines or list(self.engs)):
            for k, c in self.cnt.items():
                if k != e and c > 0:
                    self._wait(e, k, c)


class DT:
    def __init__(self, ap, n=NTL, name=""):
        self.ap = ap
        self.bufs = [Buf(name + str(i)) for i in range(n)]


def build_program():
    nc = bass.Bass("TRN2", target_bir_lowering=False)
    em = Em(nc)
    dts = {}

    def dram(name, shape, dt, kind, n=NTL):
        t = DT(nc.dram_tensor(name, list(shape), dt, kind=kind).ap(), n=n, name=name)
        dts[name] = t
        return t

    def din(name, shape, dt, n=NTL):
        return dram(name, shape, dt, "ExternalInput", n)

    def dout(name, shape, dt, n=NTL):
        return dram(name, shape, dt, "ExternalOutput", n)

    gs = ExitStack()
    uid = [0]

    def uq(name):
        uid[0] += 1
        return "%s__u%d" % (name, uid[0])

    def galloc(name, shape, dt):
        return TL(gs.enter_context(nc.sbuf_tensor(name, list(shape), dt)).ap(), name)

    ones_f = galloc("ones_f", [128, 128], F32)
    ident_b = galloc("ident_b", [128, 128], BF16)
    eps5 = galloc("eps5", [128, 1], F32)
    eps6 = galloc("eps6", [128, 1], F32)
    m240 = galloc("m240", [128, 1], F32)
    em.op("pool", lambda e: e.memset(ones_f.ap, 1.0), writes=[ones_f.b])
    em.op("pool", lambda e: e.memset(eps5.ap, 1e-5), writes=[eps5.b])
    em.op("pool", lambda e: e.memset(eps6.ap, 1e-6), writes=[eps6.b])
    em.op("pool", lambda e: e.memset(m240.ap, -240.0), writes=[m240.b])
    em.op("pool", lambda e: e.memset(ident_b.ap, 0.0), writes=[ident_b.b])
    em.op("pool", lambda e: e.affine_select(out=ident_b.ap, in_=ident_b.ap, pattern=[[-1, 128]],
                                            compare_op=ALU.not_equal, fill=1.0, base=0, channel_multiplier=1),
          reads=[ident_b.b], writes=[ident_b.b])

    def small_in(name, shape):
        d = din(name, shape, F32, n=1)
        t = galloc(name + "_sb", shape, F32)
        em.dma("sp", "ld_small", t.ap, d.ap, reads=[d.bufs[0]], writes=[t.b])
        return t

    def mm(out_tl, out_ap, lhsT, rhs, reads, start, stop):
        em.op("pe", lambda e: e.matmul(out_ap, lhsT=lhsT, rhs=rhs, start=start, stop=stop),
              reads=reads, writes=[out_tl.b])

    def act(out_tl, out_ap, in_ap, func, reads, scale=None, bias=None):
        kw = {}
        if scale is not None:
            kw["scale"] = scale
        if bias is not None:
            kw["bias"] = bias
        em.op("act", lambda e: e.activation(out=out_ap, in_=in_ap, func=func, **kw), reads=reads, writes=[out_tl.b])

    def tt(eng, out_tl, out_ap, in0, in1, op, reads):
        em.op(eng, lambda e: e.tensor_tensor(out=out_ap, in0=in0, in1=in1, op=op), reads=reads, writes=[out_tl.b])

    def ts(eng, out_tl, out_ap, in0, s1, op0, reads, s2=None, op1=None, accum=None, extra_w=()):
        kw = {}
        if op1 is not None:
            kw["op1"] = op1
        if accum is not None:
            kw["accum_out"] = accum
        em.op(eng, lambda e: e.tensor_scalar(out=out_ap, in0=in0, scalar1=s1, scalar2=s2, op0=op0, **kw),
              reads=reads, writes=[out_tl.b] + list(extra_w))

    def stt(out_tl, out_ap, in0, scalar, in1, op0, op1, reads):
        em.op("dve", lambda e: e.scalar_tensor_tensor(out=out_ap, in0=in0, scalar=scalar, in1=in1, op0=op0, op1=op1),
              reads=reads, writes=[out_tl.b])

    def copy(eng, out_tl, out_ap, in_ap, reads):
        if eng == "act":
            act(out_tl, out_ap, in_ap, AF.Copy, reads)
        else:
            em.op(eng, lambda e: e.tensor_copy(out=out_ap, in_=in_ap), reads=reads, writes=[out_tl.b])

    cast_rr = [0]

    def load_w(es_alloc, d, dst, kc, ncols, stg, kp=128):
        W = 704
        for k in range(kc):
            for c0 in range(0, ncols, W):
                cw = min(W, ncols - c0)
                s = stg[cast_rr[0] % len(stg)]
                eng = ("dve", "act")[cast_rr[0] % 2]
                cast_rr[0] += 1
                em.dma("sp", "ld_w", s.ap[0:kp, 0:cw], d.ap[k * kp:(k + 1) * kp, c0:c0 + cw],
                       reads=[d.bufs[0]], writes=[s.b], nslots=3)
                copy(eng, dst, dst.ap[0:kp, k, c0:c0 + cw], s.ap[0:kp, 0:cw], reads=[s.b])

    def ln_stats(xr, ps1, ps2, sq):
        for j in range(8):
            mm(ps1, ps1.ap[:, 0:T], ones_f.ap, xr[j].ap, [ones_f.b, xr[j].b], j == 0, j == 7)
        for j in range(8):
            s = sq[j % 2]
            act(s, s.ap, xr[j].ap, AF.Square, [xr[j].b])
            mm(ps2, ps2.ap[:, 0:T], ones_f.ap, s.ap, [ones_f.b, s.b], j == 0, j == 7)

    def ln_apply(xr, lnp, li, ps1, ps2, tmp, st):
        mean, msq, var, rstd, bt = st["mean"], st["msq"], st["var"], st["rstd"], st["bt"]
        act(mean, mean.ap, ps1.ap[:, 0:T], AF.Copy, [ps1.b], scale=1.0 / D)
        act(msq, msq.ap, mean.ap, AF.Square, [mean.b])
        stt(var, var.ap, ps2.ap[:, 0:T], 1.0 / D, msq.ap, ALU.mult, ALU.subtract, [ps2.b, msq.b])
        act(var, var.ap, var.ap, AF.Sqrt, [var.b, eps5.b], bias=eps5.ap[:, 0:1])
        em.op("dve", lambda e: e.reciprocal(out=rstd.ap, in_=var.ap), reads=[var.b], writes=[rstd.b])
        stt(bt, bt.ap, mean.ap, -1.0, rstd.ap, ALU.mult, ALU.mult, [mean.b, rstd.b])
        for j in range(8):
            t = tmp[j % 2]
            tt("dve", t, t.ap, xr[j].ap, rstd.ap, ALU.mult, [xr[j].b, rstd.b])
            tt("pool", t, t.ap, t.ap, bt.ap, ALU.add, [t.b, bt.b])
            act(xr[j], xr[j].ap, t.ap, AF.Identity, [t.b, lnp.b],
                scale=lnp.ap[:, li * 16 + j:li * 16 + j + 1], bias=lnp.ap[:, li * 16 + 8 + j:li * 16 + 8 + j + 1])

    def layer_norm(xr, lnp, li, ps1, ps2, sq, tmp, st):
        ln_stats(xr, ps1, ps2, sq)
        ln_apply(xr, lnp, li, ps1, ps2, tmp, st)

    def stage_ffn(hin_d, hout_d, w13_d, w2_d, lnp, li):
        with ExitStack() as es:
            def sb(name, shape, dt):
                return TL(es.enter_context(nc.sbuf_tensor(uq(name), list(shape), dt)).ap(), name)

            def ps(name):
                return TL(es.enter_context(nc.psum_tensor(uq(name), [128, 512], F32)).ap(), name)

            w13b = sb("w13b", [128, 8, 2 * DFF], BF16)
            w2b = sb("w2b", [128, 22, D], BF16)
            stg = [sb("stg0", [128, 704], F32), sb("stg1", [128, 704], F32), sb("stg2", [128, 704], F32)]
            load_w(es, w13_d, w13b, 8, 2 * DFF, stg)
            load_w(es, w2_d, w2b, 22, D, stg)
            X = [[sb("x%d_%d" % (b_, j), [128, T], F32) for j in range(8)] for b_ in range(2)]
            hb = sb("hb", [128, 8, T], BF16)
            G = sb("G", [128, 22, T], BF16)
            sg = [sb("sg0", [128, T], F32), sb("sg1", [128, T], F32)]
            sq = [sb("sq0", [128, T], F32), sb("sq1", [128, T], F32)]
            st = {n: sb("ln_" + n, [128, T], F32) for n in ("mean", "var", "bt")}
            st["msq"] = st["bt"]
            st["rstd"] = st["var"]
            tmpb = [sb("tmpb0", [128, T], F32), sb("tmpb1", [128, T], F32)]
            pg = [ps("pg0"), ps("pg1")]
            pu = [ps("pu0"), ps("pu1")]
            py = [ps("py0"), ps("py1")]
            ps1 = ps("ps1")
            ps2 = ps("ps2")

            def ph_a_hb(m):
                n0 = m * T
                em.dma("pool", "ld_hb", hb.ap, hin_d.ap[:, n0:n0 + T].rearrange("(k p) t -> p k t", p=128),
                       reads=[hin_d.bufs[m]], writes=[hb.b], nslots=2)

            def ph_a_x(m):
                n0 = m * T
                xb = X[m % 2]
                for j in range(8):
                    em.dma("sp", "ld_h", xb[j].ap, hin_d.ap[j * 128:(j + 1) * 128, n0:n0 + T],
                           reads=[hin_d.bufs[m]], writes=[xb[j].b], nslots=4)

            def ph_b(m):
                for c in range(22):
                    g_, u_ = pg[c % 2], pu[c % 2]
                    for k in range(8):
                        mm(g_, g_.ap[:, 0:T], w13b.ap[:, k, c * 128:(c + 1) * 128], hb.ap[:, k, :], [w13b.b, hb.b], k == 0, k == 7)
                    for k in range(8):
                        mm(u_, u_.ap[:, 0:T], w13b.ap[:, k, DFF + c * 128:DFF + (c + 1) * 128], hb.ap[:, k, :], [w13b.b, hb.b], k == 0, k == 7)
                    s_ = sg[c % 2]
                    act(s_, s_.ap, g_.ap[:, 0:T], AF.Silu, [g_.b])
                    tt("dve", G, G.ap[:, c, :], s_.ap, u_.ap[:, 0:T], ALU.mult, [s_.b, u_.b])

            def ph_c(m):
                xb = X[m % 2]
                for j in range(8):
                    y_ = py[j % 2]
                    for k in range(22):
                        mm(y_, y_.ap[:, 0:T], w2b.ap[:, k, j * 128:(j + 1) * 128], G.ap[:, k, :], [w2b.b, G.b], k == 0, k == 21)
                    act(xb[j], xb[j].ap, xb[j].ap, AF.Copy, [xb[j].b], scale=ALPHA)
                    stt(xb[j], xb[j].ap, y_.ap[:, 0:T], 0.5, xb[j].ap, ALU.mult, ALU.add, [y_.b, xb[j].b])

            def ph_d1(m):
                ln_stats(X[m % 2], ps1, ps2, sq)

            def ph_d2(m):
                n0 = m * T
                xb = X[m % 2]
                ln_apply(xb, lnp, li, ps1, ps2, tmpb, st)
                for j in range(8):
                    em.dma("sp", "st_h", hout_d.ap[j * 128:(j + 1) * 128, n0:n0 + T], xb[j].ap,
                           reads=[xb[j].b], writes=[hout_d.bufs[m]], nslots=4)

            ph_a_hb(0)
            ph_a_x(0)
            ph_b(0)
            for m in range(NTL):
                if m + 1 < NTL:
                    ph_a_hb(m + 1)
                ph_c(m)
                if m >= 1:
                    ph_d2(m - 1)
                if m + 1 < NTL:
                    ph_a_x(m + 1)
                    ph_b(m + 1)
                ph_d1(m)
            ph_d2(NTL - 1)
            em.barrier()

    def stage_proj(h1_d, wA_d, wuq_d, wuk_d, wuv_d, nrm, tabs, outs):
        c128_d, s128_d, c96_d, s96_d = tabs
        with ExitStack() as es:
            def sb(name, shape, dt):
                return TL(es.enter_context(nc.sbuf_tensor(uq(name), list(shape), dt)).ap(), name)

            def ps(name):
                return TL(es.enter_context(nc.psum_tensor(uq(name), [128, 512], F32)).ap(), name)

            wAb = sb("wAb", [128, 8, NA], BF16)
            wuqb = sb("wuqb", [128, 2, 1536], BF16)
            wukb = sb("wukb", [128, 1, 768], BF16)
            wuvb = sb("wuvb", [128, 1, 512], BF16)
            stg = [sb("stg0", [128, 704], F32), sb("stg1", [128, 704], F32), sb("stg2", [128, 704], F32)]
            load_w(es, wA_d, wAb, 8, NA, stg)
            load_w(es, wuq_d, wuqb, 2, 1536, stg)
            load_w(es, wuk_d, wukb, 1, 768, stg)
            load_w(es, wuv_d, wuvb, 1, 512, stg)
            hin = sb("hin", [128, 8, T], F32)
            hb = sb("hb", [128, 8, T], BF16)
            c128 = sb("c128", [128, T], F32)
            s128 = sb("s128", [128, T], F32)
            c96 = sb("c96", [96, T], F32)
            s96 = sb("s96", [96, T], F32)
            t1 = [sb("t1_0", [128, T], F32), sb("t1_1", [128, T], F32)]
            t2 = [sb("t2_0", [128, T], F32), sb("t2_1", [128, T], F32)]
            aq_st = sb("aq_st", [128, 4, T], BF16)
            iq_st = sb("iq_st", [128, 4, T], BF16)
            ak_st = sb("ak_st", [128, T], BF16)
            ik_st = sb("ik_st", [128, T], BF16)
            mq_st = sb("mq_st", [96, 8, T], BF16)
            mk_st = sb("mk_st", [96, 8, T], BF16)
            av_st = sb("av_st", [128, 3, 130], BF16)
            mv_st = sb("mv_st", [128, 3, 8, 65], BF16)
            iw_st = sb("iw_st", [128, 3, 8], F32)
            cq_sb = sb("cq_sb", [128, 2, T], F32)
            ckv_sb = sb("ckv_sb", [128, T], F32)
            sqs = [sb("sqs0", [128, T], F32), sb("sqs1", [128, T], F32)]
            rs = sb("rs", [128, T], F32)
            cqn = sb("cqn", [128, 2, T], BF16)
            ckvn = sb("ckvn", [128, T], BF16)
            bs = sb("bs", [96, T], F32)
            pA = [ps("pA0"), ps("pA1")]
            pB = [ps("pB0"), ps("pB1")]
            pC = [ps("pC0"), ps("pC1")]
            pS = ps("pS")
            em.op("pool", lambda e: e.memset(av_st.ap, 1.0), writes=[av_st.b])
            em.op("pool", lambda e: e.memset(mv_st.ap, 1.0), writes=[mv_st.b])
            rr = [0]

            def proj(out_ps, col0, M):
                for k in range(8):
                    mm(out_ps, out_ps.ap[0:M, 0:T], wAb.ap[:, k, col0:col0 + M], hb.ap[:, k, :], [wAb.b, hb.b], k == 0, k == 7)

            def roped(colA, colB, dst_tl, dst_ap):
                i = rr[0] % 2
                rr[0] += 1
                a_, b_ = pA[i], pB[i]
                proj(a_, colA, 128)
                proj(b_, colB, 128)
                tt("dve", t1[i], t1[i].ap, a_.ap[:, 0:T], c128.ap, ALU.mult, [a_.b, c128.b])
                tt("dve", t2[i], t2[i].ap, b_.ap[:, 0:T], s128.ap, ALU.mult, [b_.b, s128.b])
                tt("pool", dst_tl, dst_ap, t1[i].ap, t2[i].ap, ALU.add, [t1[i].b, t2[i].b])

            for m in range(NTL):
                n0 = m * T
                em.dma("sp", "ld_h", hin.ap, h1_d.ap[:, n0:n0 + T].rearrange("(k p) t -> p k t", p=128),
                       reads=[h1_d.bufs[m]], writes=[hin.b])
                for (tl_, d_) in ((c128, c128_d), (s128, s128_d), (c96, c96_d), (s96, s96_d)):
                    em.dma("sp", "ld_tab", tl_.ap, d_.ap[:, n0:n0 + T], reads=[d_.bufs[0]], writes=[tl_.b], nslots=4)
                copy("dve", hb, hb.ap[:, 0:4, :], hin.ap[:, 0:4, :], [hin.b])
                copy("pool", hb, hb.ap[:, 4:8, :], hin.ap[:, 4:8, :], [hin.b])
                for r in range(4):
                    roped(r * 128, 512 + r * 128, aq_st, aq_st.ap[:, r, :])
                roped(1024, 1152, ak_st, ak_st.ap)
                for r in range(4):
                    roped(1408 + r * 128, 1920 + r * 128, iq_st, iq_st.ap[:, r, :])
                roped(2432, 2560, ik_st, ik_st.ap)
                for c in range(3):
                    p_ = pC[c % 2]
                    for k in range(8):
                        mm(p_, p_.ap[:, 0:128], hb.ap[:, k, c * 128:(c + 1) * 128], wAb.ap[:, k, 1280:1408], [wAb.b, hb.b], k == 0, k == 7)
                    copy("act", av_st, av_st.ap[:, c, 0:64], p_.ap[:, 0:64], [p_.b])
                    copy("act", av_st, av_st.ap[:, c, 65:129], p_.ap[:, 64:128], [p_.b])
                    p2 = pC[(c + 1) % 2]
                    for k in range(8):
                        mm(p2, p2.ap[:, 0:8], hb.ap[:, k, c * 128:(c + 1) * 128], wAb.ap[:, k, 2688:2696], [wAb.b, hb.b], k == 0, k == 7)
                    act(iw_st, iw_st.ap[:, c, :], p2.ap[:, 0:8], AF.Copy, [p2.b], scale=IDX_SCALE)
                for c in range(2):
                    p_ = pC[c % 2]
                    proj(p_, 2696 + c * 128, 128)
                    copy("act", cq_sb, cq_sb.ap[:, c, :], p_.ap[:, 0:T], [p_.b])
                    act(sqs[c], sqs[c].ap, p_.ap[:, 0:T], AF.Square, [p_.b])
                for c in range(2):
                    mm(pS, pS.ap[:, 0:T], ones_f.ap, sqs[c].ap, [ones_f.b, sqs[c].b], c == 0, c == 1)
                act(rs, rs.ap, pS.ap[:, 0:T], AF.Sqrt, [pS.b, eps6.b], scale=1.0 / 256, bias=eps6.ap[:, 0:1])
                em.op("dve", lambda e: e.reciprocal(out=rs.ap, in_=rs.ap), reads=[rs.b], writes=[rs.b])
                for c in range(2):
                    stt(cqn, cqn.ap[:, c, :], cq_sb.ap[:, c, :], nrm.ap[:, c:c + 1], rs.ap, ALU.mult, ALU.mult, [cq_sb.b, nrm.b, rs.b])
                p_ = pC[0]
                proj(p_, 2952, 128)
                copy("act", ckv_sb, ckv_sb.ap, p_.ap[:, 0:T], [p_.b])
                act(sqs[0], sqs[0].ap, p_.ap[:, 0:T], AF.Square, [p_.b])
                mm(pS, pS.ap[:, 0:T], ones_f.ap, sqs[0].ap, [ones_f.b, sqs[0].b], True, True)
                act(rs, rs.ap, pS.ap[:, 0:T], AF.Sqrt, [pS.b, eps6.b], scale=1.0 / 128, bias=eps6.ap[:, 0:1])
                em.op("dve", lambda e: e.reciprocal(out=rs.ap, in_=rs.ap), reads=[rs.b], writes=[rs.b])
                stt(ckvn, ckvn.ap, ckv_sb.ap, nrm.ap[:, 2:3], rs.ap, ALU.mult, ALU.mult, [ckv_sb.b, nrm.b, rs.b])
                for h in range(8):
                    i = rr[0] % 2
                    rr[0] += 1
                    a_, b_ = pA[i], pB[i]
                    for k in range(2):
                        mm(a_, a_.ap[0:96, 0:T], wuqb.ap[:, k, 96 * h:96 * h + 96], cqn.ap[:, k, :], [wuqb.b, cqn.b], k == 0, k == 1)
                    for k in range(2):
                        mm(b_, b_.ap[0:96, 0:T], wuqb.ap[:, k, 768 + 96 * h:768 + 96 * h + 96], cqn.ap[:, k, :], [wuqb.b, cqn.b], k == 0, k == 1)
                    tt("dve", t1[i], t1[i].ap[0:96, :], a_.ap[0:96, 0:T], c96.ap, ALU.mult, [a_.b, c96.b])
                    tt("dve", t2[i], t2[i].ap[0:96, :], b_.ap[0:96, 0:T], s96.ap, ALU.mult, [b_.b, s96.b])
                    tt("pool", mq_st, mq_st.ap[:, h, :], t1[i].ap[0:96, :], t2[i].ap[0:96, :], ALU.add, [t1[i].b, t2[i].b])
                p_ = pC[1]
                proj(p_, 3176, 96)
                tt("dve", bs, bs.ap, p_.ap[0:96, 0:T], s96.ap, ALU.mult, [p_.b, s96.b])
                for h in range(8):
                    i = rr[0] % 2
                    rr[0] += 1
                    a_ = pA[i]
                    for k in range(8):
                        mm(a_, a_.ap[0:96, 0:T], wAb.ap[:, k, 3080:3176], hb.ap[:, k, :], [wAb.b, hb.b], k == 0, False)
                    mm(a_, a_.ap[0:96, 0:T], wukb.ap[:, 0, 96 * h:96 * h + 96], ckvn.ap, [wukb.b, ckvn.b], False, True)
                    tt("dve", t1[i], t1[i].ap[0:96, :], a_.ap[0:96, 0:T], c96.ap, ALU.mult, [a_.b, c96.b])
                    tt("pool", mk_st, mk_st.ap[:, h, :], t1[i].ap[0:96, :], bs.ap, ALU.add, [t1[i].b, bs.b])
                for c in range(3):
                    p_ = pB[c % 2]
                    mm(p_, p_.ap[:, 0:512], ckvn.ap[:, c * 128:(c + 1) * 128], wuvb.ap[:, 0, :], [wuvb.b, ckvn.b], True, True)
                    copy("act", mv_st, mv_st.ap[:, c, :, 0:64], p_.ap[:, 0:512].rearrange("p (h d) -> p h d", d=64), [p_.b])
                o = outs
                em.dma("pool", "st_q", o["aq"].ap[:, :, n0:n0 + T].rearrange("n p t -> p n t"), aq_st.ap, reads=[aq_st.b], writes=[o["aq"].bufs[m]], nslots=4)
                em.dma("pool", "st_q", o["iq"].ap[:, :, n0:n0 + T].rearrange("n p t -> p n t"), iq_st.ap, reads=[iq_st.b], writes=[o["iq"].bufs[m]], nslots=4)
                em.dma("pool", "st_q", o["mq"].ap[:, :, n0:n0 + T].rearrange("n p t -> p n t"), mq_st.ap, reads=[mq_st.b], writes=[o["mq"].bufs[m]], nslots=4)
                for h in range(8):
                    em.dma("pool", "st_q", o["mk"][h].ap[:, n0:n0 + T], mk_st.ap[:, h, :], reads=[mk_st.b], writes=[o["mk"][h].bufs[m]], nslots=4)
                em.dma("pool", "st_k", o["ak"].ap[:, n0:n0 + T], ak_st.ap, reads=[ak_st.b], writes=[o["ak"].bufs[m]], nslots=4)
                em.dma("pool", "st_k", o["ik"].ap[:, n0:n0 + T], ik_st.ap, reads=[ik_st.b], writes=[o["ik"].bufs[m]], nslots=4)
                em.dma("pool", "st_k", o["av"].ap[n0:n0 + T, :].rearrange("(c p) f -> p c f", p=128), av_st.ap, reads=[av_st.b], writes=[o["av"].bufs[m]], nslots=4)
                for (v_, h0_, h1_) in o["mvc"]:
                    em.dma("pool", "st_k", v_.ap[n0:n0 + T, :].rearrange("(c p) (h d) -> p c h d", p=128, d=65),
                           mv_st.ap[:, :, h0_:h1_, :], reads=[mv_st.b], writes=[v_.bufs[m]], nslots=4)
                em.dma("pool", "st_k", o["iw"].ap[:, 3 * m:3 * m + 3, :], iw_st.ap, reads=[iw_st.b], writes=[o["iw"].bufs[m]], nslots=4)
            em.barrier()

    def finalize_heads(accs, o_st, hlist, accsb, rrow, bc_ps, bc_sb, neg1):
        for r, h in enumerate(hlist):
            a_ = accs[r]
            sb_ = accsb[r % len(accsb)]
            copy("act", sb_, sb_.ap[0:65, :], a_.ap[0:65, 0:T], [a_.b])
            act(rrow, rrow.ap[64:65, :], sb_.ap[64:65, :], AF.Ln, [sb_.b])
            act(rrow, rrow.ap[64:65, :], rrow.ap[64:65, :], AF.Exp, [rrow.b], scale=-1.0)
            mm(bc_ps, bc_ps.ap[0:64, 0:T], ones_f.ap[64:65, 0:64], rrow.ap[64:65, :], [ones_f.b, rrow.b], True, True)
            copy("act", bc_sb, bc_sb.ap, bc_ps.ap[0:64, 0:T], [bc_ps.b])
            tt("pool", o_st, o_st.ap[:, h, :], sb_.ap[0:64, :], bc_sb.ap, ALU.mult, [sb_.b, bc_sb.b])

    def stage_dsa(ins, oa_d, idxm_d, iw_d):
        with ExitStack() as es:
            def sb(name, shape, dt):
                return TL(es.enter_context(nc.sbuf_tensor(uq(name), list(shape), dt)).ap(), name)

            def ps(name):
                return TL(es.enter_context(nc.psum_tensor(uq(name), [128, 512], F32)).ap(), name)

            idxm = sb("idxm_sb", [128, 3, 2, 3, 128], F32)
            em.dma("sp", "ld_small", idxm.ap, idxm_d.ap, reads=[idxm_d.bufs[0]], writes=[idxm.b])
            ik_sb = sb("ik_sb", [128, 2, NT], BF16)
            ak_sb = sb("ak_sb", [128, 2, NT], BF16)
            av_sb = sb("av_sb", [128, 2, NBK, 130], BF16)
            iw_sb = sb("iw_sb", [128, NBK, 8], F32)
            aw_sb = sb("aw_sb", [128, NBK, 8], F32)
            sg_sb = sb("sg_sb", [128, NBK, 8], F32)
            for j in range(2):
                em.dma("sp", "ld_k", ik_sb.ap[:, j, :], ins["ik"].ap[j], reads=ins["ik"].bufs, writes=[ik_sb.b], nslots=4)
                em.dma("sp", "ld_k", ak_sb.ap[:, j, :], ins["ak"].ap[j], reads=ins["ak"].bufs, writes=[ak_sb.b], nslots=4)
                for i0 in range(0, NBK, 11):
                    em.dma("sp", "ld_k", av_sb.ap[:, j, i0:i0 + 11, :],
                           ins["av"].ap[j, i0 * 128:(i0 + 11) * 128, :].rearrange("(i p) c -> p i c", p=128),
                           reads=ins["av"].bufs, writes=[av_sb.b], nslots=4)
            em.dma("sp", "ld_k", iw_sb.ap, iw_d.ap, reads=iw_d.bufs, writes=[iw_sb.b], nslots=4)
            act(aw_sb, aw_sb.ap, iw_sb.ap, AF.Abs, [iw_sb.b])
            act(sg_sb, sg_sb.ap, iw_sb.ap, AF.Sign, [iw_sb.b])
            scs = [sb("sc0", [128, 2 * NT], F32), sb("sc1", [128, 2 * NT], F32)]
            junk = sb("junk", [128, 2 * NT], U8)
            maskT = sb("maskT", [128, NBLK, T], FP8)
            iqz = [sb("iqz0", [128, 4, T], BF16), sb("iqz1", [128, 4, T], BF16)]
            aqz = [sb("aqz0", [128, 4, T], BF16), sb("aqz1", [128, 4, T], BF16)]
            for z_ in (iqz, aqz):
                em.op("pool", lambda e: e.memset(z_[0].ap[64:128, :, :], 0.0), writes=[z_[0].b])
                em.op("pool", lambda e: e.memset(z_[1].ap[0:64, :, :], 0.0), writes=[z_[1].b])
            dg = [sb("dg0", [128, 8, 128], BF16), sb("dg1", [128, 8, 128], BF16)]
            rl = [sb("rl%d" % i, [128, 512], BF16) for i in range(4)]
            mch = [sb("mch0", [128, 1024], BF16), sb("mch1", [128, 1024], BF16)]
            pt = [sb("pt%d" % i, [128, T], BF16) for i in range(4)]
            accsb = [sb("accsb0", [65, T], F32), sb("accsb1", [65, T], F32)]
            neg1 = sb("neg1", [65, T], F32)
            em.op("pool", lambda e: e.memset(neg1.ap, -1.0), writes=[neg1.b])
            thrs = [sb("thr0", [128, 1], F32), sb("thr1", [128, 1], F32)]
            cnts = [sb("cnt0", [128, 1], F32), sb("cnt1", [128, 1], F32)]
            dds = [sb("dd0", [128, 1], F32), sb("dd1", [128, 1], F32)]
            rrow = sb("rrow", [65, T], F32)
            bc_sb = sb("bc_sb", [64, T], F32)
            o_st = sb("o_st", [64, 8, T], BF16)
            psx = [ps("psx0"), ps("psx1")]
            accs = [ps("acc%d" % r) for r in range(4)]
            bc_ps = ps("bc_ps")
            ptr = TL(es.enter_context(nc.psum_tensor(uq("ptr"), [128, 1024], BF16)).ap(), "ptr")
            xi = [0]
            ci = [0]
            mi = [0]
            LA = 2

            def idx_block(m, c):
                nI = 3 * m + 3
                ncol = 2 * nI * 128
                ib = 3 * m + c
                sc = scs[ib % 2]
                scv = sc.ap[:, 0:ncol].rearrange("p (j i t) -> p j i t", j=2, t=128)
                dg_ = dg[ib % 2]
                for h in range(8):
                    act(dg_, dg_.ap[:, h, :], ident_b.ap, AF.Identity, [ident_b.b, sg_sb.b], scale=sg_sb.ap[:, ib, h:h + 1])
                steps = []
                for j in range(2):
                    for i0 in range(0, nI, 4):
                        for h in range(8):
                            steps.append((j, i0, h))
                held = {}
                for si in range(len(steps) + LA):
                    if si < len(steps):
                        j, i0, h = steps[si]
                        cw = min(4, nI - i0) * 128
                        r, hf = h // 2, h % 2
                        p_ = psx[xi[0] % 2]
                        r_ = rl[xi[0] % 4]
                        xi[0] += 1
                        mm(p_, p_.ap[:, 0:cw], iqz[hf].ap[:, r, c * 128:(c + 1) * 128],
                           ik_sb.ap[:, j, i0 * 128:i0 * 128 + cw], [iqz[hf].b, ik_sb.b], True, True)
                        act(r_, r_.ap[:, 0:cw], p_.ap[:, 0:cw], AF.Relu, [p_.b, aw_sb.b], scale=aw_sb.ap[:, ib, h:h + 1])
                        held[si] = r_
                    if si >= LA:
                        j, i0, h = steps[si - LA]
                        cw = min(4, nI - i0) * 128
                        col0 = (j * nI + i0) * 128
                        r_ = held.pop(si - LA)
                        if h == 0:
                            ci[0] += 1
                        a_ = accs[2 + ci[0] % 2]
                        mm(a_, a_.ap[:, 0:cw], dg_.ap[:, h, :], r_.ap[:, 0:cw], [dg_.b, r_.b], h == 0, h == 7)
                        if h == 7:
                            copy("act", sc, sc.ap[:, col0:col0 + cw], a_.ap[:, 0:cw], [a_.b])
                tt("pool", sc, scv[:, :, nI - 3:nI, :], scv[:, :, nI - 3:nI, :], idxm.ap[:, c, :, :, :], ALU.add, [sc.b, idxm.b])

            def bisect_block(m, c):
                nI = 3 * m + 3
                ncol = 2 * nI * 128
                ib = 3 * m + c
                sc, thr, cnt, dd = scs[ib % 2], thrs[ib % 2], cnts[ib % 2], dds[ib % 2]
                em.op("pool", lambda e: e.memset(thr.ap, 3.0e-5), writes=[thr.b])
                for it in range(1, BIS_IT + 1):
                    step = BIS_R / (2 ** it)
                    ts("dve", junk, junk.ap[:, 0:ncol], sc.ap[:, 0:ncol], thr.ap[:, 0:1], ALU.is_ge, [sc.b, thr.b],
                       op1=ALU.add, accum=cnt.ap[:, 0:1], extra_w=[cnt.b])
                    ts("dve", dd, dd.ap, cnt.ap, 255.5, ALU.is_ge, [cnt.b], s2=2.0 * step, op1=ALU.mult)
                    dec = -step if it < BIS_IT else -2.0 * step
                    stt(thr, thr.ap, thr.ap, dec, dd.ap, ALU.add, ALU.add, [thr.b, dd.b])

            def mask_block(m, c):
                nI = 3 * m + 3
                ib = 3 * m + c
                sc, thr = scs[ib % 2], thrs[ib % 2]
                nkb = 2 * nI
                for k0 in range(0, nkb, 8):
                    nk = min(8, nkb - k0)
                    mc = mch[mi[0] % 2]
                    mi[0] += 1
                    ts("dve", mc, mc.ap[:, 0:nk * 128], sc.ap[:, k0 * 128:(k0 + nk) * 128], thr.ap[:, 0:1], ALU.is_ge, [sc.b, thr.b])
                    for q in range(nk):
                        em.op("pe", lambda e: e.transpose(out=ptr.ap[:, q * 128:(q + 1) * 128], in_=mc.ap[:, q * 128:(q + 1) * 128], identity=ident_b.ap),
                              reads=[mc.b, ident_b.b], writes=[ptr.b])
                    em.op("act", lambda e: e.activation(out=maskT.ap[:, k0:k0 + nk, c * 128:(c + 1) * 128],
                                                        in_=ptr.ap[:, 0:nk * 128].rearrange("p (k t) -> p k t", t=128),
                                                        func=AF.Identity, scale=240.0, bias=m240.ap[:, 0:1], saturate=False),
                          reads=[ptr.b, m240.b], writes=[maskT.b])

            def load_iq(m):
                n0 = m * T
                for hf_ in range(2):
                    pr_ = slice(64 * hf_, 64 * hf_ + 64)
                    em.dma("sp", "ld_q", iqz[hf_].ap[pr_, :, :], ins["iq"].ap[:, pr_, n0:n0 + T].rearrange("n p t -> p n t"),
                           reads=[ins["iq"].bufs[m]], writes=[iqz[hf_].b], nslots=4)

            def load_aq(m):
                n0 = m * T
                for hf_ in range(2):
                    pr_ = slice(64 * hf_, 64 * hf_ + 64)
                    em.dma("sp", "ld_q", aqz[hf_].ap[pr_, :, :], ins["aq"].ap[:, pr_, n0:n0 + T].rearrange("n p t -> p n t"),
                           reads=[ins["aq"].bufs[m]], writes=[aqz[hf_].b], nslots=4)

            def attention(m):
                n0 = m * T
                nI = 3 * m + 3
                nkb = 2 * nI
                for g in range(2):
                    for half in range(2):
                        rs_ = [2 * half, 2 * half + 1]
                        steps = [(kb, r) for kb in range(nkb) for r in rs_]
                        held = {}
                        for si in range(len(steps) + LA):
                            if si < len(steps):
                                kb, r = steps[si]
                                j, i = kb // nI, kb % nI
                                p_ = psx[xi[0] % 2]
                                e_ = pt[xi[0] % 4]
                                xi[0] += 1
                                mm(p_, p_.ap[:, 0:T], ak_sb.ap[:, j, i * 128:(i + 1) * 128], aqz[g].ap[:, r, :], [ak_sb.b, aqz[g].b], True, False)
                                mm(p_, p_.ap[:, 0:T], ident_b.ap, maskT.ap[:, kb, :], [ident_b.b, maskT.b], False, True)
                                act(e_, e_.ap, p_.ap[:, 0:T], AF.Exp, [p_.b], scale=0.125)
                                held[si] = e_
                            if si >= LA:
                                kb, r = steps[si - LA]
                                j, i = kb // nI, kb % nI
                                m_ = held.pop(si - LA)
                                a_ = accs[r % 2]
                                mm(a_, a_.ap[0:65, 0:T], av_sb.ap[:, j, i, 65 * g:65 * g + 65], m_.ap, [av_sb.b, m_.b], kb == 0, kb == nkb - 1)
                        finalize_heads(accs[0:2], o_st, [4 * g + r for r in rs_], accsb, rrow, bc_ps, bc_sb, neg1)
                em.dma("pool", "st_o", oa_d.ap[:, :, n0:n0 + T].rearrange("n p t -> p n t"), o_st.ap, reads=[o_st.b], writes=[oa_d.bufs[m]])

            load_iq(0)
            idx_block(0, 0)
            bisect_block(0, 0)
            idx_block(0, 1)
            mask_block(0, 0)
            bisect_block(0, 1)
            idx_block(0, 2)
            mask_block(0, 1)
            bisect_block(0, 2)
            if NTL > 1:
                load_iq(1)
                idx_block(1, 0)
            mask_block(0, 2)
            load_aq(0)
            for m in range(NTL):
                if m + 1 < NTL:
                    bisect_block(m + 1, 0)
                    idx_block(m + 1, 1)
                    bisect_block(m + 1, 1)
                attention(m)
                if m + 1 < NTL:
                    load_aq(m + 1)
                    mask_block(m + 1, 0)
                    idx_block(m + 1, 2)
                    mask_block(m + 1, 1)
                    bisect_block(m + 1, 2)
                    if m + 2 < NTL:
                        load_iq(m + 2)
                        idx_block(m + 2, 0)
                    mask_block(m + 1, 2)
            em.barrier()

    def stage_mla(ins, ob_d, cmT_d):
        with ExitStack() as es:
            def sb(name, shape, dt):
                return TL(es.enter_context(nc.sbuf_tensor(uq(name), list(shape), dt)).ap(), name)

            def ps(name):
                return TL(es.enter_context(nc.psum_tensor(uq(name), [128, 512], F32)).ap(), name)

            cmT = sb("cmT_sb", [128, 6, T], BF16)
            em.dma("sp", "ld_small", cmT.ap, cmT_d.ap, reads=[cmT_d.bufs[0]], writes=[cmT.b])
            mk_sb = sb("mk_sb", [96, 2, 4, NT], BF16)
            mv_sb = sb("mv_sb", [128, 2, NBK, 260], BF16)
            mq_sb = sb("mq_sb", [96, 4, T], BF16)
            pt = [sb("pt%d" % i, [128, T], BF16) for i in range(4)]
            pm = [sb("pm%d" % i, [128, T], BF16) for i in range(4)]
            accsb = [sb("accsb0", [65, T], F32), sb("accsb1", [65, T], F32)]
            neg1 = sb("neg1", [65, T], F32)
            em.op("pool", lambda e: e.memset(neg1.ap, -1.0), writes=[neg1.b])
            rrow = sb("rrow", [65, T], F32)
            bc_sb = sb("bc_sb", [64, T], F32)
            o_st = sb("o_st", [64, 8, T], BF16)
            psx = [ps("psx0"), ps("psx1"), ps("psx2")]
            accs = [ps("acc%d" % r) for r in range(4)]
            bc_ps = ps("bc_ps")
            xi = [0]
            LA = 2
            for hh in range(2):
                for j in range(2):
                    for r in range(4):
                        mkv = ins["mk"][4 * hh + r]
                        em.dma("sp", "ld_k", mk_sb.ap[:, j, r, :], mkv.ap[j], reads=mkv.bufs, writes=[mk_sb.b], nslots=4)
                        mvv = ins["mvh"][4 * hh + r]
                        for i0 in range(0, NBK, 11):
                            em.dma("sp", "ld_k", mv_sb.ap[:, j, i0:i0 + 11, 65 * r:65 * r + 65],
                                   mvv.ap[j, i0 * 128:(i0 + 11) * 128, :].rearrange("(i p) c -> p i c", p=128),
                                   reads=mvv.bufs, writes=[mv_sb.b], nslots=4)
                for m in range(NTL):
                    n0 = m * T
                    nI = 3 * m + 3
                    nkb = 2 * nI
                    em.dma("sp", "ld_q", mq_sb.ap, ins["mq"].ap[4 * hh:4 * hh + 4, :, n0:n0 + T].rearrange("n p t -> p n t"),
                           reads=[ins["mq"].bufs[m]], writes=[mq_sb.b], nslots=4)
                    steps = [(kb, r) for kb in range(nkb) for r in range(4)]
                    held = {}
                    for si in range(len(steps) + LA):
                        if si < len(steps):
                            kb, r = steps[si]
                            j, i = kb // nI, kb % nI
                            zone = i >= nI - 3
                            p_ = psx[xi[0] % 3]
                            e_ = pt[xi[0] % 4]
                            m_ = pm[xi[0] % 4]
                            xi[0] += 1
                            mm(p_, p_.ap[:, 0:T], mk_sb.ap[:, j, r, i * 128:(i + 1) * 128], mq_sb.ap[:, r, :], [mk_sb.b, mq_sb.b], True, True)
                            act(e_, e_.ap, p_.ap[:, 0:T], AF.Exp, [p_.b], scale=float(96 ** -0.5))
                            if zone:
                                z = j * 3 + (i - (nI - 3))
                                tt("pool" if (xi[0] % 2) else "dve", m_, m_.ap, e_.ap, cmT.ap[:, z, :], ALU.mult, [e_.b, cmT.b])
                                held[si] = m_
                            else:
                                held[si] = e_
                        if si >= LA:
                            kb, r = steps[si - LA]
                            j, i = kb // nI, kb % nI
                            src = held.pop(si - LA)
                            mm(accs[r], accs[r].ap[0:65, 0:T], mv_sb.ap[:, j, i, 65 * r:65 * r + 65], src.ap, [mv_sb.b, src.b], kb == 0, kb == nkb - 1)
                    finalize_heads(accs, o_st, [r for r in range(4)], accsb, rrow, bc_ps, bc_sb, neg1)
                    em.dma("pool", "st_o", ob_d.ap[4 * hh:4 * hh + 4, :, n0:n0 + T].rearrange("n p t -> p n t"), o_st.ap[:, 0:4, :],
                           reads=[o_st.b], writes=[ob_d.bufs[m]])
                em.barrier()

    def stage_merge(h1_d, oa_d, ob_d, wg_d, wba_d, wbb_d, wout_d, lnp, li, hout_d):
        with ExitStack() as es:
            def sb(name, shape, dt):
                return TL(es.enter_context(nc.sbuf_tensor(uq(name), list(shape), dt)).ap(), name)

            def ps(name):
                return TL(es.enter_context(nc.psum_tensor(uq(name), [128, 512], F32)).ap(), name)

            wgb = sb("wgb", [128, 8, 2048], BF16)
            wbab = sb("wbab", [128, 4, D], BF16)
            wbbb = sb("wbbb", [128, 4, D], BF16)
            woutb = sb("woutb", [128, 8, D], BF16)
            stg = [sb("stg0", [128, 704], F32), sb("stg1", [128, 704], F32), sb("stg2", [128, 704], F32)]
            load_w(es, wg_d, wgb, 8, 2048, stg)
            load_w(es, wba_d, wbab, 4, D, stg)
            load_w(es, wbb_d, wbbb, 4, D, stg)
            load_w(es, wout_d, woutb, 8, D, stg)
            hin = sb("hin", [128, 8, T], F32)
            hb = sb("hb", [128, 8, T], BF16)
            oa = sb("oa", [128, 4, T], BF16)
            ob = sb("ob", [128, 4, T], BF16)
            mixed = sb("mixed", [128, 8, T], BF16)
            xr = [sb("xr%d" % j, [128, T], F32) for j in range(8)]
            sa = [sb("sa0", [128, T], F32), sb("sa1", [128, T], F32)]
            sbb = [sb("sbb0", [128, T], F32), sb("sbb1", [128, T], F32)]
            sq = [sb("sq0", [128, T], F32), sb("sq1", [128, T], F32)]
            tmp = [sb("tmp0", [128, T], F32), sb("tmp1", [128, T], F32)]
            st = {n: sb("ln_" + n, [128, T], F32) for n in ("mean", "msq", "var", "rstd", "bt")}
            pga, pgb, pba, pbb = ps("pga"), ps("pgb"), ps("pba"), ps("pbb")
            py = [ps("py0"), ps("py1")]
            ps1 = ps("ps1")
            ps2 = ps("ps2")
            for m in range(NTL):
                n0 = m * T
                em.dma("sp", "ld_h", hin.ap, h1_d.ap[:, n0:n0 + T].rearrange("(k p) t -> p k t", p=128), reads=[h1_d.bufs[m]], writes=[hin.b])
                for two in range(2):
                    pr_ = slice(64 * two, 64 * two + 64)
                    em.dma("sp", "ld_o", oa.ap[pr_, :, :], oa_d.ap[:, :, n0:n0 + T].rearrange("(q w) p t -> w p q t", w=2)[two],
                           reads=[oa_d.bufs[m]], writes=[oa.b], nslots=4)
                    em.dma("sp", "ld_o", ob.ap[pr_, :, :], ob_d.ap[:, :, n0:n0 + T].rearrange("(q w) p t -> w p q t", w=2)[two],
                           reads=[ob_d.bufs[m]], writes=[ob.b], nslots=4)
                copy("dve", hb, hb.ap[:, 0:4, :], hin.ap[:, 0:4, :], [hin.b])
                copy("pool", hb, hb.ap[:, 4:8, :], hin.ap[:, 4:8, :], [hin.b])
                for j in range(8):
                    cs = slice(j * 128, (j + 1) * 128)
                    for k in range(8):
                        mm(pga, pga.ap[:, 0:T], wgb.ap[:, k, cs], hb.ap[:, k, :], [wgb.b, hb.b], k == 0, k == 7)
                    for k in range(8):
                        mm(pgb, pgb.ap[:, 0:T], wgb.ap[:, k, 1024 + j * 128:1024 + (j + 1) * 128], hb.ap[:, k, :], [wgb.b, hb.b], k == 0, k == 7)
                    for k in range(4):
                        mm(pba, pba.ap[:, 0:T], wbab.ap[:, k, cs], oa.ap[:, k, :], [wbab.b, oa.b], k == 0, k == 3)
                    for k in range(4):
                        mm(pbb, pbb.ap[:, 0:T], wbbb.ap[:, k, cs], ob.ap[:, k, :], [wbbb.b, ob.b], k == 0, k == 3)
                    s1_, s2_ = sa[j % 2], sbb[j % 2]
                    act(s1_, s1_.ap, pga.ap[:, 0:T], AF.Sigmoid, [pga.b])
                    act(s2_, s2_.ap, pgb.ap[:, 0:T], AF.Sigmoid, [pgb.b])
                    tt("dve", s1_, s1_.ap, s1_.ap, pba.ap[:, 0:T], ALU.mult, [s1_.b, pba.b])
                    tt("dve", s2_, s2_.ap, s2_.ap, pbb.ap[:, 0:T], ALU.mult, [s2_.b, pbb.b])
                    tt("pool", mixed, mixed.ap[:, j, :], s1_.ap, s2_.ap, ALU.add, [s1_.b, s2_.b])
                for j in range(8):
                    y_ = py[j % 2]
                    for k in range(8):
                        mm(y_, y_.ap[:, 0:T], woutb.ap[:, k, j * 128:(j + 1) * 128], mixed.ap[:, k, :], [woutb.b, mixed.b], k == 0, k == 7)
                    act(xr[j], xr[j].ap, hin.ap[:, j, :], AF.Copy, [hin.b], scale=ALPHA)
                    stt(xr[j], xr[j].ap, y_.ap[:, 0:T], 1.0, xr[j].ap, ALU.mult, ALU.add, [y_.b, xr[j].b])
                layer_norm(xr, lnp, li, ps1, ps2, sq, tmp, st)
                for j in range(8):
                    em.dma("pool", "st_h", hout_d.ap[j * 128:(j + 1) * 128, n0:n0 + T], xr[j].ap,
                           reads=[xr[j].b], writes=[hout_d.bufs[m]], nslots=4)
            em.barrier()

    class V:
        def __init__(self, ap, bufs):
            self.ap = ap
            self.bufs = bufs

    def internal(name, shape, dt, n=NTL):
        return dram(name, shape, dt, "Internal", n)

    FMR = [224, 224, 192, 192, 192]
    TMC = [195, 195, 195, 65]
    MKLOC = [(0, 128), (1, 128), (2, 0), (2, 96), (3, 0), (3, 96), (4, 0), (4, 96)]
    MVLOC = [(0, 130), (1, 0), (1, 65), (1, 130), (2, 0), (2, 65), (2, 130), (3, 0)]

    def exchange(L, fm, tm):
        fa, ta = [], []
        for i, t_ in enumerate(fm):
            A = internal("fmA%d_%d" % (L, i), [2 * FMR[i], NT], BF16, 1)
            em.coll(t_.ap, A.ap, reads=t_.bufs, writes=A.bufs)
            fa.append(V(A.ap.rearrange("(j r) t -> j r t", j=2), A.bufs))
        for i, t_ in enumerate(tm):
            A = internal("tmA%d_%d" % (L, i), [2 * NT, TMC[i]], BF16, 1)
            em.coll(t_.ap, A.ap, reads=t_.bufs, writes=A.bufs)
            ta.append(V(A.ap.rearrange("(j r) c -> j r c", j=2), A.bufs))
        return {"ak": V(fa[0].ap[:, 0:128, :], fa[0].bufs), "ik": V(fa[1].ap[:, 0:128, :], fa[1].bufs),
                "mk": [V(fa[c].ap[:, r:r + 96, :], fa[c].bufs) for (c, r) in MKLOC],
                "av": V(ta[0].ap[:, :, 0:130], ta[0].bufs),
                "mvh": [V(ta[c].ap[:, :, k:k + 65], ta[c].bufs) for (c, k) in MVLOC]}

    tabs = (din("c128", [128, NT], F32, 1), din("s128", [128, NT], F32, 1),
            din("c96", [96, NT], F32, 1), din("s96", [96, NT], F32, 1))
    idxm_d = din("idxm", [128, 3, 2, 3, 128], F32, 1)
    cmT_d = din("cmT", [128, 6, T], BF16, 1)
    hin = din("xT", [D, NT], F32)
    for L in range(DEPTH):
        lnp = small_in("lnp_%d" % L, [128, 48])
        nrm = small_in("nrm_%d" % L, [128, 3])
        w = dict(w13a=din("w13a_%d" % L, [D, 2 * DFF], F32, 1), w2a=din("w2a_%d" % L, [DFF, D], F32, 1),
                 wA=din("wA_%d" % L, [D, NA], F32, 1), wuq=din("wuq_%d" % L, [256, 1536], F32, 1),
                 wuk=din("wuk_%d" % L, [128, 768], F32, 1), wuv=din("wuv_%d" % L, [128, 512], F32, 1),
                 wg=din("wg_%d" % L, [D, 2048], F32, 1), wba=din("wba_%d" % L, [512, D], F32, 1),
                 wbb=din("wbb_%d" % L, [512, D], F32, 1), wout=din("wout_%d" % L, [D, D], F32, 1),
                 w13b=din("w13b_%d" % L, [D, 2 * DFF], F32, 1), w2b=din("w2b_%d" % L, [DFF, D], F32, 1))
        h1 = internal("h1_%d" % L, [D, NT], F32)
        stage_ffn(hin, h1, w["w13a"], w["w2a"], lnp, 0)
        fm = [internal("fm%d_%d" % (L, i), [r_, NT], BF16) for i, r_ in enumerate(FMR)]
        tm = [internal("tm%d_%d" % (L, i), [NT, c_], BF16) for i, c_ in enumerate(TMC)]
        q = {"aq": internal("aq%d" % L, [4, 128, NT], BF16), "iq": internal("iq%d" % L, [4, 128, NT], BF16),
             "mq": internal("mq%d" % L, [8, 96, NT], BF16)}
        iw = internal("iw%d" % L, [128, NBK, 8], F32)
        outs = dict(q)
        outs.update({"ak": V(fm[0].ap[0:128, :], fm[0].bufs), "ik": V(fm[1].ap[0:128, :], fm[1].bufs),
                     "mk": [V(fm[c].ap[r:r + 96, :], fm[c].bufs) for (c, r) in MKLOC],
                     "av": V(tm[0].ap[:, 0:130], tm[0].bufs),
                     "mvc": [(V(tm[0].ap[:, 130:195], tm[0].bufs), 0, 1), (V(tm[1].ap, tm[1].bufs), 1, 4),
                             (V(tm[2].ap, tm[2].bufs), 4, 7), (V(tm[3].ap, tm[3].bufs), 7, 8)],
                     "iw": iw})
        stage_proj(h1, w["wA"], w["wuq"], w["wuk"], w["wuv"], nrm, tabs, outs)
        ins = dict(q)
        ins.update(exchange(L, fm, tm))
        oa = internal("oa%d" % L, [8, 64, NT], BF16)
        ob = internal("ob%d" % L, [8, 64, NT], BF16)
        h2 = internal("h2_%d" % L, [D, NT], F32)
        stage_dsa(ins, oa, idxm_d, iw)
        stage_mla(ins, ob, cmT_d)
        stage_merge(h1, oa, ob, w["wg"], w["wba"], w["wbb"], w["wout"], lnp, 1, h2)
        h3 = dout("outT", [D, NT], F32) if L == DEPTH - 1 else internal("h3_%d" % L, [D, NT], F32)
        stage_ffn(h2, h3, w["w13b"], w["w2b"], lnp, 2)
        hin = h3
    em.barrier()
    gs.close()
    return nc, em


def _rope_perm64():
    p = np.arange(64)
    p[0:8] = np.arange(8, 16)
    p[8:16] = np.arange(0, 8)
    return p


def _prep_weights(inp, L):
    f = np.float32
    w_in = np.asarray(inp["w_in"][L], f)
    perm = _rope_perm64()
    a_q = w_in[:, 0:512].reshape(D, 8, 64)
    a_k = w_in[:, 512:640].reshape(D, 2, 64)
    a_v = w_in[:, 640:768]
    i_q = w_in[:, 768:1280].reshape(D, 8, 64)
    i_k = w_in[:, 1280:1344]
    i_w = w_in[:, 1344:1352]
    c_q = w_in[:, 1352:1608]
    c_kv = w_in[:, 1608:1736]
    k_r = w_in[:, 1736:1768]
    gates = w_in[:, 1768:3816]
    aq_pair = np.stack([np.concatenate([a_q[:, r], a_q[:, 4 + r]], 1) for r in range(4)], 1)
    aq_pairB = np.stack([np.concatenate([a_q[:, r][:, perm], a_q[:, 4 + r][:, perm]], 1) for r in range(4)], 1)
    akA = a_k.reshape(D, 128)
    akB = a_k[:, :, perm].reshape(D, 128)
    iqA = i_q.reshape(D, 512)
    iqB = i_q[:, :, perm].reshape(D, 512)
    ikA = np.concatenate([i_k, i_k], 1)
    ikB = np.concatenate([i_k[:, perm], i_k[:, perm]], 1)
    krA = np.zeros((D, 96), f)
    krA[:, 64:96] = k_r
    krB = np.zeros((D, 96), f)
    krB[:, 64:80] = k_r[:, 16:32]
    krB[:, 80:96] = k_r[:, 0:16]
    wA = np.concatenate([aq_pair.reshape(D, 512), aq_pairB.reshape(D, 512), akA, akB, a_v, iqA, iqB, ikA, ikB,
                         i_w, c_q, c_kv, krA, krB], 1)
    assert wA.shape[1] == NA, wA.shape
    w_uq = np.asarray(inp["mla_w_uq"][L], f).reshape(256, 8, 96)
    uqB = w_uq.copy()
    uqB[:, :, 64:80] = w_uq[:, :, 80:96]
    uqB[:, :, 80:96] = w_uq[:, :, 64:80]
    wuq = np.concatenate([w_uq.reshape(256, 768), uqB.reshape(256, 768)], 1)
    w_ukv = np.asarray(inp["mla_w_ukv"][L], f).reshape(128, 8, 128)
    wuk = np.zeros((128, 8, 96), f)
    wuk[:, :, 0:64] = w_ukv[:, :, 0:64]
    wuv = np.ascontiguousarray(w_ukv[:, :, 64:128]).reshape(128, 512)
    ln_g = np.asarray(inp["ln_g"][L], f)
    ln_b = np.asarray(inp["ln_b"][L], f)
    lnp = np.zeros((128, 48), f)
    for li in range(3):
        lnp[:, li * 16:li * 16 + 8] = ln_g[li].reshape(8, 128).T
        lnp[:, li * 16 + 8:li * 16 + 16] = ln_b[li].reshape(8, 128).T
    nrm = np.zeros((128, 3), f)
    nrm[:, 0:2] = np.asarray(inp["mla_q_norm"][L], f).reshape(2, 128).T
    nrm[:, 2] = np.asarray(inp["mla_kv_norm"][L], f)
    c = np.ascontiguousarray
    return {
        "w13a_%d" % L: c(np.asarray(inp["ffn1_w13"][L], f)), "w2a_%d" % L: c(np.asarray(inp["ffn1_w2"][L], f)),
        "wA_%d" % L: c(wA), "wuq_%d" % L: c(wuq), "wuk_%d" % L: c(wuk.reshape(128, 768)), "wuv_%d" % L: c(wuv),
        "wg_%d" % L: c(gates), "wba_%d" % L: c(np.asarray(inp["w_branch_a"][L], f)),
        "wbb_%d" % L: c(np.asarray(inp["w_branch_b"][L], f)), "wout_%d" % L: c(np.asarray(inp["w_out"][L], f)),
        "w13b_%d" % L: c(np.asarray(inp["ffn2_w13"][L], f)), "w2b_%d" % L: c(np.asarray(inp["ffn2_w2"][L], f)),
        "lnp_%d" % L: lnp, "nrm_%d" % L: nrm,
    }


def _tables(j):
    n = np.arange(NT)
    pos = ((2 * (n // 128) + j) * 128 + n % 128).astype(np.float32)
    inv16 = (THETA ** (-(np.arange(8, dtype=np.float32) * 2.0 / 16))).astype(np.float32)
    inv32 = (THETA ** (-(np.arange(16, dtype=np.float32) * 2.0 / 32))).astype(np.float32)
    a16 = pos[None, :] * inv16[:, None]
    a32 = pos[None, :] * inv32[:, None]
    c64 = np.ones((64, NT), np.float32)
    s64 = np.zeros((64, NT), np.float32)
    c64[0:8] = np.cos(a16)
    c64[8:16] = np.cos(a16)
    s64[0:8] = -np.sin(a16)
    s64[8:16] = np.sin(a16)
    c96 = np.ones((96, NT), np.float32)
    s96 = np.zeros((96, NT), np.float32)
    c96[64:80] = np.cos(a32)
    c96[80:96] = np.cos(a32)
    s96[64:80] = -np.sin(a32)
    s96[80:96] = np.sin(a32)
    return {"c128": np.concatenate([c64, c64], 0), "s128": np.concatenate([s64, s64], 0), "c96": c96, "s96": s96}


def _masks(j):
    idxm = np.zeros((128, 3, 2, 3, 128), np.float32)
    cmT = np.zeros((128, 6, T), np.float32)
    t = np.arange(128)
    for c in range(3):
        qbz = 2 * c + j
        for jp in range(2):
            for cp in range(3):
                kbz = 2 * cp + jp
                if kbz < qbz:
                    vis = np.ones((128, 128), bool)
                elif kbz == qbz:
                    vis = t[None, :] <= t[:, None]
                else:
                    vis = np.zeros((128, 128), bool)
                idxm[:, c, jp, cp, :] = np.where(vis, 0.0, NEGM)
                cmT[:, jp * 3 + cp, c * 128:(c + 1) * 128] = vis.T.astype(np.float32)
    return {"idxm": idxm, "cmT": cmT.astype(ml_dtypes.bfloat16)}


_PROG = []


def _prog():
    if not _PROG:
        _PROG.append(build_program()[0])
    return _PROG[0]


def kernel(x, meta_tokens, ln_g, ln_b, ffn1_w13, ffn1_w2, w_in, mla_q_norm, mla_kv_norm,
           mla_w_uq, mla_w_ukv, w_branch_a, w_branch_b, w_out, ffn2_w13, ffn2_w2):
    inp = dict(x=x, meta_tokens=meta_tokens, ln_g=ln_g, ln_b=ln_b, ffn1_w13=ffn1_w13, ffn1_w2=ffn1_w2, w_in=w_in,
               mla_q_norm=mla_q_norm, mla_kv_norm=mla_kv_norm, mla_w_uq=mla_w_uq, mla_w_ukv=mla_w_ukv,
               w_branch_a=w_branch_a, w_branch_b=w_branch_b, w_out=w_out, ffn2_w13=ffn2_w13, ffn2_w2=ffn2_w2)
    x = np.asarray(x, np.float32)
    meta = np.asarray(meta_tokens, np.float32)
    W = {}
    for L in range(DEPTH):
        W.update(_prep_weights(inp, L))
    tabs = [_tables(j) for j in range(2)]
    msk = [_masks(j) for j in range(2)]
    maps = []
    for c in range(8):
        b, j = c // 2, c % 2
        h0 = np.zeros((NBLK * 128, D), np.float32)
        h0[0:N_META] = meta
        h0[N_META:N_META + SEQ] = x[b]
        mine = h0.reshape(NBLK, 128, D)[j::2].reshape(NT, D)
        m = dict(W)
        m.update(tabs[j])
        m.update(msk[j])
        m["xT"] = np.ascontiguousarray(mine.T)
        maps.append(m)
    res = run_bass_kernel_spmd(_prog(), maps, core_ids=list(range(8))).results
    out = np.zeros((BATCH, SEQ, D), np.float32)
    for b in range(BATCH):
        full = np.zeros((NBLK, 128, D), np.float32)
        for j in range(2):
            full[j::2] = np.asarray(res[2 * b + j]["outT"]).T.reshape(NBK, 128, D)
        out[b] = full.reshape(NBLK * 128, D)[N_META:N_META + SEQ]
    return out
```

```python
import numpy as np
import ml_dtypes
from contextlib import ExitStack
import concourse.bass as bass
import concourse.mybir as mybir
from concourse.bass_utils import run_bass_kernel_spmd

F32 = mybir.dt.float32
BF16 = mybir.dt.bfloat16
U8 = mybir.dt.uint8
FP8 = mybir.dt.float8e4
AF = mybir.ActivationFunctionType
ALU = mybir.AluOpType

P = 128
T = 384
NTL = 11
NBK = 33
NT = NBK * 128
NBLK = 66
D = 1024
DFF = 2816
N_META = 16
SEQ = 8192
BATCH = 4
DEPTH = 2
ALPHA = float((2 * DEPTH) ** 0.25)
THETA = 500000.0
NEGM = -1.0e4
NA = 3272
IDX_SCALE = float(64 ** -0.5 * 8 ** -0.5)
BIS_R = 16.0
BIS_IT = 17


class Buf:
    __slots__ = ("w", "r", "name")

    def __init__(self, name=""):
        self.w = {}
        self.r = {}
        self.name = name


class TL:
    __slots__ = ("ap", "b")

    def __init__(self, ap, name=""):
        self.ap = ap
        self.b = Buf(name)


class Em:
    def __init__(self, nc):
        self.nc = nc
        self.engs = {"pe": nc.tensor, "act": nc.scalar, "dve": nc.vector, "pool": nc.gpsimd, "sp": nc.sync}
        self.sems = {}
        self.cnt = {}
        self.seen = {k: {} for k in self.engs}
        self.grp = {}
        for k in self.engs:
            self.sems[k] = nc.alloc_semaphore("s_" + k)
            self.cnt[k] = 0
        self.nwait = 0
        self.nins = 0

    def _wait(self, eng, k, c):
        if self.seen[eng].get(k, 0) < c:
            self.engs[eng].wait_ge(self.sems[k], c)
            self.seen[eng][k] = c
            self.nwait += 1

    def _deps(self, eng, reads, writes):
        need = {}
        for b in reads:
            for k, c in b.w.items():
                if need.get(k, 0) < c:
                    need[k] = c
        for b in writes:
            for d in (b.w, b.r):
                for k, c in d.items():
                    if need.get(k, 0) < c:
                        need[k] = c
        for k, c in need.items():
            if k == eng and eng == "pe":
                continue
            self._wait(eng, k, c)

    def op(self, eng, fn, reads=(), writes=()):
        self._deps(eng, reads, writes)
        ins = fn(self.engs[eng])
        self.cnt[eng] += 1
        self.nins += 1
        ins.then_inc(self.sems[eng], 1)
        c = self.cnt[eng]
        for b in writes:
            b.w[eng] = c
            b.r = {}
        for b in reads:
            if b.r.get(eng, 0) < c:
                b.r[eng] = c
        return ins

    def dma(self, q, grp, out, in_, reads=(), writes=(), nslots=2):
        st = self.grp.setdefault(grp, [0, nslots])
        slot = st[0] % st[1]
        st[0] += 1
        key = (grp, slot)
        if key not in self.sems:
            self.sems[key] = self.nc.alloc_semaphore("d_%s_%d" % (grp, slot))
            self.cnt[key] = 0
        if self.cnt[key] > 0:
            self._wait(q, key, self.cnt[key])
        self._deps(q, reads, writes)
        ins = self.engs[q].dma_start(out=out, in_=in_)
        self.cnt[key] += 16
        self.nins += 1
        ins.then_inc(self.sems[key], 16)
        c = self.cnt[key]
        for b in writes:
            b.w[key] = c
            b.r = {}
        for b in reads:
            if b.r.get(key, 0) < c:
                b.r[key] = c
        return ins

    def coll(self, in_ap, out_ap, reads=(), writes=()):
        q, key = "pool", "cc"
        if key not in self.sems:
            self.sems[key] = self.nc.alloc_semaphore("s_cc")
            self.cnt[key] = 0
        if self.cnt[key] > 0:
            self._wait(q, key, self.cnt[key])
        self._deps(q, reads, writes)
        ins = self.engs[q].collective_compute("AllGather", ALU.bypass, replica_groups=[[0, 1], [2, 3], [4, 5], [6, 7]],
                                              ins=[in_ap], outs=[out_ap])
        self.cnt[key] += 1
        self.nins += 1
        ins.then_inc(self.sems[key])
        c = self.cnt[key]
        for b in writes:
            b.w[key] = c
            b.r = {}
        for b in reads:
            if b.r.get(key, 0) < c:
                b.r[key] = c
        return ins

    def barrier(self, engines=None):
        for e in (engines or list(self.engs)):
            for k, c in self.cnt.items():
                if k != e and c > 0:
                    self._wait(e, k, c)


class DT:
    def __init__(self, ap, n=NTL, name=""):
        self.ap = ap
        self.bufs = [Buf(name + str(i)) for i in range(n)]


def build_program():
    nc = bass.Bass("TRN2", target_bir_lowering=False)
    em = Em(nc)
    dts = {}

    def dram(name, shape, dt, kind, n=NTL):
        t = DT(nc.dram_tensor(name, list(shape), dt, kind=kind).ap(), n=n, name=name)
        dts[name] = t
        return t

    def din(name, shape, dt, n=NTL):
        return dram(name, shape, dt, "ExternalInput", n)

    def dout(name, shape, dt, n=NTL):
        return dram(name, shape, dt, "ExternalOutput", n)

    gs = ExitStack()
    uid = [0]

    def uq(name):
        uid[0] += 1
        return "%s__u%d" % (name, uid[0])

    def galloc(name, shape, dt):
        return TL(gs.enter_context(nc.sbuf_tensor(name, list(shape), dt)).ap(), name)

    ones_f = galloc("ones_f", [128, 128], F32)
    ident_b = galloc("ident_b", [128, 128], BF16)
    eps5 = galloc("eps5", [128, 1], F32)
    eps6 = galloc("eps6", [128, 1], F32)
    m240 = galloc("m240", [128, 1], F32)
    em.op("pool", lambda e: e.memset(ones_f.ap, 1.0), writes=[ones_f.b])
    em.op("pool", lambda e: e.memset(eps5.ap, 1e-5), writes=[eps5.b])
    em.op("pool", lambda e: e.memset(eps6.ap, 1e-6), writes=[eps6.b])
    em.op("pool", lambda e: e.memset(m240.ap, -240.0), writes=[m240.b])
    em.op("pool", lambda e: e.memset(ident_b.ap, 0.0), writes=[ident_b.b])
    em.op("pool", lambda e: e.affine_select(out=ident_b.ap, in_=ident_b.ap, pattern=[[-1, 128]],
                                            compare_op=ALU.not_equal, fill=1.0, base=0, channel_multiplier=1),
          reads=[ident_b.b], writes=[ident_b.b])

    def small_in(name, shape):
        d = din(name, shape, F32, n=1)
        t = galloc(name + "_sb", shape, F32)
        em.dma("sp", "ld_small", t.ap, d.ap, reads=[d.bufs[0]], writes=[t.b])
        return t

    def mm(out_tl, out_ap, lhsT, rhs, reads, start, stop):
        em.op("pe", lambda e: e.matmul(out_ap, lhsT=lhsT, rhs=rhs, start=start, stop=stop),
              reads=reads, writes=[out_tl.b])

    def act(out_tl, out_ap, in_ap, func, reads, scale=None, bias=None):
        kw = {}
        if scale is not None:
            kw["scale"] = scale
        if bias is not None:
            kw["bias"] = bias
        em.op("act", lambda e: e.activation(out=out_ap, in_=in_ap, func=func, **kw), reads=reads, writes=[out_tl.b])

    def tt(eng, out_tl, out_ap, in0, in1, op, reads):
        em.op(eng, lambda e: e.tensor_tensor(out=out_ap, in0=in0, in1=in1, op=op), reads=reads, writes=[out_tl.b])

    def ts(eng, out_tl, out_ap, in0, s1, op0, reads, s2=None, op1=None, accum=None, extra_w=()):
        kw = {}
        if op1 is not None:
            kw["op1"] = op1
        if accum is not None:
            kw["accum_out"] = accum
        em.op(eng, lambda e: e.tensor_scalar(out=out_ap, in0=in0, scalar1=s1, scalar2=s2, op0=op0, **kw),
              reads=reads, writes=[out_tl.b] + list(extra_w))

    def stt(out_tl, out_ap, in0, scalar, in1, op0, op1, reads):
        em.op("dve", lambda e: e.scalar_tensor_tensor(out=out_ap, in0=in0, scalar=scalar, in1=in1, op0=op0, op1=op1),
              reads=reads, writes=[out_tl.b])

    def copy(eng, out_tl, out_ap, in_ap, reads):
        if eng == "act":
            act(out_tl, out_ap, in_ap, AF.Copy, reads)
        else:
            em.op(eng, lambda e: e.tensor_copy(out=out_ap, in_=in_ap), reads=reads, writes=[out_tl.b])

    cast_rr = [0]

    def load_w(es_alloc, d, dst, kc, ncols, stg, kp=128):
        W = 704
        for k in range(kc):
            for c0 in range(0, ncols, W):
                cw = min(W, ncols - c0)
                s = stg[cast_rr[0] % len(stg)]
                eng = ("dve", "act")[cast_rr[0] % 2]
                cast_rr[0] += 1
                em.dma("sp", "ld_w", s.ap[0:kp, 0:cw], d.ap[k * kp:(k + 1) * kp, c0:c0 + cw],
                       reads=[d.bufs[0]], writes=[s.b], nslots=3)
                copy(eng, dst, dst.ap[0:kp, k, c0:c0 + cw], s.ap[0:kp, 0:cw], reads=[s.b])

    def ln_stats(xr, ps1, ps2, sq):
        for j in range(8):
            mm(ps1, ps1.ap[:, 0:T], ones_f.ap, xr[j].ap, [ones_f.b, xr[j].b], j == 0, j == 7)
        for j in range(8):
            s = sq[j % 2]
            act(s, s.ap, xr[j].ap, AF.Square, [xr[j].b])
            mm(ps2, ps2.ap[:, 0:T], ones_f.ap, s.ap, [ones_f.b, s.b], j == 0, j == 7)

    def ln_apply(xr, lnp, li, ps1, ps2, tmp, st):
        mean, msq, var, rstd, bt = st["mean"], st["msq"], st["var"], st["rstd"], st["bt"]
        act(mean, mean.ap, ps1.ap[:, 0:T], AF.Copy, [ps1.b], scale=1.0 / D)
        act(msq, msq.ap, mean.ap, AF.Square, [mean.b])
        stt(var, var.ap, ps2.ap[:, 0:T], 1.0 / D, msq.ap, ALU.mult, ALU.subtract, [ps2.b, msq.b])
        act(var, var.ap, var.ap, AF.Sqrt, [var.b, eps5.b], bias=eps5.ap[:, 0:1])
        em.op("dve", lambda e: e.reciprocal(out=rstd.ap, in_=var.ap), reads=[var.b], writes=[rstd.b])
        stt(bt, bt.ap, mean.ap, -1.0, rstd.ap, ALU.mult, ALU.mult, [mean.b, rstd.b])
        for j in range(8):
            t = tmp[j % 2]
            tt("dve", t, t.ap, xr[j].ap, rstd.ap, ALU.mult, [xr[j].b, rstd.b])
            tt("pool", t, t.ap, t.ap, bt.ap, ALU.add, [t.b, bt.b])
            act(xr[j], xr[j].ap, t.ap, AF.Identity, [t.b, lnp.b],
                scale=lnp.ap[:, li * 16 + j:li * 16 + j + 1], bias=lnp.ap[:, li * 16 + 8 + j:li * 16 + 8 + j + 1])

    def layer_norm(xr, lnp, li, ps1, ps2, sq, tmp, st):
        ln_stats(xr, ps1, ps2, sq)
        ln_apply(xr, lnp, li, ps1, ps2, tmp, st)

    def stage_ffn(hin_d, hout_d, w13_d, w2_d, lnp, li):
        with ExitStack() as es:
            def sb(name, shape, dt):
                return TL(es.enter_context(nc.sbuf_tensor(uq(name), list(shape), dt)).ap(), name)

            def ps(name):
                return TL(es.enter_context(nc.psum_tensor(uq(name), [128, 512], F32)).ap(), name)

            w13b = sb("w13b", [128, 8, 2 * DFF], BF16)
            w2b = sb("w2b", [128, 22, D], BF16)
            stg = [sb("stg0", [128, 704], F32), sb("stg1", [128, 704], F32), sb("stg2", [128, 704], F32)]
            load_w(es, w13_d, w13b, 8, 2 * DFF, stg)
            load_w(es, w2_d, w2b, 22, D, stg)
            X = [[sb("x%d_%d" % (b_, j), [128, T], F32) for j in range(8)] for b_ in range(2)]
            hb = sb("hb", [128, 8, T], BF16)
            G = sb("G", [128, 22, T], BF16)
            sg = [sb("sg0", [128, T], F32), sb("sg1", [128, T], F32)]
            sq = [sb("sq0", [128, T], F32), sb("sq1", [128, T], F32)]
            st = {n: sb("ln_" + n, [128, T], F32) for n in ("mean", "var", "bt")}
            st["msq"] = st["bt"]
            st["rstd"] = st["var"]
            tmpb = [sb("tmpb0", [128, T], F32), sb("tmpb1", [128, T], F32)]
            pg = [ps("pg0"), ps("pg1")]
            pu = [ps("pu0"), ps("pu1")]
            py = [ps("py0"), ps("py1")]
            ps1 = ps("ps1")
            ps2 = ps("ps2")

            def ph_a_hb(m):
                n0 = m * T
                em.dma("pool", "ld_hb", hb.ap, hin_d.ap[:, n0:n0 + T].rearrange("(k p) t -> p k t", p=128),
                       reads=[hin_d.bufs[m]], writes=[hb.b], nslots=2)

            def ph_a_x(m):
                n0 = m * T
                xb = X[m % 2]
                for j in range(8):
                    em.dma("sp", "ld_h", xb[j].ap, hin_d.ap[j * 128:(j + 1) * 128, n0:n0 + T],
                           reads=[hin_d.bufs[m]], writes=[xb[j].b], nslots=4)

            def ph_b(m):
                for c in range(22):
                    g_, u_ = pg[c % 2], pu[c % 2]
                    for k in range(8):
                        mm(g_, g_.ap[:, 0:T], w13b.ap[:, k, c * 128:(c + 1) * 128], hb.ap[:, k, :], [w13b.b, hb.b], k == 0, k == 7)
                    for k in range(8):
                        mm(u_, u_.ap[:, 0:T], w13b.ap[:, k, DFF + c * 128:DFF + (c + 1) * 128], hb.ap[:, k, :], [w13b.b, hb.b], k == 0, k == 7)
                    s_ = sg[c % 2]
                    act(s_, s_.ap, g_.ap[:, 0:T], AF.Silu, [g_.b])
                    tt("dve", G, G.ap[:, c, :], s_.ap, u_.ap[:, 0:T], ALU.mult, [s_.b, u_.b])

            def ph_c(m):
                xb = X[m % 2]
                for j in range(8):
                    y_ = py[j % 2]
                    for k in range(22):
                        mm(y_, y_.ap[:, 0:T], w2b.ap[:, k, j * 128:(j + 1) * 128], G.ap[:, k, :], [w2b.b, G.b], k == 0, k == 21)
                    act(xb[j], xb[j].ap, xb[j].ap, AF.Copy, [xb[j].b], scale=ALPHA)
                    stt(xb[j], xb[j].ap, y_.ap[:, 0:T], 0.5, xb[j].ap, ALU.mult, ALU.add, [y_.b, xb[j].b])

            def ph_d1(m):
                ln_stats(X[m % 2], ps1, ps2, sq)

            def ph_d2(m):
                n0 = m * T
                xb = X[m % 2]
                ln_apply(xb, lnp, li, ps1, ps2, tmpb, st)
                for j in range(8):
                    em.dma("sp", "st_h", hout_d.ap[j * 128:(j + 1) * 128, n0:n0 + T], xb[j].ap,
                           reads=[xb[j].b], writes=[hout_d.bufs[m]], nslots=4)

            ph_a_hb(0)
            ph_a_x(0)
            ph_b(0)
            for m in range(NTL):
                if m + 1 < NTL:
                    ph_a_hb(m + 1)
                ph_c(m)
                if m >= 1:
                    ph_d2(m - 1)
                if m + 1 < NTL:
                    ph_a_x(m + 1)
                    ph_b(m + 1)
                ph_d1(m)
            ph_d2(NTL - 1)
            em.barrier()

    def stage_proj(h1_d, wA_d, wuq_d, wuk_d, wuv_d, nrm, tabs, outs):
        c128_d, s128_d, c96_d, s96_d = tabs
        with ExitStack() as es:
            def sb(name, shape, dt):
                return TL(es.enter_context(nc.sbuf_tensor(uq(name), list(shape), dt)).ap(), name)

            def ps(name):
                return TL(es.enter_context(nc.psum_tensor(uq(name), [128, 512], F32)).ap(), name)

            wAb = sb("wAb", [128, 8, NA], BF16)
            wuqb = sb("wuqb", [128, 2, 1536], BF16)
            wukb = sb("wukb", [128, 1, 768], BF16)
            wuvb = sb("wuvb", [128, 1, 512], BF16)
            stg = [sb("stg0", [128, 704], F32), sb("stg1", [128, 704], F32), sb("stg2", [128, 704], F32)]
            load_w(es, wA_d, wAb, 8, NA, stg)
            load_w(es, wuq_d, wuqb, 2, 1536, stg)
            load_w(es, wuk_d, wukb, 1, 768, stg)
            load_w(es, wuv_d, wuvb, 1, 512, stg)
            hbs = [sb("hb0", [128, 8, T], BF16), sb("hb1", [128, 8, T], BF16)]
            cur = {}
            c128 = sb("c128", [128, T], F32)
            s128 = sb("s128", [128, T], F32)
            c96 = sb("c96", [96, T], F32)
            s96 = sb("s96", [96, T], F32)
            t1 = [sb("t1_0", [128, T], F32), sb("t1_1", [128, T], F32)]
            t2 = [sb("t2_0", [128, T], F32), sb("t2_1", [128, T], F32)]
            aq_st = sb("aq_st", [128, 4, T], BF16)
            iq_st = sb("iq_st", [128, 4, T], BF16)
            ak_st = sb("ak_st", [128, T], BF16)
            ik_st = sb("ik_st", [128, T], BF16)
            mq_st = sb("mq_st", [96, 8, T], BF16)
            mk_st = sb("mk_st", [96, 8, T], BF16)
            av_st = sb("av_st", [128, 3, 130], BF16)
            mv_st = sb("mv_st", [128, 3, 8, 65], BF16)
            iw_st = sb("iw_st", [128, 3, 8], F32)
            cq_sb = sb("cq_sb", [128, 2, T], F32)
            ckv_sb = sb("ckv_sb", [128, T], F32)
            sqs = [sb("sqs0", [128, T], F32), sb("sqs1", [128, T], F32)]
            rs = sb("rs", [128, T], F32)
            cqn = sb("cqn", [128, 2, T], BF16)
            ckvn = sb("ckvn", [128, T], BF16)
            bs = sb("bs", [96, T], F32)
            pA = [ps("pA0"), ps("pA1")]
            pB = [ps("pB0"), ps("pB1")]
            pC = [ps("pC0"), ps("pC1")]
            pS = ps("pS")
            em.op("pool", lambda e: e.memset(av_st.ap, 1.0), writes=[av_st.b])
            em.op("pool", lambda e: e.memset(mv_st.ap, 1.0), writes=[mv_st.b])
            rr = [0]

            def proj(out_ps, col0, M):
                hb = cur["hb"]
                for k in range(8):
                    mm(out_ps, out_ps.ap[0:M, 0:T], wAb.ap[:, k, col0:col0 + M], hb.ap[:, k, :], [wAb.b, hb.b], k == 0, k == 7)

            def roped(colA, colB, dst_tl, dst_ap):
                i = rr[0] % 2
                rr[0] += 1
                a_, b_ = pA[i], pB[i]
                proj(a_, colA, 128)
                proj(b_, colB, 128)
                tt("dve", t1[i], t1[i].ap, a_.ap[:, 0:T], c128.ap, ALU.mult, [a_.b, c128.b])
                tt("dve", t2[i], t2[i].ap, b_.ap[:, 0:T], s128.ap, ALU.mult, [b_.b, s128.b])
                tt("pool", dst_tl, dst_ap, t1[i].ap, t2[i].ap, ALU.add, [t1[i].b, t2[i].b])

            for m in range(NTL):
                n0 = m * T
                if m == 0:
                    em.dma("pool", "ld_hb", hbs[0].ap, h1_d.ap[:, 0:T].rearrange("(k p) t -> p k t", p=128),
                           reads=[h1_d.bufs[0]], writes=[hbs[0].b], nslots=2)
                if m + 1 < NTL:
                    em.dma("pool", "ld_hb", hbs[(m + 1) % 2].ap, h1_d.ap[:, n0 + T:n0 + 2 * T].rearrange("(k p) t -> p k t", p=128),
                           reads=[h1_d.bufs[m + 1]], writes=[hbs[(m + 1) % 2].b], nslots=2)
                hb = hbs[m % 2]
                cur["hb"] = hb
                for (tl_, d_) in ((c128, c128_d), (s128, s128_d), (c96, c96_d), (s96, s96_d)):
                    em.dma("sp", "ld_tab", tl_.ap, d_.ap[:, n0:n0 + T], reads=[d_.bufs[0]], writes=[tl_.b], nslots=4)
                for r in range(4):
                    roped(r * 128, 512 + r * 128, aq_st, aq_st.ap[:, r, :])
                roped(1024, 1152, ak_st, ak_st.ap)
                for r in range(4):
                    roped(1408 + r * 128, 1920 + r * 128, iq_st, iq_st.ap[:, r, :])
                roped(2432, 2560, ik_st, ik_st.ap)
                for c in range(3):
                    p_ = pC[c % 2]
                    for k in range(8):
                        mm(p_, p_.ap[:, 0:128], hb.ap[:, k, c * 128:(c + 1) * 128], wAb.ap[:, k, 1280:1408], [wAb.b, hb.b], k == 0, k == 7)
                    copy("act", av_st, av_st.ap[:, c, 0:64], p_.ap[:, 0:64], [p_.b])
                    copy("act", av_st, av_st.ap[:, c, 65:129], p_.ap[:, 64:128], [p_.b])
                    p2 = pC[(c + 1) % 2]
                    for k in range(8):
                        mm(p2, p2.ap[:, 0:8], hb.ap[:, k, c * 128:(c + 1) * 128], wAb.ap[:, k, 2688:2696], [wAb.b, hb.b], k == 0, k == 7)
                    act(iw_st, iw_st.ap[:, c, :], p2.ap[:, 0:8], AF.Copy, [p2.b], scale=IDX_SCALE)
                for c in range(2):
                    p_ = pC[c % 2]
                    proj(p_, 2696 + c * 128, 128)
                    copy("act", cq_sb, cq_sb.ap[:, c, :], p_.ap[:, 0:T], [p_.b])
                    act(sqs[c], sqs[c].ap, p_.ap[:, 0:T], AF.Square, [p_.b])
                for c in range(2):
                    mm(pS, pS.ap[:, 0:T], ones_f.ap, sqs[c].ap, [ones_f.b, sqs[c].b], c == 0, c == 1)
                act(rs, rs.ap, pS.ap[:, 0:T], AF.Sqrt, [pS.b, eps6.b], scale=1.0 / 256, bias=eps6.ap[:, 0:1])
                em.op("dve", lambda e: e.reciprocal(out=rs.ap, in_=rs.ap), reads=[rs.b], writes=[rs.b])
                for c in range(2):
                    stt(cqn, cqn.ap[:, c, :], cq_sb.ap[:, c, :], nrm.ap[:, c:c + 1], rs.ap, ALU.mult, ALU.mult, [cq_sb.b, nrm.b, rs.b])
                p_ = pC[0]
                proj(p_, 2952, 128)
                copy("act", ckv_sb, ckv_sb.ap, p_.ap[:, 0:T], [p_.b])
                act(sqs[0], sqs[0].ap, p_.ap[:, 0:T], AF.Square, [p_.b])
                mm(pS, pS.ap[:, 0:T], ones_f.ap, sqs[0].ap, [ones_f.b, sqs[0].b], True, True)
                act(rs, rs.ap, pS.ap[:, 0:T], AF.Sqrt, [pS.b, eps6.b], scale=1.0 / 128, bias=eps6.ap[:, 0:1])
                em.op("dve", lambda e: e.reciprocal(out=rs.ap, in_=rs.ap), reads=[rs.b], writes=[rs.b])
                stt(ckvn, ckvn.ap, ckv_sb.ap, nrm.ap[:, 2:3], rs.ap, ALU.mult, ALU.mult, [ckv_sb.b, nrm.b, rs.b])
                for h in range(8):
                    i = rr[0] % 2
                    rr[0] += 1
                    a_, b_ = pA[i], pB[i]
                    for k in range(2):
                        mm(a_, a_.ap[0:96, 0:T], wuqb.ap[:, k, 96 * h:96 * h + 96], cqn.ap[:, k, :], [wuqb.b, cqn.b], k == 0, k == 1)
                    for k in range(2):
                        mm(b_, b_.ap[0:96, 0:T], wuqb.ap[:, k, 768 + 96 * h:768 + 96 * h + 96], cqn.ap[:, k, :], [wuqb.b, cqn.b], k == 0, k == 1)
                    tt("dve", t1[i], t1[i].ap[0:96, :], a_.ap[0:96, 0:T], c96.ap, ALU.mult, [a_.b, c96.b])
                    tt("dve", t2[i], t2[i].ap[0:96, :], b_.ap[0:96, 0:T], s96.ap, ALU.mult, [b_.b, s96.b])
                    tt("pool", mq_st, mq_st.ap[:, h, :], t1[i].ap[0:96, :], t2[i].ap[0:96, :], ALU.add, [t1[i].b, t2[i].b])
                p_ = pC[1]
                proj(p_, 3176, 96)
                tt("dve", bs, bs.ap, p_.ap[0:96, 0:T], s96.ap, ALU.mult, [p_.b, s96.b])
                for h in range(8):
                    i = rr[0] % 2
                    rr[0] += 1
                    a_ = pA[i]
                    for k in range(8):
                        mm(a_, a_.ap[0:96, 0:T], wAb.ap[:, k, 3080:3176], hb.ap[:, k, :], [wAb.b, hb.b], k == 0, False)
                    mm(a_, a_.ap[0:96, 0:T], wukb.ap[:, 0, 96 * h:96 * h + 96], ckvn.ap, [wukb.b, ckvn.b], False, True)
                    tt("dve", t1[i], t1[i].ap[0:96, :], a_.ap[0:96, 0:T], c96.ap, ALU.mult, [a_.b, c96.b])
                    tt("pool", mk_st, mk_st.ap[:, h, :], t1[i].ap[0:96, :], bs.ap, ALU.add, [t1[i].b, bs.b])
                for c in range(3):
                    p_ = pB[c % 2]
                    mm(p_, p_.ap[:, 0:512], ckvn.ap[:, c * 128:(c + 1) * 128], wuvb.ap[:, 0, :], [wuvb.b, ckvn.b], True, True)
                    copy("act", mv_st, mv_st.ap[:, c, :, 0:64], p_.ap[:, 0:512].rearrange("p (h d) -> p h d", d=64), [p_.b])
                o = outs
                em.dma("pool", "st_q", o["aq"].ap[:, :, n0:n0 + T].rearrange("n p t -> p n t"), aq_st.ap, reads=[aq_st.b], writes=[o["aq"].bufs[m]], nslots=4)
                em.dma("pool", "st_q", o["iq"].ap[:, :, n0:n0 + T].rearrange("n p t -> p n t"), iq_st.ap, reads=[iq_st.b], writes=[o["iq"].bufs[m]], nslots=4)
                em.dma("pool", "st_q", o["mq"].ap[:, :, n0:n0 + T].rearrange("n p t -> p n t"), mq_st.ap, reads=[mq_st.b], writes=[o["mq"].bufs[m]], nslots=4)
                for h in range(8):
                    em.dma("pool", "st_q", o["mk"][h].ap[:, n0:n0 + T], mk_st.ap[:, h, :], reads=[mk_st.b], writes=[o["mk"][h].bufs[m]], nslots=4)
                em.dma("pool", "st_k", o["ak"].ap[:, n0:n0 + T], ak_st.ap, reads=[ak_st.b], writes=[o["ak"].bufs[m]], nslots=4)
                em.dma("pool", "st_k", o["ik"].ap[:, n0:n0 + T], ik_st.ap, reads=[ik_st.b], writes=[o["ik"].bufs[m]], nslots=4)
                em.dma("pool", "st_k", o["av"].ap[n0:n0 + T, :].rearrange("(c p) f -> p c f", p=128), av_st.ap, reads=[av_st.b], writes=[o["av"].bufs[m]], nslots=4)
                for (v_, h0_, h1_) in o["mvc"]:
                    em.dma("pool", "st_k", v_.ap[n0:n0 + T, :].rearrange("(c p) (h d) -> p c h d", p=128, d=65),
                           mv_st.ap[:, :, h0_:h1_, :], reads=[mv_st.b], writes=[v_.bufs[m]], nslots=4)
                em.dma("pool", "st_k", o["iw"].ap[:, 3 * m:3 * m + 3, :], iw_st.ap, reads=[iw_st.b], writes=[o["iw"].bufs[m]], nslots=4)
            em.barrier()

    def finalize_heads(accs, o_st, hlist, accsb, rrow, bc_ps, bc_sb, neg1):
        for r, h in enumerate(hlist):
            a_ = accs[r]
            sb_ = accsb[r % len(accsb)]
            copy("act", sb_, sb_.ap[0:65, :], a_.ap[0:65, 0:T], [a_.b])
            act(rrow, rrow.ap[64:65, :], sb_.ap[64:65, :], AF.Ln, [sb_.b])
            act(rrow, rrow.ap[64:65, :], rrow.ap[64:65, :], AF.Exp, [rrow.b], scale=-1.0)
            mm(bc_ps, bc_ps.ap[0:64, 0:T], ones_f.ap[64:65, 0:64], rrow.ap[64:65, :], [ones_f.b, rrow.b], True, True)
            copy("act", bc_sb, bc_sb.ap, bc_ps.ap[0:64, 0:T], [bc_ps.b])
            tt("pool", o_st, o_st.ap[:, h, :], sb_.ap[0:64, :], bc_sb.ap, ALU.mult, [sb_.b, bc_sb.b])

    def stage_dsa(ins, oa_d, idxm_d, iw_d):
        with ExitStack() as es:
            def sb(name, shape, dt):
                return TL(es.enter_context(nc.sbuf_tensor(uq(name), list(shape), dt)).ap(), name)

            def ps(name):
                return TL(es.enter_context(nc.psum_tensor(uq(name), [128, 512], F32)).ap(), name)

            idxm = sb("idxm_sb", [128, 3, 2, 3, 128], F32)
            em.dma("sp", "ld_small", idxm.ap, idxm_d.ap, reads=[idxm_d.bufs[0]], writes=[idxm.b])
            ik_sb = sb("ik_sb", [128, 2, NT], BF16)
            ak_sb = sb("ak_sb", [128, 2, NT], BF16)
            av_sb = sb("av_sb", [128, 2, NBK, 130], BF16)
            iw_sb = sb("iw_sb", [128, NBK, 8], F32)
            aw_sb = sb("aw_sb", [128, NBK, 8], F32)
            sg_sb = sb("sg_sb", [128, NBK, 8], F32)
            for j in range(2):
                em.dma("sp", "ld_k", ik_sb.ap[:, j, :], ins["ik"].ap[j], reads=ins["ik"].bufs, writes=[ik_sb.b], nslots=4)
                em.dma("sp", "ld_k", ak_sb.ap[:, j, :], ins["ak"].ap[j], reads=ins["ak"].bufs, writes=[ak_sb.b], nslots=4)
                for i0 in range(0, NBK, 11):
                    em.dma("sp", "ld_k", av_sb.ap[:, j, i0:i0 + 11, :],
                           ins["av"].ap[j, i0 * 128:(i0 + 11) * 128, :].rearrange("(i p) c -> p i c", p=128),
                           reads=ins["av"].bufs, writes=[av_sb.b], nslots=4)
            em.dma("sp", "ld_k", iw_sb.ap, iw_d.ap, reads=iw_d.bufs, writes=[iw_sb.b], nslots=4)
            act(aw_sb, aw_sb.ap, iw_sb.ap, AF.Abs, [iw_sb.b])
            act(sg_sb, sg_sb.ap, iw_sb.ap, AF.Sign, [iw_sb.b])
            scs = [sb("sc0", [128, 2 * NT], F32), sb("sc1", [128, 2 * NT], F32)]
            junk = sb("junk", [128, 2 * NT], U8)
            maskT = sb("maskT", [128, NBLK, T], FP8)
            iqz = [sb("iqz0", [128, 4, T], BF16), sb("iqz1", [128, 4, T], BF16)]
            aqz = [sb("aqz0", [128, 4, T], BF16), sb("aqz1", [128, 4, T], BF16)]
            for z_ in (iqz, aqz):
                em.op("pool", lambda e: e.memset(z_[0].ap[64:128, :, :], 0.0), writes=[z_[0].b])
                em.op("pool", lambda e: e.memset(z_[1].ap[0:64, :, :], 0.0), writes=[z_[1].b])
            dg = [sb("dg0", [128, 8, 128], BF16), sb("dg1", [128, 8, 128], BF16)]
            rl = [sb("rl%d" % i, [128, 512], BF16) for i in range(4)]
            mch = [sb("mch0", [128, 1024], BF16), sb("mch1", [128, 1024], BF16)]
            pt = [sb("pt%d" % i, [128, T], BF16) for i in range(4)]
            accsb = [sb("accsb0", [65, T], F32), sb("accsb1", [65, T], F32)]
            neg1 = sb("neg1", [65, T], F32)
            em.op("pool", lambda e: e.memset(neg1.ap, -1.0), writes=[neg1.b])
            thrs = [sb("thr0", [128, 1], F32), sb("thr1", [128, 1], F32)]
            cnts = [sb("cnt0", [128, 1], F32), sb("cnt1", [128, 1], F32)]
            dds = [sb("dd0", [128, 1], F32), sb("dd1", [128, 1], F32)]
            rrow = sb("rrow", [65, T], F32)
            bc_sb = sb("bc_sb", [64, T], F32)
            o_st = sb("o_st", [64, 8, T], BF16)
            psx = [ps("psx0"), ps("psx1")]
            accs = [ps("acc%d" % r) for r in range(4)]
            bc_ps = ps("bc_ps")
            ptr = TL(es.enter_context(nc.psum_tensor(uq("ptr"), [128, 1024], BF16)).ap(), "ptr")
            xi = [0]
            ci = [0]
            mi = [0]
            LA = 2

            def idx_block(m, c):
                nI = 3 * m + 3
                ncol = 2 * nI * 128
                ib = 3 * m + c
                sc = scs[ib % 2]
                scv = sc.ap[:, 0:ncol].rearrange("p (j i t) -> p j i t", j=2, t=128)
                dg_ = dg[ib % 2]
                for h in range(8):
                    act(dg_, dg_.ap[:, h, :], ident_b.ap, AF.Identity, [ident_b.b, sg_sb.b], scale=sg_sb.ap[:, ib, h:h + 1])
                steps = []
                for j in range(2):
                    for i0 in range(0, nI, 4):
                        for h in range(8):
                            steps.append((j, i0, h))
                held = {}
                for si in range(len(steps) + LA):
                    if si < len(steps):
                        j, i0, h = steps[si]
                        cw = min(4, nI - i0) * 128
                        r, hf = h // 2, h % 2
                        p_ = psx[xi[0] % 2]
                        r_ = rl[xi[0] % 4]
                        xi[0] += 1
                        mm(p_, p_.ap[:, 0:cw], iqz[hf].ap[:, r, c * 128:(c + 1) * 128],
                           ik_sb.ap[:, j, i0 * 128:i0 * 128 + cw], [iqz[hf].b, ik_sb.b], True, True)
                        act(r_, r_.ap[:, 0:cw], p_.ap[:, 0:cw], AF.Relu, [p_.b, aw_sb.b], scale=aw_sb.ap[:, ib, h:h + 1])
                        held[si] = r_
                    if si >= LA:
                        j, i0, h = steps[si - LA]
                        cw = min(4, nI - i0) * 128
                        col0 = (j * nI + i0) * 128
                        r_ = held.pop(si - LA)
                        if h == 0:
                            ci[0] += 1
                        a_ = accs[2 + ci[0] % 2]
                        mm(a_, a_.ap[:, 0:cw], dg_.ap[:, h, :], r_.ap[:, 0:cw], [dg_.b, r_.b], h == 0, h == 7)
                        if h == 7:
                            copy("act", sc, sc.ap[:, col0:col0 + cw], a_.ap[:, 0:cw], [a_.b])
                tt("pool", sc, scv[:, :, nI - 3:nI, :], scv[:, :, nI - 3:nI, :], idxm.ap[:, c, :, :, :], ALU.add, [sc.b, idxm.b])

            def bisect_block(m, c):
                nI = 3 * m + 3
                ncol = 2 * nI * 128
                ib = 3 * m + c
                sc, thr, cnt, dd = scs[ib % 2], thrs[ib % 2], cnts[ib % 2], dds[ib % 2]
                em.op("pool", lambda e: e.memset(thr.ap, 3.0e-5), writes=[thr.b])
                for it in range(1, BIS_IT + 1):
                    step = BIS_R / (2 ** it)
                    ts("dve", junk, junk.ap[:, 0:ncol], sc.ap[:, 0:ncol], thr.ap[:, 0:1], ALU.is_ge, [sc.b, thr.b],
                       op1=ALU.add, accum=cnt.ap[:, 0:1], extra_w=[cnt.b])
                    ts("dve", dd, dd.ap, cnt.ap, 255.5, ALU.is_ge, [cnt.b], s2=2.0 * step, op1=ALU.mult)
                    dec = -step if it < BIS_IT else -2.0 * step
                    stt(thr, thr.ap, thr.ap, dec, dd.ap, ALU.add, ALU.add, [thr.b, dd.b])

            def mask_block(m, c):
                nI = 3 * m + 3
                ib = 3 * m + c
                sc, thr = scs[ib % 2], thrs[ib % 2]
                nkb = 2 * nI
                for k0 in range(0, nkb, 8):
                    nk = min(8, nkb - k0)
                    mc = mch[mi[0] % 2]
                    mi[0] += 1
                    ts("dve", mc, mc.ap[:, 0:nk * 128], sc.ap[:, k0 * 128:(k0 + nk) * 128], thr.ap[:, 0:1], ALU.is_ge, [sc.b, thr.b])
                    for q in range(nk):
                        em.op("pe", lambda e: e.transpose(out=ptr.ap[:, q * 128:(q + 1) * 128], in_=mc.ap[:, q * 128:(q + 1) * 128], identity=ident_b.ap),
                              reads=[mc.b, ident_b.b], writes=[ptr.b])
                    em.op("act", lambda e: e.activation(out=maskT.ap[:, k0:k0 + nk, c * 128:(c + 1) * 128],
                                                        in_=ptr.ap[:, 0:nk * 128].rearrange("p (k t) -> p k t", t=128),
                                                        func=AF.Identity, scale=240.0, bias=m240.ap[:, 0:1], saturate=False),
                          reads=[ptr.b, m240.b], writes=[maskT.b])

            def load_iq(m):
                n0 = m * T
                for hf_ in range(2):
                    pr_ = slice(64 * hf_, 64 * hf_ + 64)
                    em.dma("sp", "ld_q", iqz[hf_].ap[pr_, :, :], ins["iq"].ap[:, pr_, n0:n0 + T].rearrange("n p t -> p n t"),
                           reads=[ins["iq"].bufs[m]], writes=[iqz[hf_].b], nslots=4)

            def load_aq(m):
                n0 = m * T
                for hf_ in range(2):
                    pr_ = slice(64 * hf_, 64 * hf_ + 64)
                    em.dma("sp", "ld_q", aqz[hf_].ap[pr_, :, :], ins["aq"].ap[:, pr_, n0:n0 + T].rearrange("n p t -> p n t"),
                           reads=[ins["aq"].bufs[m]], writes=[aqz[hf_].b], nslots=4)

            def attention(m):
                n0 = m * T
                nI = 3 * m + 3
                nkb = 2 * nI
                for g in range(2):
                    for half in range(2):
                        rs_ = [2 * half, 2 * half + 1]
                        steps = [(kb, r) for kb in range(nkb) for r in rs_]
                        held = {}
                        for si in range(len(steps) + LA):
                            if si < len(steps):
                                kb, r = steps[si]
                                j, i = kb // nI, kb % nI
                                p_ = psx[xi[0] % 2]
                                e_ = pt[xi[0] % 4]
                                xi[0] += 1
                                mm(p_, p_.ap[:, 0:T], ak_sb.ap[:, j, i * 128:(i + 1) * 128], aqz[g].ap[:, r, :], [ak_sb.b, aqz[g].b], True, False)
                                mm(p_, p_.ap[:, 0:T], ident_b.ap, maskT.ap[:, kb, :], [ident_b.b, maskT.b], False, True)
                                act(e_, e_.ap, p_.ap[:, 0:T], AF.Exp, [p_.b], scale=0.125)
                                held[si] = e_
                            if si >= LA:
                                kb, r = steps[si - LA]
                                j, i = kb // nI, kb % nI
                                m_ = held.pop(si - LA)
                                a_ = accs[r % 2]
                                mm(a_, a_.ap[0:65, 0:T], av_sb.ap[:, j, i, 65 * g:65 * g + 65], m_.ap, [av_sb.b, m_.b], kb == 0, kb == nkb - 1)
                        finalize_heads(accs[0:2], o_st, [4 * g + r for r in rs_], accsb, rrow, bc_ps, bc_sb, neg1)
                em.dma("pool", "st_o", oa_d.ap[:, :, n0:n0 + T].rearrange("n p t -> p n t"), o_st.ap, reads=[o_st.b], writes=[oa_d.bufs[m]])

            load_iq(0)
            idx_block(0, 0)
            bisect_block(0, 0)
            idx_block(0, 1)
            mask_block(0, 0)
            bisect_block(0, 1)
            idx_block(0, 2)
            mask_block(0, 1)
            bisect_block(0, 2)
            if NTL > 1:
                load_iq(1)
                idx_block(1, 0)
            mask_block(0, 2)
            load_aq(0)
            for m in range(NTL):
                if m + 1 < NTL:
                    bisect_block(m + 1, 0)
                    idx_block(m + 1, 1)
                    bisect_block(m + 1, 1)
                attention(m)
                if m + 1 < NTL:
                    load_aq(m + 1)
                    mask_block(m + 1, 0)
                    idx_block(m + 1, 2)
                    mask_block(m + 1, 1)
                    bisect_block(m + 1, 2)
                    if m + 2 < NTL:
                        load_iq(m + 2)
                        idx_block(m + 2, 0)
                    mask_block(m + 1, 2)
            em.barrier()

    def stage_mla(ins, ob_d, cmT_d):
        with ExitStack() as es:
            def sb(name, shape, dt):
                return TL(es.enter_context(nc.sbuf_tensor(uq(name), list(shape), dt)).ap(), name)

            def ps(name):
                return TL(es.enter_context(nc.psum_tensor(uq(name), [128, 512], F32)).ap(), name)

            cmT = sb("cmT_sb", [128, 6, T], BF16)
            em.dma("sp", "ld_small", cmT.ap, cmT_d.ap, reads=[cmT_d.bufs[0]], writes=[cmT.b])
            mk_sb = sb("mk_sb", [96, 2, 4, NT], BF16)
            mv_sb = sb("mv_sb", [128, 2, NBK, 260], BF16)
            mq_sb = sb("mq_sb", [96, 4, T], BF16)
            pt = [sb("pt%d" % i, [128, T], BF16) for i in range(4)]
            pm = [sb("pm%d" % i, [128, T], BF16) for i in range(4)]
            accsb = [sb("accsb0", [65, T], F32), sb("accsb1", [65, T], F32)]
            neg1 = sb("neg1", [65, T], F32)
            em.op("pool", lambda e: e.memset(neg1.ap, -1.0), writes=[neg1.b])
            rrow = sb("rrow", [65, T], F32)
            bc_sb = sb("bc_sb", [64, T], F32)
            o_st = sb("o_st", [64, 8, T], BF16)
            psx = [ps("psx0"), ps("psx1"), ps("psx2")]
            accs = [ps("acc%d" % r) for r in range(4)]
            bc_ps = ps("bc_ps")
            xi = [0]
            LA = 2
            for hh in range(2):
                for j in range(2):
                    for r in range(4):
                        mkv = ins["mk"][4 * hh + r]
                        em.dma("sp", "ld_k", mk_sb.ap[:, j, r, :], mkv.ap[j], reads=mkv.bufs, writes=[mk_sb.b], nslots=4)
                        mvv = ins["mvh"][4 * hh + r]
                        for i0 in range(0, NBK, 11):
                            em.dma("sp", "ld_k", mv_sb.ap[:, j, i0:i0 + 11, 65 * r:65 * r + 65],
                                   mvv.ap[j, i0 * 128:(i0 + 11) * 128, :].rearrange("(i p) c -> p i c", p=128),
                                   reads=mvv.bufs, writes=[mv_sb.b], nslots=4)
                for m in range(NTL):
                    n0 = m * T
                    nI = 3 * m + 3
                    nkb = 2 * nI
                    em.dma("sp", "ld_q", mq_sb.ap, ins["mq"].ap[4 * hh:4 * hh + 4, :, n0:n0 + T].rearrange("n p t -> p n t"),
                           reads=[ins["mq"].bufs[m]], writes=[mq_sb.b], nslots=4)
                    steps = [(kb, r) for kb in range(nkb) for r in range(4)]
                    held = {}
                    for si in range(len(steps) + LA):
                        if si < len(steps):
                            kb, r = steps[si]
                            j, i = kb // nI, kb % nI
                            zone = i >= nI - 3
                            p_ = psx[xi[0] % 3]
                            e_ = pt[xi[0] % 4]
                            m_ = pm[xi[0] % 4]
                            xi[0] += 1
                            mm(p_, p_.ap[:, 0:T], mk_sb.ap[:, j, r, i * 128:(i + 1) * 128], mq_sb.ap[:, r, :], [mk_sb.b, mq_sb.b], True, True)
                            act(e_, e_.ap, p_.ap[:, 0:T], AF.Exp, [p_.b], scale=float(96 ** -0.5))
                            if zone:
                                z = j * 3 + (i - (nI - 3))
                                tt("pool" if (xi[0] % 2) else "dve", m_, m_.ap, e_.ap, cmT.ap[:, z, :], ALU.mult, [e_.b, cmT.b])
                                held[si] = m_
                            else:
                                held[si] = e_
                        if si >= LA:
                            kb, r = steps[si - LA]
                            j, i = kb // nI, kb % nI
                            src = held.pop(si - LA)
                            mm(accs[r], accs[r].ap[0:65, 0:T], mv_sb.ap[:, j, i, 65 * r:65 * r + 65], src.ap, [mv_sb.b, src.b], kb == 0, kb == nkb - 1)
                    finalize_heads(accs, o_st, [r for r in range(4)], accsb, rrow, bc_ps, bc_sb, neg1)
                    em.dma("pool", "st_o", ob_d.ap[4 * hh:4 * hh + 4, :, n0:n0 + T].rearrange("n p t -> p n t"), o_st.ap[:, 0:4, :],
                           reads=[o_st.b], writes=[ob_d.bufs[m]])
                em.barrier()

    def stage_merge(h1_d, oa_d, ob_d, wg_d, wba_d, wbb_d, wout_d, lnp, li, hout_d):
        with ExitStack() as es:
            def sb(name, shape, dt):
                return TL(es.enter_context(nc.sbuf_tensor(uq(name), list(shape), dt)).ap(), name)

            def ps(name):
                return TL(es.enter_context(nc.psum_tensor(uq(name), [128, 512], F32)).ap(), name)

            wgb = sb("wgb", [128, 8, 2048], BF16)
            wbab = sb("wbab", [128, 4, D], BF16)
            wbbb = sb("wbbb", [128, 4, D], BF16)
            woutb = sb("woutb", [128, 8, D], BF16)
            stg = [sb("stg0", [128, 704], F32), sb("stg1", [128, 704], F32), sb("stg2", [128, 704], F32)]
            load_w(es, wg_d, wgb, 8, 2048, stg)
            load_w(es, wba_d, wbab, 4, D, stg)
            load_w(es, wbb_d, wbbb, 4, D, stg)
            load_w(es, wout_d, woutb, 8, D, stg)
            hin = sb("hin", [128, 8, T], F32)
            hb = sb("hb", [128, 8, T], BF16)
            oa = sb("oa", [128, 4, T], BF16)
            ob = sb("ob", [128, 4, T], BF16)
            mixed = sb("mixed", [128, 8, T], BF16)
            xr = [sb("xr%d" % j, [128, T], F32) for j in range(8)]
            sa = [sb("sa0", [128, T], F32), sb("sa1", [128, T], F32)]
            sbb = [sb("sbb0", [128, T], F32), sb("sbb1", [128, T], F32)]
            sq = [sb("sq0", [128, T], F32), sb("sq1", [128, T], F32)]
            tmp = [sb("tmp0", [128, T], F32), sb("tmp1", [128, T], F32)]
            st = {n: sb("ln_" + n, [128, T], F32) for n in ("mean", "msq", "var", "rstd", "bt")}
            pga, pgb, pba, pbb = ps("pga"), ps("pgb"), ps("pba"), ps("pbb")
            py = [ps("py0"), ps("py1")]
            ps1 = ps("ps1")
            ps2 = ps("ps2")
            for m in range(NTL):
                n0 = m * T
                em.dma("sp", "ld_h", hin.ap, h1_d.ap[:, n0:n0 + T].rearrange("(k p) t -> p k t", p=128), reads=[h1_d.bufs[m]], writes=[hin.b])
                for two in range(2):
                    pr_ = slice(64 * two, 64 * two + 64)
                    em.dma("sp", "ld_o", oa.ap[pr_, :, :], oa_d.ap[:, :, n0:n0 + T].rearrange("(q w) p t -> w p q t", w=2)[two],
                           reads=[oa_d.bufs[m]], writes=[oa.b], nslots=4)
                    em.dma("sp", "ld_o", ob.ap[pr_, :, :], ob_d.ap[:, :, n0:n0 + T].rearrange("(q w) p t -> w p q t", w=2)[two],
                           reads=[ob_d.bufs[m]], writes=[ob.b], nslots=4)
                copy("dve", hb, hb.ap[:, 0:4, :], hin.ap[:, 0:4, :], [hin.b])
                copy("pool", hb, hb.ap[:, 4:8, :], hin.ap[:, 4:8, :], [hin.b])
                for j in range(8):
                    cs = slice(j * 128, (j + 1) * 128)
                    for k in range(8):
                        mm(pga, pga.ap[:, 0:T], wgb.ap[:, k, cs], hb.ap[:, k, :], [wgb.b, hb.b], k == 0, k == 7)
                    for k in range(8):
                        mm(pgb, pgb.ap[:, 0:T], wgb.ap[:, k, 1024 + j * 128:1024 + (j + 1) * 128], hb.ap[:, k, :], [wgb.b, hb.b], k == 0, k == 7)
                    for k in range(4):
                        mm(pba, pba.ap[:, 0:T], wbab.ap[:, k, cs], oa.ap[:, k, :], [wbab.b, oa.b], k == 0, k == 3)
                    for k in range(4):
                        mm(pbb, pbb.ap[:, 0:T], wbbb.ap[:, k, cs], ob.ap[:, k, :], [wbbb.b, ob.b], k == 0, k == 3)
                    s1_, s2_ = sa[j % 2], sbb[j % 2]
                    act(s1_, s1_.ap, pga.ap[:, 0:T], AF.Sigmoid, [pga.b])
                    act(s2_, s2_.ap, pgb.ap[:, 0:T], AF.Sigmoid, [pgb.b])
                    tt("dve", s1_, s1_.ap, s1_.ap, pba.ap[:, 0:T], ALU.mult, [s1_.b, pba.b])
                    tt("dve", s2_, s2_.ap, s2_.ap, pbb.ap[:, 0:T], ALU.mult, [s2_.b, pbb.b])
                    tt("pool", mixed, mixed.ap[:, j, :], s1_.ap, s2_.ap, ALU.add, [s1_.b, s2_.b])
                for j in range(8):
                    y_ = py[j % 2]
                    for k in range(8):
                        mm(y_, y_.ap[:, 0:T], woutb.ap[:, k, j * 128:(j + 1) * 128], mixed.ap[:, k, :], [woutb.b, mixed.b], k == 0, k == 7)
                    act(xr[j], xr[j].ap, hin.ap[:, j, :], AF.Copy, [hin.b], scale=ALPHA)
                    stt(xr[j], xr[j].ap, y_.ap[:, 0:T], 1.0, xr[j].ap, ALU.mult, ALU.add, [y_.b, xr[j].b])
                layer_norm(xr, lnp, li, ps1, ps2, sq, tmp, st)
                for j in range(8):
                    em.dma("pool", "st_h", hout_d.ap[j * 128:(j + 1) * 128, n0:n0 + T], xr[j].ap,
                           reads=[xr[j].b], writes=[hout_d.bufs[m]], nslots=4)
            em.barrier()

    class V:
        def __init__(self, ap, bufs):
            self.ap = ap
            self.bufs = bufs

    def internal(name, shape, dt, n=NTL):
        return dram(name, shape, dt, "Internal", n)

    FMR = [224, 224, 192, 192, 192]
    TMC = [195, 195, 195, 65]
    MKLOC = [(0, 128), (1, 128), (2, 0), (2, 96), (3, 0), (3, 96), (4, 0), (4, 96)]
    MVLOC = [(0, 130), (1, 0), (1, 65), (1, 130), (2, 0), (2, 65), (2, 130), (3, 0)]

    def exchange(L, fm, tm):
        fa, ta = [], []
        for i, t_ in enumerate(fm):
            A = internal("fmA%d_%d" % (L, i), [2 * FMR[i], NT], BF16, 1)
            em.coll(t_.ap, A.ap, reads=t_.bufs, writes=A.bufs)
            fa.append(V(A.ap.rearrange("(j r) t -> j r t", j=2), A.bufs))
        for i, t_ in enumerate(tm):
            A = internal("tmA%d_%d" % (L, i), [2 * NT, TMC[i]], BF16, 1)
            em.coll(t_.ap, A.ap, reads=t_.bufs, writes=A.bufs)
            ta.append(V(A.ap.rearrange("(j r) c -> j r c", j=2), A.bufs))
        return {"ak": V(fa[0].ap[:, 0:128, :], fa[0].bufs), "ik": V(fa[1].ap[:, 0:128, :], fa[1].bufs),
                "mk": [V(fa[c].ap[:, r:r + 96, :], fa[c].bufs) for (c, r) in MKLOC],
                "av": V(ta[0].ap[:, :, 0:130], ta[0].bufs),
                "mvh": [V(ta[c].ap[:, :, k:k + 65], ta[c].bufs) for (c, k) in MVLOC]}

    tabs = (din("c128", [128, NT], F32, 1), din("s128", [128, NT], F32, 1),
            din("c96", [96, NT], F32, 1), din("s96", [96, NT], F32, 1))
    idxm_d = din("idxm", [128, 3, 2, 3, 128], F32, 1)
    cmT_d = din("cmT", [128, 6, T], BF16, 1)
    hin = din("xT", [D, NT], F32)
    for L in range(DEPTH):
        lnp = small_in("lnp_%d" % L, [128, 48])
        nrm = small_in("nrm_%d" % L, [128, 3])
        w = dict(w13a=din("w13a_%d" % L, [D, 2 * DFF], F32, 1), w2a=din("w2a_%d" % L, [DFF, D], F32, 1),
                 wA=din("wA_%d" % L, [D, NA], F32, 1), wuq=din("wuq_%d" % L, [256, 1536], F32, 1),
                 wuk=din("wuk_%d" % L, [128, 768], F32, 1), wuv=din("wuv_%d" % L, [128, 512], F32, 1),
                 wg=din("wg_%d" % L, [D, 2048], F32, 1), wba=din("wba_%d" % L, [512, D], F32, 1),
                 wbb=din("wbb_%d" % L, [512, D], F32, 1), wout=din("wout_%d" % L, [D, D], F32, 1),
                 w13b=din("w13b_%d" % L, [D, 2 * DFF], F32, 1), w2b=din("w2b_%d" % L, [DFF, D], F32, 1))
        h1 = internal("h1_%d" % L, [D, NT], F32)
        stage_ffn(hin, h1, w["w13a"], w["w2a"], lnp, 0)
        fm = [internal("fm%d_%d" % (L, i), [r_, NT], BF16) for i, r_ in enumerate(FMR)]
        tm = [internal("tm%d_%d" % (L, i), [NT, c_], BF16) for i, c_ in enumerate(TMC)]
        q = {"aq": internal("aq%d" % L, [4, 128, NT], BF16), "iq": internal("iq%d" % L, [4, 128, NT], BF16),
             "mq": internal("mq%d" % L, [8, 96, NT], BF16)}
        iw = internal("iw%d" % L, [128, NBK, 8], F32)
        outs = dict(q)
        outs.update({"ak": V(fm[0].ap[0:128, :], fm[0].bufs), "ik": V(fm[1].ap[0:128, :], fm[1].bufs),
                     "mk": [V(fm[c].ap[r:r + 96, :], fm[c].bufs) for (c, r) in MKLOC],
                     "av": V(tm[0].ap[:, 0:130], tm[0].bufs),
                     "mvc": [(V(tm[0].ap[:, 130:195], tm[0].bufs), 0, 1), (V(tm[1].ap, tm[1].bufs), 1, 4),
                             (V(tm[2].ap, tm[2].bufs), 4, 7), (V(tm[3].ap, tm[3].bufs), 7, 8)],
                     "iw": iw})
        stage_proj(h1, w["wA"], w["wuq"], w["wuk"], w["wuv"], nrm, tabs, outs)
        ins = dict(q)
        ins.update(exchange(L, fm, tm))
        oa = internal("oa%d" % L, [8, 64, NT], BF16)
        ob = internal("ob%d" % L, [8, 64, NT], BF16)
        h2 = internal("h2_%d" % L, [D, NT], F32)
        stage_dsa(ins, oa, idxm_d, iw)
        stage_mla(ins, ob, cmT_d)
        stage_merge(h1, oa, ob, w["wg"], w["wba"], w["wbb"], w["wout"], lnp, 1, h2)
        h3 = dout("outT", [D, NT], F32) if L == DEPTH - 1 else internal("h3_%d" % L, [D, NT], F32)
        stage_ffn(h2, h3, w["w13b"], w["w2b"], lnp, 2)
        hin = h3
    em.barrier()
    gs.close()
    return nc, em


def _rope_perm64():
    p = np.arange(64)
    p[0:8] = np.arange(8, 16)
    p[8:16] = np.arange(0, 8)
    return p


def _prep_weights(inp, L):
    f = np.float32
    w_in = np.asarray(inp["w_in"][L], f)
    perm = _rope_perm64()
    a_q = w_in[:, 0:512].reshape(D, 8, 64)
    a_k = w_in[:, 512:640].reshape(D, 2, 64)
    a_v = w_in[:, 640:768]
    i_q = w_in[:, 768:1280].reshape(D, 8, 64)
    i_k = w_in[:, 1280:1344]
    i_w = w_in[:, 1344:1352]
    c_q = w_in[:, 1352:1608]
    c_kv = w_in[:, 1608:1736]
    k_r = w_in[:, 1736:1768]
    gates = w_in[:, 1768:3816]
    aq_pair = np.stack([np.concatenate([a_q[:, r], a_q[:, 4 + r]], 1) for r in range(4)], 1)
    aq_pairB = np.stack([np.concatenate([a_q[:, r][:, perm], a_q[:, 4 + r][:, perm]], 1) for r in range(4)], 1)
    akA = a_k.reshape(D, 128)
    akB = a_k[:, :, perm].reshape(D, 128)
    iqA = i_q.reshape(D, 512)
    iqB = i_q[:, :, perm].reshape(D, 512)
    ikA = np.concatenate([i_k, i_k], 1)
    ikB = np.concatenate([i_k[:, perm], i_k[:, perm]], 1)
    krA = np.zeros((D, 96), f)
    krA[:, 64:96] = k_r
    krB = np.zeros((D, 96), f)
    krB[:, 64:80] = k_r[:, 16:32]
    krB[:, 80:96] = k_r[:, 0:16]
    wA = np.concatenate([aq_pair.reshape(D, 512), aq_pairB.reshape(D, 512), akA, akB, a_v, iqA, iqB, ikA, ikB,
                         i_w, c_q, c_kv, krA, krB], 1)
    assert wA.shape[1] == NA, wA.shape
    w_uq = np.asarray(inp["mla_w_uq"][L], f).reshape(256, 8, 96)
    uqB = w_uq.copy()
    uqB[:, :, 64:80] = w_uq[:, :, 80:96]
    uqB[:, :, 80:96] = w_uq[:, :, 64:80]
    wuq = np.concatenate([w_uq.reshape(256, 768), uqB.reshape(256, 768)], 1)
    w_ukv = np.asarray(inp["mla_w_ukv"][L], f).reshape(128, 8, 128)
    wuk = np.zeros((128, 8, 96), f)
    wuk[:, :, 0:64] = w_ukv[:, :, 0:64]
    wuv = np.ascontiguousarray(w_ukv[:, :, 64:128]).reshape(128, 512)
    ln_g = np.asarray(inp["ln_g"][L], f)
    ln_b = np.asarray(inp["ln_b"][L], f)
    lnp = np.zeros((128, 48), f)
    for li in range(3):
        lnp[:, li * 16:li * 16 + 8] = ln_g[li].reshape(8, 128).T
        lnp[:, li * 16 + 8:li * 16 + 16] = ln_b[li].reshape(8, 128).T
    nrm = np.zeros((128, 3), f)
    nrm[:, 0:2] = np.asarray(inp["mla_q_norm"][L], f).reshape(2, 128).T
    nrm[:, 2] = np.asarray(inp["mla_kv_norm"][L], f)
    c = np.ascontiguousarray
    return {
        "w13a_%d" % L: c(np.asarray(inp["ffn1_w13"][L], f)), "w2a_%d" % L: c(np.asarray(inp["ffn1_w2"][L], f)),
        "wA_%d" % L: c(wA), "wuq_%d" % L: c(wuq), "wuk_%d" % L: c(wuk.reshape(128, 768)), "wuv_%d" % L: c(wuv),
        "wg_%d" % L: c(gates), "wba_%d" % L: c(np.asarray(inp["w_branch_a"][L], f)),
        "wbb_%d" % L: c(np.asarray(inp["w_branch_b"][L], f)), "wout_%d" % L: c(np.asarray(inp["w_out"][L], f)),
        "w13b_%d" % L: c(np.asarray(inp["ffn2_w13"][L], f)), "w2b_%d" % L: c(np.asarray(inp["ffn2_w2"][L], f)),
        "lnp_%d" % L: lnp, "nrm_%d" % L: nrm,
    }


def _tables(j):
    n = np.arange(NT)
    pos = ((2 * (n // 128) + j) * 128 + n % 128).astype(np.float32)
    inv16 = (THETA ** (-(np.arange(8, dtype=np.float32) * 2.0 / 16))).astype(np.float32)
    inv32 = (THETA ** (-(np.arange(16, dtype=np.float32) * 2.0 / 32))).astype(np.float32)
    a16 = pos[None, :] * inv16[:, None]
    a32 = pos[None, :] * inv32[:, None]
    c64 = np.ones((64, NT), np.float32)
    s64 = np.zeros((64, NT), np.float32)
    c64[0:8] = np.cos(a16)
    c64[8:16] = np.cos(a16)
    s64[0:8] = -np.sin(a16)
    s64[8:16] = np.sin(a16)
    c96 = np.ones((96, NT), np.float32)
    s96 = np.zeros((96, NT), np.float32)
    c96[64:80] = np.cos(a32)
    c96[80:96] = np.cos(a32)
    s96[64:80] = -np.sin(a32)
    s96[80:96] = np.sin(a32)
    return {"c128": np.concatenate([c64, c64], 0), "s128": np.concatenate([s64, s64], 0), "c96": c96, "s96": s96}


def _masks(j):
    idxm = np.zeros((128, 3, 2, 3, 128), np.float32)
    cmT = np.zeros((128, 6, T), np.float32)
    t = np.arange(128)
    for c in range(3):
        qbz = 2 * c + j
        for jp in range(2):
            for cp in range(3):
                kbz = 2 * cp + jp
                if kbz < qbz:
                    vis = np.ones((128, 128), bool)
                elif kbz == qbz:
                    vis = t[None, :] <= t[:, None]
                else:
                    vis = np.zeros((128, 128), bool)
                idxm[:, c, jp, cp, :] = np.where(vis, 0.0, NEGM)
                cmT[:, jp * 3 + cp, c * 128:(c + 1) * 128] = vis.T.astype(np.float32)
    return {"idxm": idxm, "cmT": cmT.astype(ml_dtypes.bfloat16)}


_PROG = []


def _prog():
    if not _PROG:
        _PROG.append(build_program()[0])
    return _PROG[0]


def kernel(x, meta_tokens, ln_g, ln_b, ffn1_w13, ffn1_w2, w_in, mla_q_norm, mla_kv_norm,
           mla_w_uq, mla_w_ukv, w_branch_a, w_branch_b, w_out, ffn2_w13, ffn2_w2):
    inp = dict(x=x, meta_tokens=meta_tokens, ln_g=ln_g, ln_b=ln_b, ffn1_w13=ffn1_w13, ffn1_w2=ffn1_w2, w_in=w_in,
               mla_q_norm=mla_q_norm, mla_kv_norm=mla_kv_norm, mla_w_uq=mla_w_uq, mla_w_ukv=mla_w_ukv,
               w_branch_a=w_branch_a, w_branch_b=w_branch_b, w_out=w_out, ffn2_w13=ffn2_w13, ffn2_w2=ffn2_w2)
    x = np.asarray(x, np.float32)
    meta = np.asarray(meta_tokens, np.float32)
    W = {}
    for L in range(DEPTH):
        W.update(_prep_weights(inp, L))
    tabs = [_tables(j) for j in range(2)]
    msk = [_masks(j) for j in range(2)]
    maps = []
    for c in range(8):
        b, j = c // 2, c % 2
        h0 = np.zeros((NBLK * 128, D), np.float32)
        h0[0:N_META] = meta
        h0[N_META:N_META + SEQ] = x[b]
        mine = h0.reshape(NBLK, 128, D)[j::2].reshape(NT, D)
        m = dict(W)
        m.update(tabs[j])
        m.update(msk[j])
        m["xT"] = np.ascontiguousarray(mine.T)
        maps.append(m)
    res = run_bass_kernel_spmd(_prog(), maps, core_ids=list(range(8))).results
    out = np.zeros((BATCH, SEQ, D), np.float32)
    for b in range(BATCH):
        full = np.zeros((NBLK, 128, D), np.float32)
        for j in range(2):
            full[j::2] = np.asarray(res[2 * b + j]["outT"]).T.reshape(NBK, 128, D)
        out[b] = full.reshape(NBLK * 128, D)[N_META:N_META + SEQ]
    return out
```

```python
import numpy as np
import ml_dtypes
from contextlib import ExitStack
import concourse.bass as bass
import concourse.mybir as mybir
from concourse.bass_utils import run_bass_kernel_spmd

F32 = mybir.dt.float32
BF16 = mybir.dt.bfloat16
U8 = mybir.dt.uint8
FP8 = mybir.dt.float8e4
AF = mybir.ActivationFunctionType
ALU = mybir.AluOpType

P = 128
T = 384
NTL = 11
NBK = 33
NT = NBK * 128
NBLK = 66
D = 1024
DFF = 2816
N_META = 16
SEQ = 8192
BATCH = 4
DEPTH = 2
ALPHA = float((2 * DEPTH) ** 0.25)
THETA = 500000.0
NEGM = -1.0e4
NA = 3272
IDX_SCALE = float(64 ** -0.5 * 8 ** -0.5)
BIS_R = 16.0
BIS_IT = 17


class Buf:
    __slots__ = ("w", "r", "name")

    def __init__(self, name=""):
        self.w = {}
        self.r = {}
        self.name = name


class TL:
    __slots__ = ("ap", "b")

    def __init__(self, ap, name=""):
        self.ap = ap
        self.b = Buf(name)


class Em:
    def __init__(self, nc):
        self.nc = nc
        self.engs = {"pe": nc.tensor, "act": nc.scalar, "dve": nc.vector, "pool": nc.gpsimd, "sp": nc.sync}
        self.sems = {}
        self.cnt = {}
        self.seen = {k: {} for k in self.engs}
        self.grp = {}
        for k in self.engs:
            self.sems[k] = nc.alloc_semaphore("s_" + k)
            self.cnt[k] = 0
        self.nwait = 0
        self.nins = 0

    def _wait(self, eng, k, c):
        if self.seen[eng].get(k, 0) < c:
            self.engs[eng].wait_ge(self.sems[k], c)
            self.seen[eng][k] = c
            self.nwait += 1

    def _deps(self, eng, reads, writes):
        need = {}
        for b in reads:
            for k, c in b.w.items():
                if need.get(k, 0) < c:
                    need[k] = c
        for b in writes:
            for d in (b.w, b.r):
                for k, c in d.items():
                    if need.get(k, 0) < c:
                        need[k] = c
        for k, c in need.items():
            if k == eng and eng == "pe":
                continue
            self._wait(eng, k, c)

    def op(self, eng, fn, reads=(), writes=()):
        self._deps(eng, reads, writes)
        ins = fn(self.engs[eng])
        self.cnt[eng] += 1
        self.nins += 1
        ins.then_inc(self.sems[eng], 1)
        c = self.cnt[eng]
        for b in writes:
            b.w[eng] = c
            b.r = {}
        for b in reads:
            if b.r.get(eng, 0) < c:
                b.r[eng] = c
        return ins

    def dma(self, q, grp, out, in_, reads=(), writes=(), nslots=2):
        st = self.grp.setdefault(grp, [0, nslots])
        slot = st[0] % st[1]
        st[0] += 1
        key = (grp, slot)
        if key not in self.sems:
            self.sems[key] = self.nc.alloc_semaphore("d_%s_%d" % (grp, slot))
            self.cnt[key] = 0
        if self.cnt[key] > 0:
            self._wait(q, key, self.cnt[key])
        self._deps(q, reads, writes)
        ins = self.engs[q].dma_start(out=out, in_=in_)
        self.cnt[key] += 16
        self.nins += 1
        ins.then_inc(self.sems[key], 16)
        c = self.cnt[key]
        for b in writes:
            b.w[key] = c
            b.r = {}
        for b in reads:
            if b.r.get(key, 0) < c:
                b.r[key] = c
        return ins

    def coll(self, in_ap, out_ap, reads=(), writes=()):
        q, key = "pool", "cc"
        if key not in self.sems:
            self.sems[key] = self.nc.alloc_semaphore("s_cc")
            self.cnt[key] = 0
        if self.cnt[key] > 0:
            self._wait(q, key, self.cnt[key])
        self._deps(q, reads, writes)
        ins = self.engs[q].collective_compute("AllGather", ALU.bypass, replica_groups=[[0, 1], [2, 3], [4, 5], [6, 7]],
                                              ins=[in_ap], outs=[out_ap])
        self.cnt[key] += 1
        self.nins += 1
        ins.then_inc(self.sems[key])
        c = self.cnt[key]
        for b in writes:
            b.w[key] = c
            b.r = {}
        for b in reads:
            if b.r.get(key, 0) < c:
                b.r[key] = c
        return ins

    def barrier(self, engines=None):
        for e in (engines or list(self.engs)):
            for k, c in self.cnt.items():
                if k != e and c > 0:
                    self._wait(e, k, c)


class DT:
    def __init__(self, ap, n=NTL, name=""):
        self.ap = ap
        self.bufs = [Buf(name + str(i)) for i in range(n)]


def build_program():
    nc = bass.Bass("TRN2", target_bir_lowering=False)
    em = Em(nc)
    dts = {}

    def dram(name, shape, dt, kind, n=NTL):
        t = DT(nc.dram_tensor(name, list(shape), dt, kind=kind).ap(), n=n, name=name)
        dts[name] = t
        return t

    def din(name, shape, dt, n=NTL):
        return dram(name, shape, dt, "ExternalInput", n)

    def dout(name, shape, dt, n=NTL):
        return dram(name, shape, dt, "ExternalOutput", n)

    gs = ExitStack()
    uid = [0]

    def uq(name):
        uid[0] += 1
        return "%s__u%d" % (name, uid[0])

    def galloc(name, shape, dt):
        return TL(gs.enter_context(nc.sbuf_tensor(name, list(shape), dt)).ap(), name)

    ones_f = galloc("ones_f", [128, 128], F32)
    ident_b = galloc("ident_b", [128, 128], BF16)
    eps5 = galloc("eps5", [128, 1], F32)
    eps6 = galloc("eps6", [128, 1], F32)
    m240 = galloc("m240", [128, 1], F32)
    em.op("pool", lambda e: e.memset(ones_f.ap, 1.0), writes=[ones_f.b])
    em.op("pool", lambda e: e.memset(eps5.ap, 1e-5), writes=[eps5.b])
    em.op("pool", lambda e: e.memset(eps6.ap, 1e-6), writes=[eps6.b])
    em.op("pool", lambda e: e.memset(m240.ap, -240.0), writes=[m240.b])
    em.op("pool", lambda e: e.memset(ident_b.ap, 0.0), writes=[ident_b.b])
    em.op("pool", lambda e: e.affine_select(out=ident_b.ap, in_=ident_b.ap, pattern=[[-1, 128]],
                                            compare_op=ALU.not_equal, fill=1.0, base=0, channel_multiplier=1),
          reads=[ident_b.b], writes=[ident_b.b])

    def small_in(name, shape):
        d = din(name, shape, F32, n=1)
        t = galloc(name + "_sb", shape, F32)
        em.dma("sp", "ld_small", t.ap, d.ap, reads=[d.bufs[0]], writes=[t.b])
        return t

    def mm(out_tl, out_ap, lhsT, rhs, reads, start, stop):
        em.op("pe", lambda e: e.matmul(out_ap, lhsT=lhsT, rhs=rhs, start=start, stop=stop),
              reads=reads, writes=[out_tl.b])

    def act(out_tl, out_ap, in_ap, func, reads, scale=None, bias=None):
        kw = {}
        if scale is not None:
            kw["scale"] = scale
        if bias is not None:
            kw["bias"] = bias
        em.op("act", lambda e: e.activation(out=out_ap, in_=in_ap, func=func, **kw), reads=reads, writes=[out_tl.b])

    def tt(eng, out_tl, out_ap, in0, in1, op, reads):
        em.op(eng, lambda e: e.tensor_tensor(out=out_ap, in0=in0, in1=in1, op=op), reads=reads, writes=[out_tl.b])

    def ts(eng, out_tl, out_ap, in0, s1, op0, reads, s2=None, op1=None, accum=None, extra_w=()):
        kw = {}
        if op1 is not None:
            kw["op1"] = op1
        if accum is not None:
            kw["accum_out"] = accum
        em.op(eng, lambda e: e.tensor_scalar(out=out_ap, in0=in0, scalar1=s1, scalar2=s2, op0=op0, **kw),
              reads=reads, writes=[out_tl.b] + list(extra_w))

    def stt(out_tl, out_ap, in0, scalar, in1, op0, op1, reads):
        em.op("dve", lambda e: e.scalar_tensor_tensor(out=out_ap, in0=in0, scalar=scalar, in1=in1, op0=op0, op1=op1),
              reads=reads, writes=[out_tl.b])

    def copy(eng, out_tl, out_ap, in_ap, reads):
        if eng == "act":
            act(out_tl, out_ap, in_ap, AF.Copy, reads)
        else:
            em.op(eng, lambda e: e.tensor_copy(out=out_ap, in_=in_ap), reads=reads, writes=[out_tl.b])

    cast_rr = [0]

    def load_w(es_alloc, d, dst, kc, ncols, stg, kp=128):
        W = 704
        for k in range(kc):
            for c0 in range(0, ncols, W):
                cw = min(W, ncols - c0)
                s = stg[cast_rr[0] % len(stg)]
                eng = ("dve", "act")[cast_rr[0] % 2]
                cast_rr[0] += 1
                em.dma("sp", "ld_w", s.ap[0:kp, 0:cw], d.ap[k * kp:(k + 1) * kp, c0:c0 + cw],
                       reads=[d.bufs[0]], writes=[s.b], nslots=3)
                copy(eng, dst, dst.ap[0:kp, k, c0:c0 + cw], s.ap[0:kp, 0:cw], reads=[s.b])

    def ln_stats(xr, ps1, ps2, sq):
        for j in range(8):
            mm(ps1, ps1.ap[:, 0:T], ones_f.ap, xr[j].ap, [ones_f.b, xr[j].b], j == 0, j == 7)
        for j in range(8):
            s = sq[j % 2]
            act(s, s.ap, xr[j].ap, AF.Square, [xr[j].b])
            mm(ps2, ps2.ap[:, 0:T], ones_f.ap, s.ap, [ones_f.b, s.b], j == 0, j == 7)

    def ln_apply(xr, lnp, li, ps1, ps2, tmp, st):
        mean, msq, var, rstd, bt = st["mean"], st["msq"], st["var"], st["rstd"], st["bt"]
        act(mean, mean.ap, ps1.ap[:, 0:T], AF.Copy, [ps1.b], scale=1.0 / D)
        act(msq, msq.ap, mean.ap, AF.Square, [mean.b])
        stt(var, var.ap, ps2.ap[:, 0:T], 1.0 / D, msq.ap, ALU.mult, ALU.subtract, [ps2.b, msq.b])
        act(var, var.ap, var.ap, AF.Sqrt, [var.b, eps5.b], bias=eps5.ap[:, 0:1])
        em.op("dve", lambda e: e.reciprocal(out=rstd.ap, in_=var.ap), reads=[var.b], writes=[rstd.b])
        stt(bt, bt.ap, mean.ap, -1.0, rstd.ap, ALU.mult, ALU.mult, [mean.b, rstd.b])
        for j in range(8):
            t = tmp[j % 2]
            tt("dve", t, t.ap, xr[j].ap, rstd.ap, ALU.mult, [xr[j].b, rstd.b])
            tt("pool", t, t.ap, t.ap, bt.ap, ALU.add, [t.b, bt.b])
            act(xr[j], xr[j].ap, t.ap, AF.Identity, [t.b, lnp.b],
                scale=lnp.ap[:, li * 16 + j:li * 16 + j + 1], bias=lnp.ap[:, li * 16 + 8 + j:li * 16 + 8 + j + 1])

    def layer_norm(xr, lnp, li, ps1, ps2, sq, tmp, st):
        ln_stats(xr, ps1, ps2, sq)
        ln_apply(xr, lnp, li, ps1, ps2, tmp, st)

    def stage_ffn(hin_d, hout_d, w13_d, w2_d, lnp, li):
        with ExitStack() as es:
            def sb(name, shape, dt):
                return TL(es.enter_context(nc.sbuf_tensor(uq(name), list(shape), dt)).ap(), name)

            def ps(name):
                return TL(es.enter_context(nc.psum_tensor(uq(name), [128, 512], F32)).ap(), name)

            w13b = sb("w13b", [128, 8, 2 * DFF], BF16)
            w2b = sb("w2b", [128, 22, D], BF16)
            stg = [sb("stg0", [128, 704], F32), sb("stg1", [128, 704], F32), sb("stg2", [128, 704], F32)]
            load_w(es, w13_d, w13b, 8, 2 * DFF, stg)
            load_w(es, w2_d, w2b, 22, D, stg)
            X = [[sb("x%d_%d" % (b_, j), [128, T], F32) for j in range(8)] for b_ in range(2)]
            hb = sb("hb", [128, 8, T], BF16)
            G = sb("G", [128, 22, T], BF16)
            sg = [sb("sg0", [128, T], F32), sb("sg1", [128, T], F32)]
            sq = [sb("sq0", [128, T], F32), sb("sq1", [128, T], F32)]
            st = {n: sb("ln_" + n, [128, T], F32) for n in ("mean", "var", "bt")}
            st["msq"] = st["bt"]
            st["rstd"] = st["var"]
            tmpb = [sb("tmpb0", [128, T], F32), sb("tmpb1", [128, T], F32)]
            pg = [ps("pg0"), ps("pg1")]
            pu = [ps("pu0"), ps("pu1")]
            py = [ps("py0"), ps("py1")]
            ps1 = ps("ps1")
            ps2 = ps("ps2")

            def ph_a_hb(m):
                n0 = m * T
                em.dma("pool", "ld_hb", hb.ap, hin_d.ap[:, n0:n0 + T].rearrange("(k p) t -> p k t", p=128),
                       reads=[hin_d.bufs[m]], writes=[hb.b], nslots=2)

            def ph_a_x(m):
                n0 = m * T
                xb = X[m % 2]
                for j in range(8):
                    em.dma("sp", "ld_h", xb[j].ap, hin_d.ap[j * 128:(j + 1) * 128, n0:n0 + T],
                           reads=[hin_d.bufs[m]], writes=[xb[j].b], nslots=4)

            def ph_b(m):
                for c in range(22):
                    g_, u_ = pg[c % 2], pu[c % 2]
                    for k in range(8):
                        mm(g_, g_.ap[:, 0:T], w13b.ap[:, k, c * 128:(c + 1) * 128], hb.ap[:, k, :], [w13b.b, hb.b], k == 0, k == 7)
                    for k in range(8):
                        mm(u_, u_.ap[:, 0:T], w13b.ap[:, k, DFF + c * 128:DFF + (c + 1) * 128], hb.ap[:, k, :], [w13b.b, hb.b], k == 0, k == 7)
                    s_ = sg[c % 2]
                    act(s_, s_.ap, g_.ap[:, 0:T], AF.Silu, [g_.b])
                    tt("dve", G, G.ap[:, c, :], s_.ap, u_.ap[:, 0:T], ALU.mult, [s_.b, u_.b])

            def ph_c(m):
                xb = X[m % 2]
                for j in range(8):
                    y_ = py[j % 2]
                    for k in range(22):
                        mm(y_, y_.ap[:, 0:T], w2b.ap[:, k, j * 128:(j + 1) * 128], G.ap[:, k, :], [w2b.b, G.b], k == 0, k == 21)
                    act(xb[j], xb[j].ap, xb[j].ap, AF.Copy, [xb[j].b], scale=ALPHA)
                    stt(xb[j], xb[j].ap, y_.ap[:, 0:T], 0.5, xb[j].ap, ALU.mult, ALU.add, [y_.b, xb[j].b])

            def ph_d1(m):
                ln_stats(X[m % 2], ps1, ps2, sq)

            def ph_d2(m):
                n0 = m * T
                xb = X[m % 2]
                ln_apply(xb, lnp, li, ps1, ps2, tmpb, st)
                for j in range(8):
                    em.dma("sp", "st_h", hout_d.ap[j * 128:(j + 1) * 128, n0:n0 + T], xb[j].ap,
                           reads=[xb[j].b], writes=[hout_d.bufs[m]], nslots=4)

            ph_a_hb(0)
            ph_a_x(0)
            ph_b(0)
            for m in range(NTL):
                if m + 1 < NTL:
                    ph_a_hb(m + 1)
                ph_c(m)
                if m >= 1:
                    ph_d2(m - 1)
                if m + 1 < NTL:
                    ph_a_x(m + 1)
                    ph_b(m + 1)
                ph_d1(m)
            ph_d2(NTL - 1)
            em.barrier()

    def stage_proj(h1_d, wA_d, wuq_d, wuk_d, wuv_d, nrm, tabs, outs):
        c128_d, s128_d, c96_d, s96_d = tabs
        with ExitStack() as es:
            def sb(name, shape, dt):
                return TL(es.enter_context(nc.sbuf_tensor(uq(name), list(shape), dt)).ap(), name)

            def ps(name):
                return TL(es.enter_context(nc.psum_tensor(uq(name), [128, 512], F32)).ap(), name)

            wAb = sb("wAb", [128, 8, NA], BF16)
            wuqb = sb("wuqb", [128, 2, 1536], BF16)
            wukb = sb("wukb", [128, 1, 768], BF16)
            wuvb = sb("wuvb", [128, 1, 512], BF16)
            stg = [sb("stg0", [128, 704], F32), sb("stg1", [128, 704], F32), sb("stg2", [128, 704], F32)]
            load_w(es, wA_d, wAb, 8, NA, stg)
            load_w(es, wuq_d, wuqb, 2, 1536, stg)
            load_w(es, wuk_d, wukb, 1, 768, stg)
            load_w(es, wuv_d, wuvb, 1, 512, stg)
            hbs = [sb("hb0", [128, 8, T], BF16), sb("hb1", [128, 8, T], BF16)]
            cur = {}
            c128 = sb("c128", [128, T], F32)
            s128 = sb("s128", [128, T], F32)
            c96 = sb("c96", [96, T], F32)
            s96 = sb("s96", [96, T], F32)
            t1 = [sb("t1_0", [128, T], F32), sb("t1_1", [128, T], F32)]
            t2 = [sb("t2_0", [128, T], F32), sb("t2_1", [128, T], F32)]
            aq_st = sb("aq_st", [128, 4, T], BF16)
            iq_st = sb("iq_st", [128, 4, T], BF16)
            ak_st = sb("ak_st", [128, T], BF16)
            ik_st = sb("ik_st", [128, T], BF16)
            mq_st = sb("mq_st", [96, 8, T], BF16)
            mk_st = sb("mk_st", [96, 8, T], BF16)
            av_st = sb("av_st", [128, 3, 130], BF16)
            mv_st = sb("mv_st", [128, 3, 8, 65], BF16)
            iw_st = sb("iw_st", [128, 3, 8], F32)
            cq_sb = sb("cq_sb", [128, 2, T], F32)
            ckv_sb = sb("ckv_sb", [128, T], F32)
            sqs = [sb("sqs0", [128, T], F32), sb("sqs1", [128, T], F32)]
            rs = sb("rs", [128, T], F32)
            cqn = sb("cqn", [128, 2, T], BF16)
            ckvn = sb("ckvn", [128, T], BF16)
            bs = sb("bs", [96, T], F32)
            pA = [ps("pA0"), ps("pA1")]
            pB = [ps("pB0"), ps("pB1")]
            pC = [ps("pC0"), ps("pC1")]
            pS = ps("pS")
            em.op("pool", lambda e: e.memset(av_st.ap, 1.0), writes=[av_st.b])
            em.op("pool", lambda e: e.memset(mv_st.ap, 1.0), writes=[mv_st.b])
            rr = [0]

            def proj(out_ps, col0, M):
                hb = cur["hb"]
                for k in range(8):
                    mm(out_ps, out_ps.ap[0:M, 0:T], wAb.ap[:, k, col0:col0 + M], hb.ap[:, k, :], [wAb.b, hb.b], k == 0, k == 7)

            def roped(colA, colB, dst_tl, dst_ap):
                i = rr[0] % 2
                rr[0] += 1
                a_, b_ = pA[i], pB[i]
                proj(a_, colA, 128)
                proj(b_, colB, 128)
                tt("dve", t1[i], t1[i].ap, a_.ap[:, 0:T], c128.ap, ALU.mult, [a_.b, c128.b])
                tt("dve", t2[i], t2[i].ap, b_.ap[:, 0:T], s128.ap, ALU.mult, [b_.b, s128.b])
                tt("pool", dst_tl, dst_ap, t1[i].ap, t2[i].ap, ALU.add, [t1[i].b, t2[i].b])

            for m in range(NTL):
                n0 = m * T
                if m == 0:
                    em.dma("pool", "ld_hb", hbs[0].ap, h1_d.ap[:, 0:T].rearrange("(k p) t -> p k t", p=128),
                           reads=[h1_d.bufs[0]], writes=[hbs[0].b], nslots=2)
                if m + 1 < NTL:
                    em.dma("pool", "ld_hb", hbs[(m + 1) % 2].ap, h1_d.ap[:, n0 + T:n0 + 2 * T].rearrange("(k p) t -> p k t", p=128),
                           reads=[h1_d.bufs[m + 1]], writes=[hbs[(m + 1) % 2].b], nslots=2)
                hb = hbs[m % 2]
                cur["hb"] = hb
                for (tl_, d_) in ((c128, c128_d), (s128, s128_d), (c96, c96_d), (s96, s96_d)):
                    em.dma("sp", "ld_tab", tl_.ap, d_.ap[:, n0:n0 + T], reads=[d_.bufs[0]], writes=[tl_.b], nslots=4)
                for r in range(4):
                    roped(r * 128, 512 + r * 128, aq_st, aq_st.ap[:, r, :])
                roped(1024, 1152, ak_st, ak_st.ap)
                for r in range(4):
                    roped(1408 + r * 128, 1920 + r * 128, iq_st, iq_st.ap[:, r, :])
                roped(2432, 2560, ik_st, ik_st.ap)
                for c in range(3):
                    p_ = pC[c % 2]
                    for k in range(8):
                        mm(p_, p_.ap[:, 0:128], hb.ap[:, k, c * 128:(c + 1) * 128], wAb.ap[:, k, 1280:1408], [wAb.b, hb.b], k == 0, k == 7)
                    copy("act", av_st, av_st.ap[:, c, 0:64], p_.ap[:, 0:64], [p_.b])
                    copy("act", av_st, av_st.ap[:, c, 65:129], p_.ap[:, 64:128], [p_.b])
                    p2 = pC[(c + 1) % 2]
                    for k in range(8):
                        mm(p2, p2.ap[:, 0:8], hb.ap[:, k, c * 128:(c + 1) * 128], wAb.ap[:, k, 2688:2696], [wAb.b, hb.b], k == 0, k == 7)
                    act(iw_st, iw_st.ap[:, c, :], p2.ap[:, 0:8], AF.Copy, [p2.b], scale=IDX_SCALE)
                for c in range(2):
                    p_ = pC[c % 2]
                    proj(p_, 2696 + c * 128, 128)
                    copy("act", cq_sb, cq_sb.ap[:, c, :], p_.ap[:, 0:T], [p_.b])
                    act(sqs[c], sqs[c].ap, p_.ap[:, 0:T], AF.Square, [p_.b])
                for c in range(2):
                    mm(pS, pS.ap[:, 0:T], ones_f.ap, sqs[c].ap, [ones_f.b, sqs[c].b], c == 0, c == 1)
                act(rs, rs.ap, pS.ap[:, 0:T], AF.Sqrt, [pS.b, eps6.b], scale=1.0 / 256, bias=eps6.ap[:, 0:1])
                em.op("dve", lambda e: e.reciprocal(out=rs.ap, in_=rs.ap), reads=[rs.b], writes=[rs.b])
                for c in range(2):
                    stt(cqn, cqn.ap[:, c, :], cq_sb.ap[:, c, :], nrm.ap[:, c:c + 1], rs.ap, ALU.mult, ALU.mult, [cq_sb.b, nrm.b, rs.b])
                p_ = pC[0]
                proj(p_, 2952, 128)
                copy("act", ckv_sb, ckv_sb.ap, p_.ap[:, 0:T], [p_.b])
                act(sqs[0], sqs[0].ap, p_.ap[:, 0:T], AF.Square, [p_.b])
                mm(pS, pS.ap[:, 0:T], ones_f.ap, sqs[0].ap, [ones_f.b, sqs[0].b], True, True)
                act(rs, rs.ap, pS.ap[:, 0:T], AF.Sqrt, [pS.b, eps6.b], scale=1.0 / 128, bias=eps6.ap[:, 0:1])
                em.op("dve", lambda e: e.reciprocal(out=rs.ap, in_=rs.ap), reads=[rs.b], writes=[rs.b])
                stt(ckvn, ckvn.ap, ckv_sb.ap, nrm.ap[:, 2:3], rs.ap, ALU.mult, ALU.mult, [ckv_sb.b, nrm.b, rs.b])
                for h in range(8):
                    i = rr[0] % 2
                    rr[0] += 1
                    a_, b_ = pA[i], pB[i]
                    for k in range(2):
                        mm(a_, a_.ap[0:96, 0:T], wuqb.ap[:, k, 96 * h:96 * h + 96], cqn.ap[:, k, :], [wuqb.b, cqn.b], k == 0, k == 1)
                    for k in range(2):
                        mm(b_, b_.ap[0:96, 0:T], wuqb.ap[:, k, 768 + 96 * h:768 + 96 * h + 96], cqn.ap[:, k, :], [wuqb.b, cqn.b], k == 0, k == 1)
                    tt("dve", t1[i], t1[i].ap[0:96, :], a_.ap[0:96, 0:T], c96.ap, ALU.mult, [a_.b, c96.b])
                    tt("dve", t2[i], t2[i].ap[0:96, :], b_.ap[0:96, 0:T], s96.ap, ALU.mult, [b_.b, s96.b])
                    tt("pool", mq_st, mq_st.ap[:, h, :], t1[i].ap[0:96, :], t2[i].ap[0:96, :], ALU.add, [t1[i].b, t2[i].b])
                p_ = pC[1]
                proj(p_, 3176, 96)
                tt("dve", bs, bs.ap, p_.ap[0:96, 0:T], s96.ap, ALU.mult, [p_.b, s96.b])
                for h in range(8):
                    i = rr[0] % 2
                    rr[0] += 1
                    a_ = pA[i]
                    for k in range(8):
                        mm(a_, a_.ap[0:96, 0:T], wAb.ap[:, k, 3080:3176], hb.ap[:, k, :], [wAb.b, hb.b], k == 0, False)
                    mm(a_, a_.ap[0:96, 0:T], wukb.ap[:, 0, 96 * h:96 * h + 96], ckvn.ap, [wukb.b, ckvn.b], False, True)
                    tt("dve", t1[i], t1[i].ap[0:96, :], a_.ap[0:96, 0:T], c96.ap, ALU.mult, [a_.b, c96.b])
                    tt("pool", mk_st, mk_st.ap[:, h, :], t1[i].ap[0:96, :], bs.ap, ALU.add, [t1[i].b, bs.b])
                for c in range(3):
                    p_ = pB[c % 2]
                    mm(p_, p_.ap[:, 0:512], ckvn.ap[:, c * 128:(c + 1) * 128], wuvb.ap[:, 0, :], [wuvb.b, ckvn.b], True, True)
                    copy("act", mv_st, mv_st.ap[:, c, :, 0:64], p_.ap[:, 0:512].rearrange("p (h d) -> p h d", d=64), [p_.b])
                o = outs
                em.dma("pool", "st_q", o["aq"].ap[:, :, n0:n0 + T].rearrange("n p t -> p n t"), aq_st.ap, reads=[aq_st.b], writes=[o["aq"].bufs[m]], nslots=4)
                em.dma("pool", "st_q", o["iq"].ap[:, :, n0:n0 + T].rearrange("n p t -> p n t"), iq_st.ap, reads=[iq_st.b], writes=[o["iq"].bufs[m]], nslots=4)
                em.dma("pool", "st_q", o["mq"].ap[:, :, n0:n0 + T].rearrange("n p t -> p n t"), mq_st.ap, reads=[mq_st.b], writes=[o["mq"].bufs[m]], nslots=4)
                for h in range(8):
                    em.dma("pool", "st_q", o["mk"][h].ap[:, n0:n0 + T], mk_st.ap[:, h, :], reads=[mk_st.b], writes=[o["mk"][h].bufs[m]], nslots=4)
                em.dma("pool", "st_k", o["ak"].ap[:, n0:n0 + T], ak_st.ap, reads=[ak_st.b], writes=[o["ak"].bufs[m]], nslots=4)
                em.dma("pool", "st_k", o["ik"].ap[:, n0:n0 + T], ik_st.ap, reads=[ik_st.b], writes=[o["ik"].bufs[m]], nslots=4)
                em.dma("pool", "st_k", o["av"].ap[n0:n0 + T, :].rearrange("(c p) f -> p c f", p=128), av_st.ap, reads=[av_st.b], writes=[o["av"].bufs[m]], nslots=4)
                for (v_, h0_, h1_) in o["mvc"]:
                    em.dma("pool", "st_k", v_.ap[n0:n0 + T, :].rearrange("(c p) (h d) -> p c h d", p=128, d=65),
                           mv_st.ap[:, :, h0_:h1_, :], reads=[mv_st.b], writes=[v_.bufs[m]], nslots=4)
                em.dma("pool", "st_k", o["iw"].ap[:, 3 * m:3 * m + 3, :], iw_st.ap, reads=[iw_st.b], writes=[o["iw"].bufs[m]], nslots=4)
            em.barrier()

    def finalize_heads(accs, o_st, hlist, accsb, rrow, bc_ps, bc_sb, neg1):
        for r, h in enumerate(hlist):
            a_ = accs[r]
            sb_ = accsb[r % len(accsb)]
            copy("act", sb_, sb_.ap[0:65, :], a_.ap[0:65, 0:T], [a_.b])
            act(rrow, rrow.ap[64:65, :], sb_.ap[64:65, :], AF.Ln, [sb_.b])
            act(rrow, rrow.ap[64:65, :], rrow.ap[64:65, :], AF.Exp, [rrow.b], scale=-1.0)
            mm(bc_ps, bc_ps.ap[0:64, 0:T], ones_f.ap[64:65, 0:64], rrow.ap[64:65, :], [ones_f.b, rrow.b], True, True)
            copy("act", bc_sb, bc_sb.ap, bc_ps.ap[0:64, 0:T], [bc_ps.b])
            tt("pool", o_st, o_st.ap[:, h, :], sb_.ap[0:64, :], bc_sb.ap, ALU.mult, [sb_.b, bc_sb.b])

    def stage_dsa(ins, oa_d, idxm_d, iw_d):
        with ExitStack() as es:
            def sb(name, shape, dt):
                return TL(es.enter_context(nc.sbuf_tensor(uq(name), list(shape), dt)).ap(), name)

            def ps(name):
                return TL(es.enter_context(nc.psum_tensor(uq(name), [128, 512], F32)).ap(), name)

            idxm = sb("idxm_sb", [128, 3, 2, 3, 128], F32)
            em.dma("sp", "ld_small", idxm.ap, idxm_d.ap, reads=[idxm_d.bufs[0]], writes=[idxm.b])
            ik_sb = sb("ik_sb", [128, 2, NT], BF16)
            ak_sb = sb("ak_sb", [128, 2, NT], BF16)
            av_sb = sb("av_sb", [128, 2, NBK, 130], BF16)
            iw_sb = sb("iw_sb", [128, NBK, 8], F32)
            aw_sb = sb("aw_sb", [128, NBK, 8], F32)
            sg_sb = sb("sg_sb", [128, NBK, 8], F32)
            for j in range(2):
                em.dma("sp", "ld_k", ik_sb.ap[:, j, :], ins["ik"].ap[j], reads=ins["ik"].bufs, writes=[ik_sb.b], nslots=4)
                em.dma("sp", "ld_k", ak_sb.ap[:, j, :], ins["ak"].ap[j], reads=ins["ak"].bufs, writes=[ak_sb.b], nslots=4)
                for i0 in range(0, NBK, 11):
                    em.dma("sp", "ld_k", av_sb.ap[:, j, i0:i0 + 11, :],
                           ins["av"].ap[j, i0 * 128:(i0 + 11) * 128, :].rearrange("(i p) c -> p i c", p=128),
                           reads=ins["av"].bufs, writes=[av_sb.b], nslots=4)
            em.dma("sp", "ld_k", iw_sb.ap, iw_d.ap, reads=iw_d.bufs, writes=[iw_sb.b], nslots=4)
            act(aw_sb, aw_sb.ap, iw_sb.ap, AF.Abs, [iw_sb.b])
            act(sg_sb, sg_sb.ap, iw_sb.ap, AF.Sign, [iw_sb.b])
            scs = [sb("sc0", [128, 2 * NT], F32), sb("sc1", [128, 2 * NT], F32)]
            junk = sb("junk", [128, 2 * NT], U8)
            maskT = sb("maskT", [128, NBLK, T], FP8)
            iqz = [sb("iqz0", [128, 4, T], BF16), sb("iqz1", [128, 4, T], BF16)]
            aqz = [sb("aqz0", [128, 4, T], BF16), sb("aqz1", [128, 4, T], BF16)]
            for z_ in (iqz, aqz):
                em.op("dve", lambda e: e.memset(z_[0].ap[64:128, :, :], 0.0), writes=[z_[0].b])
                em.op("dve", lambda e: e.memset(z_[1].ap[0:64, :, :], 0.0), writes=[z_[1].b])
            dg = [sb("dg0", [128, 8, 128], BF16), sb("dg1", [128, 8, 128], BF16)]
            rl = [sb("rl%d" % i, [128, 512], BF16) for i in range(4)]
            mch = [sb("mch0", [128, 1024], BF16), sb("mch1", [128, 1024], BF16)]
            pt = [sb("pt%d" % i, [128, T], BF16) for i in range(4)]
            accsb = [sb("accsb0", [65, T], F32), sb("accsb1", [65, T], F32)]
            neg1 = sb("neg1", [65, T], F32)
            em.op("pool", lambda e: e.memset(neg1.ap, -1.0), writes=[neg1.b])
            thrs = [sb("thr0", [128, 1], F32), sb("thr1", [128, 1], F32)]
            cnts = [sb("cnt0", [128, 1], F32), sb("cnt1", [128, 1], F32)]
            dds = [sb("dd0", [128, 1], F32), sb("dd1", [128, 1], F32)]
            rrow = sb("rrow", [65, T], F32)
            bc_sb = sb("bc_sb", [64, T], F32)
            o_st = sb("o_st", [64, 8, T], BF16)
            psx = [ps("psx0"), ps("psx1")]
            accs = [ps("acc%d" % r) for r in range(4)]
            bc_ps = ps("bc_ps")
            ptr = TL(es.enter_context(nc.psum_tensor(uq("ptr"), [128, 1024], BF16)).ap(), "ptr")
            xi = [0]
            ci = [0]
            mi = [0]
            LA = 2

            def idx_block(m, c):
                nI = 3 * m + 3
                ncol = 2 * nI * 128
                ib = 3 * m + c
                sc = scs[ib % 2]
                scv = sc.ap[:, 0:ncol].rearrange("p (j i t) -> p j i t", j=2, t=128)
                dg_ = dg[ib % 2]
                for h in range(8):
                    act(dg_, dg_.ap[:, h, :], ident_b.ap, AF.Identity, [ident_b.b, sg_sb.b], scale=sg_sb.ap[:, ib, h:h + 1])
                steps = []
                for j in range(2):
                    for i0 in range(0, nI, 4):
                        for h in range(8):
                            steps.append((j, i0, h))
                held = {}
                for si in range(len(steps) + LA):
                    if si < len(steps):
                        j, i0, h = steps[si]
                        cw = min(4, nI - i0) * 128
                        r, hf = h // 2, h % 2
                        p_ = psx[xi[0] % 2]
                        r_ = rl[xi[0] % 4]
                        xi[0] += 1
                        mm(p_, p_.ap[:, 0:cw], iqz[hf].ap[:, r, c * 128:(c + 1) * 128],
                           ik_sb.ap[:, j, i0 * 128:i0 * 128 + cw], [iqz[hf].b, ik_sb.b], True, True)
                        act(r_, r_.ap[:, 0:cw], p_.ap[:, 0:cw], AF.Relu, [p_.b, aw_sb.b], scale=aw_sb.ap[:, ib, h:h + 1])
                        held[si] = r_
                    if si >= LA:
                        j, i0, h = steps[si - LA]
                        cw = min(4, nI - i0) * 128
                        col0 = (j * nI + i0) * 128
                        r_ = held.pop(si - LA)
                        if h == 0:
                            ci[0] += 1
                        a_ = accs[2 + ci[0] % 2]
                        mm(a_, a_.ap[:, 0:cw], dg_.ap[:, h, :], r_.ap[:, 0:cw], [dg_.b, r_.b], h == 0, h == 7)
                        if h == 7:
                            copy("act", sc, sc.ap[:, col0:col0 + cw], a_.ap[:, 0:cw], [a_.b])

            def bisect_block(m, c):
                nI = 3 * m + 3
                ncol = 2 * nI * 128
                ib = 3 * m + c
                sc, thr, cnt, dd = scs[ib % 2], thrs[ib % 2], cnts[ib % 2], dds[ib % 2]
                scv = sc.ap[:, 0:ncol].rearrange("p (j i t) -> p j i t", j=2, t=128)
                tt("dve", sc, scv[:, :, nI - 3:nI, :], scv[:, :, nI - 3:nI, :], idxm.ap[:, c, :, :, :], ALU.add, [sc.b, idxm.b])
                em.op("dve", lambda e: e.memset(thr.ap, 3.0e-5), writes=[thr.b])
                for it in range(1, BIS_IT + 1):
                    step = BIS_R / (2 ** it)
                    ts("dve", junk, junk.ap[:, 0:ncol], sc.ap[:, 0:ncol], thr.ap[:, 0:1], ALU.is_ge, [sc.b, thr.b],
                       op1=ALU.add, accum=cnt.ap[:, 0:1], extra_w=[cnt.b])
                    ts("dve", dd, dd.ap, cnt.ap, 255.5, ALU.is_ge, [cnt.b], s2=2.0 * step, op1=ALU.mult)
                    dec = -step if it < BIS_IT else -2.0 * step
                    stt(thr, thr.ap, thr.ap, dec, dd.ap, ALU.add, ALU.add, [thr.b, dd.b])

            def mask_block(m, c):
                nI = 3 * m + 3
                ib = 3 * m + c
                sc, thr = scs[ib % 2], thrs[ib % 2]
                nkb = 2 * nI
                for k0 in range(0, nkb, 8):
                    nk = min(8, nkb - k0)
                    mc = mch[mi[0] % 2]
                    mi[0] += 1
                    ts("dve", mc, mc.ap[:, 0:nk * 128], sc.ap[:, k0 * 128:(k0 + nk) * 128], thr.ap[:, 0:1], ALU.is_ge, [sc.b, thr.b])
                    for q in range(nk):
                        em.op("pe", lambda e: e.transpose(out=ptr.ap[:, q * 128:(q + 1) * 128], in_=mc.ap[:, q * 128:(q + 1) * 128], identity=ident_b.ap),
                              reads=[mc.b, ident_b.b], writes=[ptr.b])
                    em.op("act", lambda e: e.activation(out=maskT.ap[:, k0:k0 + nk, c * 128:(c + 1) * 128],
                                                        in_=ptr.ap[:, 0:nk * 128].rearrange("p (k t) -> p k t", t=128),
                                                        func=AF.Identity, scale=240.0, bias=m240.ap[:, 0:1], saturate=False),
                          reads=[ptr.b, m240.b], writes=[maskT.b])

            def load_iq(m):
                n0 = m * T
                for hf_ in range(2):
                    pr_ = slice(64 * hf_, 64 * hf_ + 64)
                    em.dma("sp", "ld_q", iqz[hf_].ap[pr_, :, :], ins["iq"].ap[:, pr_, n0:n0 + T].rearrange("n p t -> p n t"),
                           reads=[ins["iq"].bufs[m]], writes=[iqz[hf_].b], nslots=4)

            def load_aq(m):
                n0 = m * T
                for hf_ in range(2):
                    pr_ = slice(64 * hf_, 64 * hf_ + 64)
                    em.dma("sp", "ld_q", aqz[hf_].ap[pr_, :, :], ins["aq"].ap[:, pr_, n0:n0 + T].rearrange("n p t -> p n t"),
                           reads=[ins["aq"].bufs[m]], writes=[aqz[hf_].b], nslots=4)

            def attention(m):
                n0 = m * T
                nI = 3 * m + 3
                nkb = 2 * nI
                for g in range(2):
                    for half in range(2):
                        rs_ = [2 * half, 2 * half + 1]
                        steps = [(kb, r) for kb in range(nkb) for r in rs_]
                        held = {}
                        for si in range(len(steps) + LA):
                            if si < len(steps):
                                kb, r = steps[si]
                                j, i = kb // nI, kb % nI
                                p_ = psx[xi[0] % 2]
                                e_ = pt[xi[0] % 4]
                                xi[0] += 1
                                mm(p_, p_.ap[:, 0:T], ak_sb.ap[:, j, i * 128:(i + 1) * 128], aqz[g].ap[:, r, :], [ak_sb.b, aqz[g].b], True, False)
                                mm(p_, p_.ap[:, 0:T], ident_b.ap, maskT.ap[:, kb, :], [ident_b.b, maskT.b], False, True)
                                act(e_, e_.ap, p_.ap[:, 0:T], AF.Exp, [p_.b], scale=0.125)
                                held[si] = e_
                            if si >= LA:
                                kb, r = steps[si - LA]
                                j, i = kb // nI, kb % nI
                                m_ = held.pop(si - LA)
                                a_ = accs[r % 2]
                                mm(a_, a_.ap[0:65, 0:T], av_sb.ap[:, j, i, 65 * g:65 * g + 65], m_.ap, [av_sb.b, m_.b], kb == 0, kb == nkb - 1)
                        finalize_heads(accs[0:2], o_st, [4 * g + r for r in rs_], accsb, rrow, bc_ps, bc_sb, neg1)
                em.dma("pool", "st_o", oa_d.ap[:, :, n0:n0 + T].rearrange("n p t -> p n t"), o_st.ap, reads=[o_st.b], writes=[oa_d.bufs[m]])

            load_iq(0)
            idx_block(0, 0)
            bisect_block(0, 0)
            idx_block(0, 1)
            mask_block(0, 0)
            bisect_block(0, 1)
            idx_block(0, 2)
            mask_block(0, 1)
            bisect_block(0, 2)
            if NTL > 1:
                load_iq(1)
                idx_block(1, 0)
            mask_block(0, 2)
            load_aq(0)
            for m in range(NTL):
                if m + 1 < NTL:
                    bisect_block(m + 1, 0)
                    idx_block(m + 1, 1)
                    bisect_block(m + 1, 1)
                attention(m)
                if m + 1 < NTL:
                    load_aq(m + 1)
                    mask_block(m + 1, 0)
                    idx_block(m + 1, 2)
                    mask_block(m + 1, 1)
                    bisect_block(m + 1, 2)
                    if m + 2 < NTL:
                        load_iq(m + 2)
                        idx_block(m + 2, 0)
                    mask_block(m + 1, 2)
            em.barrier()

    def stage_mla(ins, ob_d, cmT_d):
        with ExitStack() as es:
            def sb(name, shape, dt):
                return TL(es.enter_context(nc.sbuf_tensor(uq(name), list(shape), dt)).ap(), name)

            def ps(name):
                return TL(es.enter_context(nc.psum_tensor(uq(name), [128, 512], F32)).ap(), name)

            cmT = sb("cmT_sb", [128, 6, T], BF16)
            em.dma("sp", "ld_small", cmT.ap, cmT_d.ap, reads=[cmT_d.bufs[0]], writes=[cmT.b])
            mk_sb = sb("mk_sb", [96, 2, 4, NT], BF16)
            mv_sb = sb("mv_sb", [128, 2, NBK, 260], BF16)
            mq_sb = sb("mq_sb", [96, 4, T], BF16)
            pt = [sb("pt%d" % i, [128, T], BF16) for i in range(4)]
            pm = [sb("pm%d" % i, [128, T], BF16) for i in range(4)]
            accsb = [sb("accsb0", [65, T], F32), sb("accsb1", [65, T], F32)]
            neg1 = sb("neg1", [65, T], F32)
            em.op("pool", lambda e: e.memset(neg1.ap, -1.0), writes=[neg1.b])
            rrow = sb("rrow", [65, T], F32)
            bc_sb = sb("bc_sb", [64, T], F32)
            o_st = sb("o_st", [64, 8, T], BF16)
            psx = [ps("psx0"), ps("psx1"), ps("psx2")]
            accs = [ps("acc%d" % r) for r in range(4)]
            bc_ps = ps("bc_ps")
            xi = [0]
            LA = 2
            for hh in range(2):
                for j in range(2):
                    for r in range(4):
                        mkv = ins["mk"][4 * hh + r]
                        em.dma("sp", "ld_k", mk_sb.ap[:, j, r, :], mkv.ap[j], reads=mkv.bufs, writes=[mk_sb.b], nslots=4)
                        mvv = ins["mvh"][4 * hh + r]
                        for i0 in range(0, NBK, 11):
                            em.dma("sp", "ld_k", mv_sb.ap[:, j, i0:i0 + 11, 65 * r:65 * r + 65],
                                   mvv.ap[j, i0 * 128:(i0 + 11) * 128, :].rearrange("(i p) c -> p i c", p=128),
                                   reads=mvv.bufs, writes=[mv_sb.b], nslots=4)
                for m in range(NTL):
                    n0 = m * T
                    nI = 3 * m + 3
                    nkb = 2 * nI
                    em.dma("sp", "ld_q", mq_sb.ap, ins["mq"].ap[4 * hh:4 * hh + 4, :, n0:n0 + T].rearrange("n p t -> p n t"),
                           reads=[ins["mq"].bufs[m]], writes=[mq_sb.b], nslots=4)
                    steps = [(kb, r) for kb in range(nkb) for r in range(4)]
                    held = {}
                    for si in range(len(steps) + LA):
                        if si < len(steps):
                            kb, r = steps[si]
                            j, i = kb // nI, kb % nI
                            zone = i >= nI - 3
                            p_ = psx[xi[0] % 3]
                            e_ = pt[xi[0] % 4]
                            m_ = pm[xi[0] % 4]
                            xi[0] += 1
                            mm(p_, p_.ap[:, 0:T], mk_sb.ap[:, j, r, i * 128:(i + 1) * 128], mq_sb.ap[:, r, :], [mk_sb.b, mq_sb.b], True, True)
                            act(e_, e_.ap, p_.ap[:, 0:T], AF.Exp, [p_.b], scale=float(96 ** -0.5))
                            if zone:
                                z = j * 3 + (i - (nI - 3))
                                tt("pool" if (xi[0] % 2) else "dve", m_, m_.ap, e_.ap, cmT.ap[:, z, :], ALU.mult, [e_.b, cmT.b])
                                held[si] = m_
                            else:
                                held[si] = e_
                        if si >= LA:
                            kb, r = steps[si - LA]
                            j, i = kb // nI, kb % nI
                            src = held.pop(si - LA)
                            mm(accs[r], accs[r].ap[0:65, 0:T], mv_sb.ap[:, j, i, 65 * r:65 * r + 65], src.ap, [mv_sb.b, src.b], kb == 0, kb == nkb - 1)
                    finalize_heads(accs, o_st, [r for r in range(4)], accsb, rrow, bc_ps, bc_sb, neg1)
                    em.dma("pool", "st_o", ob_d.ap[4 * hh:4 * hh + 4, :, n0:n0 + T].rearrange("n p t -> p n t"), o_st.ap[:, 0:4, :],
                           reads=[o_st.b], writes=[ob_d.bufs[m]])
                em.barrier()

    def stage_merge(h1_d, oa_d, ob_d, wg_d, wba_d, wbb_d, wout_d, lnp, li, hout_d):
        with ExitStack() as es:
            def sb(name, shape, dt):
                return TL(es.enter_context(nc.sbuf_tensor(uq(name), list(shape), dt)).ap(), name)

            def ps(name):
                return TL(es.enter_context(nc.psum_tensor(uq(name), [128, 512], F32)).ap(), name)

            wgb = sb("wgb", [128, 8, 2048], BF16)
            wbab = sb("wbab", [128, 4, D], BF16)
            wbbb = sb("wbbb", [128, 4, D], BF16)
            woutb = sb("woutb", [128, 8, D], BF16)
            stg = [sb("stg0", [128, 704], F32), sb("stg1", [128, 704], F32), sb("stg2", [128, 704], F32)]
            load_w(es, wg_d, wgb, 8, 2048, stg)
            load_w(es, wba_d, wbab, 4, D, stg)
            load_w(es, wbb_d, wbbb, 4, D, stg)
            load_w(es, wout_d, woutb, 8, D, stg)
            hin = sb("hin", [128, 8, T], F32)
            hb = sb("hb", [128, 8, T], BF16)
            oa = sb("oa", [128, 4, T], BF16)
            ob = sb("ob", [128, 4, T], BF16)
            mixed = sb("mixed", [128, 8, T], BF16)
            xr = [sb("xr%d" % j, [128, T], F32) for j in range(8)]
            sa = [sb("sa0", [128, T], F32), sb("sa1", [128, T], F32)]
            sbb = [sb("sbb0", [128, T], F32), sb("sbb1", [128, T], F32)]
            sq = [sb("sq0", [128, T], F32), sb("sq1", [128, T], F32)]
            tmp = [sb("tmp0", [128, T], F32), sb("tmp1", [128, T], F32)]
            st = {n: sb("ln_" + n, [128, T], F32) for n in ("mean", "msq", "var", "rstd", "bt")}
            pga, pgb, pba, pbb = ps("pga"), ps("pgb"), ps("pba"), ps("pbb")
            py = [ps("py0"), ps("py1")]
            ps1 = ps("ps1")
            ps2 = ps("ps2")
            for m in range(NTL):
                n0 = m * T
                em.dma("sp", "ld_h", hin.ap, h1_d.ap[:, n0:n0 + T].rearrange("(k p) t -> p k t", p=128), reads=[h1_d.bufs[m]], writes=[hin.b])
                for two in range(2):
                    pr_ = slice(64 * two, 64 * two + 64)
                    em.dma("sp", "ld_o", oa.ap[pr_, :, :], oa_d.ap[:, :, n0:n0 + T].rearrange("(q w) p t -> w p q t", w=2)[two],
                           reads=[oa_d.bufs[m]], writes=[oa.b], nslots=4)
                    em.dma("sp", "ld_o", ob.ap[pr_, :, :], ob_d.ap[:, :, n0:n0 + T].rearrange("(q w) p t -> w p q t", w=2)[two],
                           reads=[ob_d.bufs[m]], writes=[ob.b], nslots=4)
                copy("dve", hb, hb.ap[:, 0:4, :], hin.ap[:, 0:4, :], [hin.b])
                copy("pool", hb, hb.ap[:, 4:8, :], hin.ap[:, 4:8, :], [hin.b])
                for j in range(8):
                    cs = slice(j * 128, (j + 1) * 128)
                    for k in range(8):
                        mm(pga, pga.ap[:, 0:T], wgb.ap[:, k, cs], hb.ap[:, k, :], [wgb.b, hb.b], k == 0, k == 7)
                    for k in range(8):
                        mm(pgb, pgb.ap[:, 0:T], wgb.ap[:, k, 1024 + j * 128:1024 + (j + 1) * 128], hb.ap[:, k, :], [wgb.b, hb.b], k == 0, k == 7)
                    for k in range(4):
                        mm(pba, pba.ap[:, 0:T], wbab.ap[:, k, cs], oa.ap[:, k, :], [wbab.b, oa.b], k == 0, k == 3)
                    for k in range(4):
                        mm(pbb, pbb.ap[:, 0:T], wbbb.ap[:, k, cs], ob.ap[:, k, :], [wbbb.b, ob.b], k == 0, k == 3)
                    s1_, s2_ = sa[j % 2], sbb[j % 2]
                    act(s1_, s1_.ap, pga.ap[:, 0:T], AF.Sigmoid, [pga.b])
                    act(s2_, s2_.ap, pgb.ap[:, 0:T], AF.Sigmoid, [pgb.b])
                    tt("dve", s1_, s1_.ap, s1_.ap, pba.ap[:, 0:T], ALU.mult, [s1_.b, pba.b])
                    tt("dve", s2_, s2_.ap, s2_.ap, pbb.ap[:, 0:T], ALU.mult, [s2_.b, pbb.b])
                    tt("pool", mixed, mixed.ap[:, j, :], s1_.ap, s2_.ap, ALU.add, [s1_.b, s2_.b])
                for j in range(8):
                    y_ = py[j % 2]
                    for k in range(8):
                        mm(y_, y_.ap[:, 0:T], woutb.ap[:, k, j * 128:(j + 1) * 128], mixed.ap[:, k, :], [woutb.b, mixed.b], k == 0, k == 7)
                    act(xr[j], xr[j].ap, hin.ap[:, j, :], AF.Copy, [hin.b], scale=ALPHA)
                    stt(xr[j], xr[j].ap, y_.ap[:, 0:T], 1.0, xr[j].ap, ALU.mult, ALU.add, [y_.b, xr[j].b])
                layer_norm(xr, lnp, li, ps1, ps2, sq, tmp, st)
                for j in range(8):
                    em.dma("pool", "st_h", hout_d.ap[j * 128:(j + 1) * 128, n0:n0 + T], xr[j].ap,
                           reads=[xr[j].b], writes=[hout_d.bufs[m]], nslots=4)
            em.barrier()

    class V:
        def __init__(self, ap, bufs):
            self.ap = ap
            self.bufs = bufs

    def internal(name, shape, dt, n=NTL):
        return dram(name, shape, dt, "Internal", n)

    FMR = [224, 224, 192, 192, 192]
    TMC = [195, 195, 195, 65]
    MKLOC = [(0, 128), (1, 128), (2, 0), (2, 96), (3, 0), (3, 96), (4, 0), (4, 96)]
    MVLOC = [(0, 130), (1, 0), (1, 65), (1, 130), (2, 0), (2, 65), (2, 130), (3, 0)]

    def exchange(L, fm, tm):
        fa, ta = [None] * len(fm), [None] * len(tm)

        def xf(i):
            A = internal("fmA%d_%d" % (L, i), [2 * FMR[i], NT], BF16, 1)
            em.coll(fm[i].ap, A.ap, reads=fm[i].bufs, writes=A.bufs)
            fa[i] = V(A.ap.rearrange("(j r) t -> j r t", j=2), A.bufs)

        def xt(i):
            A = internal("tmA%d_%d" % (L, i), [2 * NT, TMC[i]], BF16, 1)
            em.coll(tm[i].ap, A.ap, reads=tm[i].bufs, writes=A.bufs)
            ta[i] = V(A.ap.rearrange("(j r) c -> j r c", j=2), A.bufs)

        xf(0)
        xf(1)
        xt(0)
        for i in range(2, len(fm)):
            xf(i)
        for i in range(1, len(tm)):
            xt(i)
        return {"ak": V(fa[0].ap[:, 0:128, :], fa[0].bufs), "ik": V(fa[1].ap[:, 0:128, :], fa[1].bufs),
                "mk": [V(fa[c].ap[:, r:r + 96, :], fa[c].bufs) for (c, r) in MKLOC],
                "av": V(ta[0].ap[:, :, 0:130], ta[0].bufs),
                "mvh": [V(ta[c].ap[:, :, k:k + 65], ta[c].bufs) for (c, k) in MVLOC]}

    tabs = (din("c128", [128, NT], F32, 1), din("s128", [128, NT], F32, 1),
            din("c96", [96, NT], F32, 1), din("s96", [96, NT], F32, 1))
    idxm_d = din("idxm", [128, 3, 2, 3, 128], F32, 1)
    cmT_d = din("cmT", [128, 6, T], BF16, 1)
    hin = din("xT", [D, NT], F32)
    for L in range(DEPTH):
        lnp = small_in("lnp_%d" % L, [128, 48])
        nrm = small_in("nrm_%d" % L, [128, 3])
        w = dict(w13a=din("w13a_%d" % L, [D, 2 * DFF], F32, 1), w2a=din("w2a_%d" % L, [DFF, D], F32, 1),
                 wA=din("wA_%d" % L, [D, NA], F32, 1), wuq=din("wuq_%d" % L, [256, 1536], F32, 1),
                 wuk=din("wuk_%d" % L, [128, 768], F32, 1), wuv=din("wuv_%d" % L, [128, 512], F32, 1),
                 wg=din("wg_%d" % L, [D, 2048], F32, 1), wba=din("wba_%d" % L, [512, D], F32, 1),
                 wbb=din("wbb_%d" % L, [512, D], F32, 1), wout=din("wout_%d" % L, [D, D], F32, 1),
                 w13b=din("w13b_%d" % L, [D, 2 * DFF], F32, 1), w2b=din("w2b_%d" % L, [DFF, D], F32, 1))
        h1 = internal("h1_%d" % L, [D, NT], F32)
        stage_ffn(hin, h1, w["w13a"], w["w2a"], lnp, 0)
        fm = [internal("fm%d_%d" % (L, i), [r_, NT], BF16) for i, r_ in enumerate(FMR)]
        tm = [internal("tm%d_%d" % (L, i), [NT, c_], BF16) for i, c_ in enumerate(TMC)]
        q = {"aq": internal("aq%d" % L, [4, 128, NT], BF16), "iq": internal("iq%d" % L, [4, 128, NT], BF16),
             "mq": internal("mq%d" % L, [8, 96, NT], BF16)}
        iw = internal("iw%d" % L, [128, NBK, 8], F32)
        outs = dict(q)
        outs.update({"ak": V(fm[0].ap[0:128, :], fm[0].bufs), "ik": V(fm[1].ap[0:128, :], fm[1].bufs),
                     "mk": [V(fm[c].ap[r:r + 96, :], fm[c].bufs) for (c, r) in MKLOC],
                     "av": V(tm[0].ap[:, 0:130], tm[0].bufs),
                     "mvc": [(V(tm[0].ap[:, 130:195], tm[0].bufs), 0, 1), (V(tm[1].ap, tm[1].bufs), 1, 4),
                             (V(tm[2].ap, tm[2].bufs), 4, 7), (V(tm[3].ap, tm[3].bufs), 7, 8)],
                     "iw": iw})
        stage_proj(h1, w["wA"], w["wuq"], w["wuk"], w["wuv"], nrm, tabs, outs)
        ins = dict(q)
        ins.update(exchange(L, fm, tm))
        oa = internal("oa%d" % L, [8, 64, NT], BF16)
        ob = internal("ob%d" % L, [8, 64, NT], BF16)
        h2 = internal("h2_%d" % L, [D, NT], F32)
        stage_dsa(ins, oa, idxm_d, iw)
        stage_mla(ins, ob, cmT_d)
        stage_merge(h1, oa, ob, w["wg"], w["wba"], w["wbb"], w["wout"], lnp, 1, h2)
        h3 = dout("outT", [D, NT], F32) if L == DEPTH - 1 else internal("h3_%d" % L, [D, NT], F32)
        stage_ffn(h2, h3, w["w13b"], w["w2b"], lnp, 2)
        hin = h3
    em.barrier()
    gs.close()
    return nc, em


def _rope_perm64():
    p = np.arange(64)
    p[0:8] = np.arange(8, 16)
    p[8:16] = np.arange(0, 8)
    return p


def _prep_weights(inp, L):
    f = np.float32
    w_in = np.asarray(inp["w_in"][L], f)
    perm = _rope_perm64()
    a_q = w_in[:, 0:512].reshape(D, 8, 64)
    a_k = w_in[:, 512:640].reshape(D, 2, 64)
    a_v = w_in[:, 640:768]
    i_q = w_in[:, 768:1280].reshape(D, 8, 64)
    i_k = w_in[:, 1280:1344]
    i_w = w_in[:, 1344:1352]
    c_q = w_in[:, 1352:1608]
    c_kv = w_in[:, 1608:1736]
    k_r = w_in[:, 1736:1768]
    gates = w_in[:, 1768:3816]
    aq_pair = np.stack([np.concatenate([a_q[:, r], a_q[:, 4 + r]], 1) for r in range(4)], 1)
    aq_pairB = np.stack([np.concatenate([a_q[:, r][:, perm], a_q[:, 4 + r][:, perm]], 1) for r in range(4)], 1)
    akA = a_k.reshape(D, 128)
    akB = a_k[:, :, perm].reshape(D, 128)
    iqA = i_q.reshape(D, 512)
    iqB = i_q[:, :, perm].reshape(D, 512)
    ikA = np.concatenate([i_k, i_k], 1)
    ikB = np.concatenate([i_k[:, perm], i_k[:, perm]], 1)
    krA = np.zeros((D, 96), f)
    krA[:, 64:96] = k_r
    krB = np.zeros((D, 96), f)
    krB[:, 64:80] = k_r[:, 16:32]
    krB[:, 80:96] = k_r[:, 0:16]
    wA = np.concatenate([aq_pair.reshape(D, 512), aq_pairB.reshape(D, 512), akA, akB, a_v, iqA, iqB, ikA, ikB,
                         i_w, c_q, c_kv, krA, krB], 1)
    assert wA.shape[1] == NA, wA.shape
    w_uq = np.asarray(inp["mla_w_uq"][L], f).reshape(256, 8, 96)
    uqB = w_uq.copy()
    uqB[:, :, 64:80] = w_uq[:, :, 80:96]
    uqB[:, :, 80:96] = w_uq[:, :, 64:80]
    wuq = np.concatenate([w_uq.reshape(256, 768), uqB.reshape(256, 768)], 1)
    w_ukv = np.asarray(inp["mla_w_ukv"][L], f).reshape(128, 8, 128)
    wuk = np.zeros((128, 8, 96), f)
    wuk[:, :, 0:64] = w_ukv[:, :, 0:64]
    wuv = np.ascontiguousarray(w_ukv[:, :, 64:128]).reshape(128, 512)
    ln_g = np.asarray(inp["ln_g"][L], f)
    ln_b = np.asarray(inp["ln_b"][L], f)
    lnp = np.zeros((128, 48), f)
    for li in range(3):
        lnp[:, li * 16:li * 16 + 8] = ln_g[li].reshape(8, 128).T
        lnp[:, li * 16 + 8:li * 16 + 16] = ln_b[li].reshape(8, 128).T
    nrm = np.zeros((128, 3), f)
    nrm[:, 0:2] = np.asarray(inp["mla_q_norm"][L], f).reshape(2, 128).T
    nrm[:, 2] = np.asarray(inp["mla_kv_norm"][L], f)
    c = np.ascontiguousarray
    return {
        "w13a_%d" % L: c(np.asarray(inp["ffn1_w13"][L], f)), "w2a_%d" % L: c(np.asarray(inp["ffn1_w2"][L], f)),
        "wA_%d" % L: c(wA), "wuq_%d" % L: c(wuq), "wuk_%d" % L: c(wuk.reshape(128, 768)), "wuv_%d" % L: c(wuv),
        "wg_%d" % L: c(gates), "wba_%d" % L: c(np.asarray(inp["w_branch_a"][L], f)),
        "wbb_%d" % L: c(np.asarray(inp["w_branch_b"][L], f)), "wout_%d" % L: c(np.asarray(inp["w_out"][L], f)),
        "w13b_%d" % L: c(np.asarray(inp["ffn2_w13"][L], f)), "w2b_%d" % L: c(np.asarray(inp["ffn2_w2"][L], f)),
        "lnp_%d" % L: lnp, "nrm_%d" % L: nrm,
    }


def _tables(j):
    n = np.arange(NT)
    pos = ((2 * (n // 128) + j) * 128 + n % 128).astype(np.float32)
    inv16 = (THETA ** (-(np.arange(8, dtype=np.float32) * 2.0 / 16))).astype(np.float32)
    inv32 = (THETA ** (-(np.arange(16, dtype=np.float32) * 2.0 / 32))).astype(np.float32)
    a16 = pos[None, :] * inv16[:, None]
    a32 = pos[None, :] * inv32[:, None]
    c64 = np.ones((64, NT), np.float32)
    s64 = np.zeros((64, NT), np.float32)
    c64[0:8] = np.cos(a16)
    c64[8:16] = np.cos(a16)
    s64[0:8] = -np.sin(a16)
    s64[8:16] = np.sin(a16)
    c96 = np.ones((96, NT), np.float32)
    s96 = np.zeros((96, NT), np.float32)
    c96[64:80] = np.cos(a32)
    c96[80:96] = np.cos(a32)
    s96[64:80] = -np.sin(a32)
    s96[80:96] = np.sin(a32)
    return {"c128": np.concatenate([c64, c64], 0), "s128": np.concatenate([s64, s64], 0), "c96": c96, "s96": s96}


def _masks(j):
    idxm = np.zeros((128, 3, 2, 3, 128), np.float32)
    cmT = np.zeros((128, 6, T), np.float32)
    t = np.arange(128)
    for c in range(3):
        qbz = 2 * c + j
        for jp in range(2):
            for cp in range(3):
                kbz = 2 * cp + jp
                if kbz < qbz:
                    vis = np.ones((128, 128), bool)
                elif kbz == qbz:
                    vis = t[None, :] <= t[:, None]
                else:
                    vis = np.zeros((128, 128), bool)
                idxm[:, c, jp, cp, :] = np.where(vis, 0.0, NEGM)
                cmT[:, jp * 3 + cp, c * 128:(c + 1) * 128] = vis.T.astype(np.float32)
    return {"idxm": idxm, "cmT": cmT.astype(ml_dtypes.bfloat16)}


_PROG = []


def _prog():
    if not _PROG:
        _PROG.append(build_program()[0])
    return _PROG[0]


def kernel(x, meta_tokens, ln_g, ln_b, ffn1_w13, ffn1_w2, w_in, mla_q_norm, mla_kv_norm,
           mla_w_uq, mla_w_ukv, w_branch_a, w_branch_b, w_out, ffn2_w13, ffn2_w2):
    inp = dict(x=x, meta_tokens=meta_tokens, ln_g=ln_g, ln_b=ln_b, ffn1_w13=ffn1_w13, ffn1_w2=ffn1_w2, w_in=w_in,
               mla_q_norm=mla_q_norm, mla_kv_norm=mla_kv_norm, mla_w_uq=mla_w_uq, mla_w_ukv=mla_w_ukv,
               w_branch_a=w_branch_a, w_branch_b=w_branch_b, w_out=w_out, ffn2_w13=ffn2_w13, ffn2_w2=ffn2_w2)
    x = np.asarray(x, np.float32)
    meta = np.asarray(meta_tokens, np.float32)
    W = {}
    for L in range(DEPTH):
        W.update(_prep_weights(inp, L))
    tabs = [_tables(j) for j in range(2)]
    msk = [_masks(j) for j in range(2)]
    maps = []
    for c in range(8):
        b, j = c // 2, c % 2
        h0 = np.zeros((NBLK * 128, D), np.float32)
        h0[0:N_META] = meta
        h0[N_META:N_META + SEQ] = x[b]
        mine = h0.reshape(NBLK, 128, D)[j::2].reshape(NT, D)
        m = dict(W)
        m.update(tabs[j])
        m.update(msk[j])
        m["xT"] = np.ascontiguousarray(mine.T)
        maps.append(m)
    res = run_bass_kernel_spmd(_prog(), maps, core_ids=list(range(8))).results
    out = np.zeros((BATCH, SEQ, D), np.float32)
    for b in range(BATCH):
        full = np.zeros((NBLK, 128, D), np.float32)
        for j in range(2):
            full[j::2] = np.asarray(res[2 * b + j]["outT"]).T.reshape(NBK, 128, D)
        out[b] = full.reshape(NBLK * 128, D)[N_META:N_META + SEQ]
    return out
```

```python
import numpy as np
import ml_dtypes
from contextlib import ExitStack
import concourse.bass as bass
import concourse.mybir as mybir
from concourse.bass_utils import run_bass_kernel_spmd

F32 = mybir.dt.float32
BF16 = mybir.dt.bfloat16
U8 = mybir.dt.uint8
FP8 = mybir.dt.float8e4
AF = mybir.ActivationFunctionType
ALU = mybir.AluOpType

P = 128
T = 384
NTL = 11
NBK = 33
NT = NBK * 128
NBLK = 66
D = 1024
DFF = 2816
N_META = 16
SEQ = 8192
BATCH = 4
DEPTH = 2
ALPHA = float((2 * DEPTH) ** 0.25)
THETA = 500000.0
NEGM = -1.0e4
NA = 3272
IDX_SCALE = float(64 ** -0.5 * 8 ** -0.5)
BIS_R = 16.0
BIS_IT = 17


class Buf:
    __slots__ = ("w", "r", "name")

    def __init__(self, name=""):
        self.w = {}
        self.r = {}
        self.name = name


class TL:
    __slots__ = ("ap", "b")

    def __init__(self, ap, name=""):
        self.ap = ap
        self.b = Buf(name)


class Em:
    def __init__(self, nc):
        self.nc = nc
        self.engs = {"pe": nc.tensor, "act": nc.scalar, "dve": nc.vector, "pool": nc.gpsimd, "sp": nc.sync}
        self.sems = {}
        self.cnt = {}
        self.seen = {k: {} for k in self.engs}
        self.grp = {}
        for k in self.engs:
            self.sems[k] = nc.alloc_semaphore("s_" + k)
            self.cnt[k] = 0
        self.nwait = 0
        self.nins = 0

    def _wait(self, eng, k, c):
        if self.seen[eng].get(k, 0) < c:
            self.engs[eng].wait_ge(self.sems[k], c)
            self.seen[eng][k] = c
            self.nwait += 1

    def _deps(self, eng, reads, writes):
        need = {}
        for b in reads:
            for k, c in b.w.items():
                if need.get(k, 0) < c:
                    need[k] = c
        for b in writes:
            for d in (b.w, b.r):
                for k, c in d.items():
                    if need.get(k, 0) < c:
                        need[k] = c
        for k, c in need.items():
            if k == eng and eng == "pe":
                continue
            self._wait(eng, k, c)

    def op(self, eng, fn, reads=(), writes=()):
        self._deps(eng, reads, writes)
        ins = fn(self.engs[eng])
        self.cnt[eng] += 1
        self.nins += 1
        ins.then_inc(self.sems[eng], 1)
        c = self.cnt[eng]
        for b in writes:
            b.w[eng] = c
            b.r = {}
        for b in reads:
            if b.r.get(eng, 0) < c:
                b.r[eng] = c
        return ins

    def dma(self, q, grp, out, in_, reads=(), writes=(), nslots=2):
        st = self.grp.setdefault(grp, [0, nslots])
        slot = st[0] % st[1]
        st[0] += 1
        key = (grp, slot)
        if key not in self.sems:
            self.sems[key] = self.nc.alloc_semaphore("d_%s_%d" % (grp, slot))
            self.cnt[key] = 0
        if self.cnt[key] > 0:
            self._wait(q, key, self.cnt[key])
        self._deps(q, reads, writes)
        ins = self.engs[q].dma_start(out=out, in_=in_)
        self.cnt[key] += 16
        self.nins += 1
        ins.then_inc(self.sems[key], 16)
        c = self.cnt[key]
        for b in writes:
            b.w[key] = c
            b.r = {}
        for b in reads:
            if b.r.get(key, 0) < c:
                b.r[key] = c
        return ins

    def coll(self, in_ap, out_ap, reads=(), writes=()):
        q, key = "pool", "cc"
        if key not in self.sems:
            self.sems[key] = self.nc.alloc_semaphore("s_cc")
            self.cnt[key] = 0
        if self.cnt[key] > 0:
            self._wait(q, key, self.cnt[key])
        self._deps(q, reads, writes)
        ins = self.engs[q].collective_compute("AllGather", ALU.bypass, replica_groups=[[0, 1], [2, 3], [4, 5], [6, 7]],
                                              ins=[in_ap], outs=[out_ap])
        self.cnt[key] += 1
        self.nins += 1
        ins.then_inc(self.sems[key])
        c = self.cnt[key]
        for b in writes:
            b.w[key] = c
            b.r = {}
        for b in reads:
            if b.r.get(key, 0) < c:
                b.r[key] = c
        return ins

    def barrier(self, engines=None):
        for e in (engines or list(self.engs)):
            for k, c in self.cnt.items():
                if k != e and c > 0:
                    self._wait(e, k, c)


class DT:
    def __init__(self, ap, n=NTL, name=""):
        self.ap = ap
        self.bufs = [Buf(name + str(i)) for i in range(n)]


def build_program():
    nc = bass.Bass("TRN2", target_bir_lowering=False)
    em = Em(nc)
    dts = {}

    def dram(name, shape, dt, kind, n=NTL):
        t = DT(nc.dram_tensor(name, list(shape), dt, kind=kind).ap(), n=n, name=name)
        dts[name] = t
        return t

    def din(name, shape, dt, n=NTL):
        return dram(name, shape, dt, "ExternalInput", n)

    def dout(name, shape, dt, n=NTL):
        return dram(name, shape, dt, "ExternalOutput", n)

    gs = ExitStack()
    uid = [0]

    def uq(name):
        uid[0] += 1
        return "%s__u%d" % (name, uid[0])

    def galloc(name, shape, dt):
        return TL(gs.enter_context(nc.sbuf_tensor(name, list(shape), dt)).ap(), name)

    ones_f = galloc("ones_f", [128, 128], F32)
    ident_b = galloc("ident_b", [128, 128], BF16)
    eps5 = galloc("eps5", [128, 1], F32)
    eps6 = galloc("eps6", [128, 1], F32)
    m240 = galloc("m240", [128, 1], F32)
    em.op("pool", lambda e: e.memset(ones_f.ap, 1.0), writes=[ones_f.b])
    em.op("pool", lambda e: e.memset(eps5.ap, 1e-5), writes=[eps5.b])
    em.op("pool", lambda e: e.memset(eps6.ap, 1e-6), writes=[eps6.b])
    em.op("pool", lambda e: e.memset(m240.ap, -240.0), writes=[m240.b])
    em.op("pool", lambda e: e.memset(ident_b.ap, 0.0), writes=[ident_b.b])
    em.op("pool", lambda e: e.affine_select(out=ident_b.ap, in_=ident_b.ap, pattern=[[-1, 128]],
                                            compare_op=ALU.not_equal, fill=1.0, base=0, channel_multiplier=1),
          reads=[ident_b.b], writes=[ident_b.b])

    def small_in(name, shape):
        d = din(name, shape, F32, n=1)
        t = galloc(name + "_sb", shape, F32)
        em.dma("sp", "ld_small", t.ap, d.ap, reads=[d.bufs[0]], writes=[t.b])
        return t

    def mm(out_tl, out_ap, lhsT, rhs, reads, start, stop):
        em.op("pe", lambda e: e.matmul(out_ap, lhsT=lhsT, rhs=rhs, start=start, stop=stop),
              reads=reads, writes=[out_tl.b])

    def act(out_tl, out_ap, in_ap, func, reads, scale=None, bias=None):
        kw = {}
        if scale is not None:
            kw["scale"] = scale
        if bias is not None:
            kw["bias"] = bias
        em.op("act", lambda e: e.activation(out=out_ap, in_=in_ap, func=func, **kw), reads=reads, writes=[out_tl.b])

    def tt(eng, out_tl, out_ap, in0, in1, op, reads):
        em.op(eng, lambda e: e.tensor_tensor(out=out_ap, in0=in0, in1=in1, op=op), reads=reads, writes=[out_tl.b])

    def ts(eng, out_tl, out_ap, in0, s1, op0, reads, s2=None, op1=None, accum=None, extra_w=()):
        kw = {}
        if op1 is not None:
            kw["op1"] = op1
        if accum is not None:
            kw["accum_out"] = accum
        em.op(eng, lambda e: e.tensor_scalar(out=out_ap, in0=in0, scalar1=s1, scalar2=s2, op0=op0, **kw),
              reads=reads, writes=[out_tl.b] + list(extra_w))

    def stt(out_tl, out_ap, in0, scalar, in1, op0, op1, reads):
        em.op("dve", lambda e: e.scalar_tensor_tensor(out=out_ap, in0=in0, scalar=scalar, in1=in1, op0=op0, op1=op1),
              reads=reads, writes=[out_tl.b])

    def copy(eng, out_tl, out_ap, in_ap, reads):
        if eng == "act":
            act(out_tl, out_ap, in_ap, AF.Copy, reads)
        else:
            em.op(eng, lambda e: e.tensor_copy(out=out_ap, in_=in_ap), reads=reads, writes=[out_tl.b])

    cast_rr = [0]

    def load_w(es_alloc, d, dst, kc, ncols, stg, kp=128):
        W = 704
        for k in range(kc):
            for c0 in range(0, ncols, W):
                cw = min(W, ncols - c0)
                s = stg[cast_rr[0] % len(stg)]
                eng = ("dve", "act")[cast_rr[0] % 2]
                cast_rr[0] += 1
                em.dma("sp", "ld_w", s.ap[0:kp, 0:cw], d.ap[k * kp:(k + 1) * kp, c0:c0 + cw],
                       reads=[d.bufs[0]], writes=[s.b], nslots=3)
                copy(eng, dst, dst.ap[0:kp, k, c0:c0 + cw], s.ap[0:kp, 0:cw], reads=[s.b])

    def ln_stats(xr, ps1, ps2, sq):
        for j in range(8):
            mm(ps1, ps1.ap[:, 0:T], ones_f.ap, xr[j].ap, [ones_f.b, xr[j].b], j == 0, j == 7)
        for j in range(8):
            s = sq[j % 2]
            act(s, s.ap, xr[j].ap, AF.Square, [xr[j].b])
            mm(ps2, ps2.ap[:, 0:T], ones_f.ap, s.ap, [ones_f.b, s.b], j == 0, j == 7)

    def ln_apply(xr, lnp, li, ps1, ps2, tmp, st):
        mean, msq, var, rstd, bt = st["mean"], st["msq"], st["var"], st["rstd"], st["bt"]
        act(mean, mean.ap, ps1.ap[:, 0:T], AF.Copy, [ps1.b], scale=1.0 / D)
        act(msq, msq.ap, mean.ap, AF.Square, [mean.b])
        stt(var, var.ap, ps2.ap[:, 0:T], 1.0 / D, msq.ap, ALU.mult, ALU.subtract, [ps2.b, msq.b])
        act(var, var.ap, var.ap, AF.Sqrt, [var.b, eps5.b], bias=eps5.ap[:, 0:1])
        em.op("dve", lambda e: e.reciprocal(out=rstd.ap, in_=var.ap), reads=[var.b], writes=[rstd.b])
        stt(bt, bt.ap, mean.ap, -1.0, rstd.ap, ALU.mult, ALU.mult, [mean.b, rstd.b])
        for j in range(8):
            t = tmp[j % 2]
            tt("dve", t, t.ap, xr[j].ap, rstd.ap, ALU.mult, [xr[j].b, rstd.b])
            tt("pool", t, t.ap, t.ap, bt.ap, ALU.add, [t.b, bt.b])
            act(xr[j], xr[j].ap, t.ap, AF.Identity, [t.b, lnp.b],
                scale=lnp.ap[:, li * 16 + j:li * 16 + j + 1], bias=lnp.ap[:, li * 16 + 8 + j:li * 16 + 8 + j + 1])

    def layer_norm(xr, lnp, li, ps1, ps2, sq, tmp, st):
        ln_stats(xr, ps1, ps2, sq)
        ln_apply(xr, lnp, li, ps1, ps2, tmp, st)

    def stage_ffn(hin_d, hout_d, w13_d, w2_d, lnp, li):
        with ExitStack() as es:
            def sb(name, shape, dt):
                return TL(es.enter_context(nc.sbuf_tensor(uq(name), list(shape), dt)).ap(), name)

            def ps(name):
                return TL(es.enter_context(nc.psum_tensor(uq(name), [128, 512], F32)).ap(), name)

            w13b = sb("w13b", [128, 8, 2 * DFF], BF16)
            w2b = sb("w2b", [128, 22, D], BF16)
            stg = [sb("stg0", [128, 704], F32), sb("stg1", [128, 704], F32), sb("stg2", [128, 704], F32)]
            load_w(es, w13_d, w13b, 8, 2 * DFF, stg)
            load_w(es, w2_d, w2b, 22, D, stg)
            X = [[sb("x%d_%d" % (b_, j), [128, T], F32) for j in range(8)] for b_ in range(2)]
            hb = sb("hb", [128, 8, T], BF16)
            G = sb("G", [128, 22, T], BF16)
            sg = [sb("sg0", [128, T], F32), sb("sg1", [128, T], F32)]
            sq = [sb("sq0", [128, T], F32), sb("sq1", [128, T], F32)]
            st = {n: sb("ln_" + n, [128, T], F32) for n in ("mean", "var", "bt")}
            st["msq"] = st["bt"]
            st["rstd"] = st["var"]
            tmpb = [sb("tmpb0", [128, T], F32), sb("tmpb1", [128, T], F32)]
            pg = [ps("pg0"), ps("pg1")]
            pu = [ps("pu0"), ps("pu1")]
            py = [ps("py0"), ps("py1")]
            ps1 = ps("ps1")
            ps2 = ps("ps2")

            def ph_a_hb(m):
                n0 = m * T
                em.dma("pool", "ld_hb", hb.ap, hin_d.ap[:, n0:n0 + T].rearrange("(k p) t -> p k t", p=128),
                       reads=[hin_d.bufs[m]], writes=[hb.b], nslots=2)

            def ph_a_x(m):
                n0 = m * T
                xb = X[m % 2]
                for j in range(8):
                    em.dma("sp", "ld_h", xb[j].ap, hin_d.ap[j * 128:(j + 1) * 128, n0:n0 + T],
                           reads=[hin_d.bufs[m]], writes=[xb[j].b], nslots=4)

            def ph_b(m):
                for c in range(22):
                    g_, u_ = pg[c % 2], pu[c % 2]
                    for k in range(8):
                        mm(g_, g_.ap[:, 0:T], w13b.ap[:, k, c * 128:(c + 1) * 128], hb.ap[:, k, :], [w13b.b, hb.b], k == 0, k == 7)
                    for k in range(8):
                        mm(u_, u_.ap[:, 0:T], w13b.ap[:, k, DFF + c * 128:DFF + (c + 1) * 128], hb.ap[:, k, :], [w13b.b, hb.b], k == 0, k == 7)
                    s_ = sg[c % 2]
                    act(s_, s_.ap, g_.ap[:, 0:T], AF.Silu, [g_.b])
                    tt("dve", G, G.ap[:, c, :], s_.ap, u_.ap[:, 0:T], ALU.mult, [s_.b, u_.b])

            def ph_c(m):
                xb = X[m % 2]
                for j in range(8):
                    y_ = py[j % 2]
                    for k in range(22):
                        mm(y_, y_.ap[:, 0:T], w2b.ap[:, k, j * 128:(j + 1) * 128], G.ap[:, k, :], [w2b.b, G.b], k == 0, k == 21)
                    act(xb[j], xb[j].ap, xb[j].ap, AF.Copy, [xb[j].b], scale=ALPHA)
                    stt(xb[j], xb[j].ap, y_.ap[:, 0:T], 0.5, xb[j].ap, ALU.mult, ALU.add, [y_.b, xb[j].b])

            def ph_d1(m):
                ln_stats(X[m % 2], ps1, ps2, sq)

            def ph_d2(m):
                n0 = m * T
                xb = X[m % 2]
                ln_apply(xb, lnp, li, ps1, ps2, tmpb, st)
                for j in range(8):
                    em.dma("sp", "st_h", hout_d.ap[j * 128:(j + 1) * 128, n0:n0 + T], xb[j].ap,
                           reads=[xb[j].b], writes=[hout_d.bufs[m]], nslots=4)

            ph_a_hb(0)
            ph_a_x(0)
            ph_b(0)
            for m in range(NTL):
                if m + 1 < NTL:
                    ph_a_hb(m + 1)
                ph_c(m)
                if m >= 1:
                    ph_d2(m - 1)
                if m + 1 < NTL:
                    ph_a_x(m + 1)
                    ph_b(m + 1)
                ph_d1(m)
            ph_d2(NTL - 1)
            em.barrier()

    def stage_proj(h1_d, wA_d, wuq_d, wuk_d, wuv_d, nrm, tabs, outs):
        c128_d, s128_d, c96_d, s96_d = tabs
        with ExitStack() as es:
            def sb(name, shape, dt):
                return TL(es.enter_context(nc.sbuf_tensor(uq(name), list(shape), dt)).ap(), name)

            def ps(name):
                return TL(es.enter_context(nc.psum_tensor(uq(name), [128, 512], F32)).ap(), name)

            wAb = sb("wAb", [128, 8, NA], BF16)
            wuqb = sb("wuqb", [128, 2, 1536], BF16)
            wukb = sb("wukb", [128, 1, 768], BF16)
            wuvb = sb("wuvb", [128, 1, 512], BF16)
            stg = [sb("stg0", [128, 704], F32), sb("stg1", [128, 704], F32), sb("stg2", [128, 704], F32)]
            load_w(es, wA_d, wAb, 8, NA, stg)
            load_w(es, wuq_d, wuqb, 2, 1536, stg)
            load_w(es, wuk_d, wukb, 1, 768, stg)
            load_w(es, wuv_d, wuvb, 1, 512, stg)
            hbs = [sb("hb0", [128, 8, T], BF16), sb("hb1", [128, 8, T], BF16)]
            cur = {}
            c128 = sb("c128", [128, T], F32)
            s128 = sb("s128", [128, T], F32)
            c96 = sb("c96", [96, T], F32)
            s96 = sb("s96", [96, T], F32)
            t1 = [sb("t1_0", [128, T], F32), sb("t1_1", [128, T], F32)]
            t2 = [sb("t2_0", [128, T], F32), sb("t2_1", [128, T], F32)]
            aq_st = sb("aq_st", [128, 4, T], BF16)
            iq_st = sb("iq_st", [128, 4, T], BF16)
            ak_st = sb("ak_st", [128, T], BF16)
            ik_st = sb("ik_st", [128, T], BF16)
            mq_st = sb("mq_st", [96, 8, T], BF16)
            mk_st = sb("mk_st", [96, 8, T], BF16)
            av_st = sb("av_st", [128, 3, 130], BF16)
            mv_st = sb("mv_st", [128, 3, 8, 65], BF16)
            iw_st = sb("iw_st", [128, 3, 8], F32)
            cq_sb = sb("cq_sb", [128, 2, T], F32)
            ckv_sb = sb("ckv_sb", [128, T], F32)
            sqs = [sb("sqs0", [128, T], F32), sb("sqs1", [128, T], F32)]
            rs = sb("rs", [128, T], F32)
            cqn = sb("cqn", [128, 2, T], BF16)
            ckvn = sb("ckvn", [128, T], BF16)
            bs = sb("bs", [96, T], F32)
            pA = [ps("pA0"), ps("pA1")]
            pB = [ps("pB0"), ps("pB1")]
            pC = [ps("pC0"), ps("pC1")]
            pS = ps("pS")
            em.op("pool", lambda e: e.memset(av_st.ap, 1.0), writes=[av_st.b])
            em.op("pool", lambda e: e.memset(mv_st.ap, 1.0), writes=[mv_st.b])
            rr = [0]

            def proj(out_ps, col0, M):
                hb = cur["hb"]
                for k in range(8):
                    mm(out_ps, out_ps.ap[0:M, 0:T], wAb.ap[:, k, col0:col0 + M], hb.ap[:, k, :], [wAb.b, hb.b], k == 0, k == 7)

            def roped(colA, colB, dst_tl, dst_ap):
                i = rr[0] % 2
                rr[0] += 1
                a_, b_ = pA[i], pB[i]
                proj(a_, colA, 128)
                proj(b_, colB, 128)
                tt("dve", t1[i], t1[i].ap, a_.ap[:, 0:T], c128.ap, ALU.mult, [a_.b, c128.b])
                tt("dve", t2[i], t2[i].ap, b_.ap[:, 0:T], s128.ap, ALU.mult, [b_.b, s128.b])
                tt("pool", dst_tl, dst_ap, t1[i].ap, t2[i].ap, ALU.add, [t1[i].b, t2[i].b])

            for m in range(NTL):
                n0 = m * T
                if m == 0:
                    em.dma("pool", "ld_hb", hbs[0].ap, h1_d.ap[:, 0:T].rearrange("(k p) t -> p k t", p=128),
                           reads=[h1_d.bufs[0]], writes=[hbs[0].b], nslots=2)
                if m + 1 < NTL:
                    em.dma("pool", "ld_hb", hbs[(m + 1) % 2].ap, h1_d.ap[:, n0 + T:n0 + 2 * T].rearrange("(k p) t -> p k t", p=128),
                           reads=[h1_d.bufs[m + 1]], writes=[hbs[(m + 1) % 2].b], nslots=2)
                hb = hbs[m % 2]
                cur["hb"] = hb
                for (tl_, d_) in ((c128, c128_d), (s128, s128_d), (c96, c96_d), (s96, s96_d)):
                    em.dma("sp", "ld_tab", tl_.ap, d_.ap[:, n0:n0 + T], reads=[d_.bufs[0]], writes=[tl_.b], nslots=4)
                for r in range(4):
                    roped(r * 128, 512 + r * 128, aq_st, aq_st.ap[:, r, :])
                roped(1024, 1152, ak_st, ak_st.ap)
                for r in range(4):
                    roped(1408 + r * 128, 1920 + r * 128, iq_st, iq_st.ap[:, r, :])
                roped(2432, 2560, ik_st, ik_st.ap)
                for c in range(3):
                    p_ = pC[c % 2]
                    for k in range(8):
                        mm(p_, p_.ap[:, 0:128], hb.ap[:, k, c * 128:(c + 1) * 128], wAb.ap[:, k, 1280:1408], [wAb.b, hb.b], k == 0, k == 7)
                    copy("act", av_st, av_st.ap[:, c, 0:64], p_.ap[:, 0:64], [p_.b])
                    copy("act", av_st, av_st.ap[:, c, 65:129], p_.ap[:, 64:128], [p_.b])
                    p2 = pC[(c + 1) % 2]
                    for k in range(8):
                        mm(p2, p2.ap[:, 0:8], hb.ap[:, k, c * 128:(c + 1) * 128], wAb.ap[:, k, 2688:2696], [wAb.b, hb.b], k == 0, k == 7)
                    act(iw_st, iw_st.ap[:, c, :], p2.ap[:, 0:8], AF.Copy, [p2.b], scale=IDX_SCALE)
                for c in range(2):
                    p_ = pC[c % 2]
                    proj(p_, 2696 + c * 128, 128)
                    copy("act", cq_sb, cq_sb.ap[:, c, :], p_.ap[:, 0:T], [p_.b])
                    act(sqs[c], sqs[c].ap, p_.ap[:, 0:T], AF.Square, [p_.b])
                for c in range(2):
                    mm(pS, pS.ap[:, 0:T], ones_f.ap, sqs[c].ap, [ones_f.b, sqs[c].b], c == 0, c == 1)
                act(rs, rs.ap, pS.ap[:, 0:T], AF.Sqrt, [pS.b, eps6.b], scale=1.0 / 256, bias=eps6.ap[:, 0:1])
                em.op("dve", lambda e: e.reciprocal(out=rs.ap, in_=rs.ap), reads=[rs.b], writes=[rs.b])
                for c in range(2):
                    stt(cqn, cqn.ap[:, c, :], cq_sb.ap[:, c, :], nrm.ap[:, c:c + 1], rs.ap, ALU.mult, ALU.mult, [cq_sb.b, nrm.b, rs.b])
                p_ = pC[0]
                proj(p_, 2952, 128)
                copy("act", ckv_sb, ckv_sb.ap, p_.ap[:, 0:T], [p_.b])
                act(sqs[0], sqs[0].ap, p_.ap[:, 0:T], AF.Square, [p_.b])
                mm(pS, pS.ap[:, 0:T], ones_f.ap, sqs[0].ap, [ones_f.b, sqs[0].b], True, True)
                act(rs, rs.ap, pS.ap[:, 0:T], AF.Sqrt, [pS.b, eps6.b], scale=1.0 / 128, bias=eps6.ap[:, 0:1])
                em.op("dve", lambda e: e.reciprocal(out=rs.ap, in_=rs.ap), reads=[rs.b], writes=[rs.b])
                stt(ckvn, ckvn.ap, ckv_sb.ap, nrm.ap[:, 2:3], rs.ap, ALU.mult, ALU.mult, [ckv_sb.b, nrm.b, rs.b])
                for h in range(8):
                    i = rr[0] % 2
                    rr[0] += 1
                    a_, b_ = pA[i], pB[i]
                    for k in range(2):
                        mm(a_, a_.ap[0:96, 0:T], wuqb.ap[:, k, 96 * h:96 * h + 96], cqn.ap[:, k, :], [wuqb.b, cqn.b], k == 0, k == 1)
                    for k in range(2):
                        mm(b_, b_.ap[0:96, 0:T], wuqb.ap[:, k, 768 + 96 * h:768 + 96 * h + 96], cqn.ap[:, k, :], [wuqb.b, cqn.b], k == 0, k == 1)
                    tt("dve", t1[i], t1[i].ap[0:96, :], a_.ap[0:96, 0:T], c96.ap, ALU.mult, [a_.b, c96.b])
                    tt("dve", t2[i], t2[i].ap[0:96, :], b_.ap[0:96, 0:T], s96.ap, ALU.mult, [b_.b, s96.b])
                    tt("pool", mq_st, mq_st.ap[:, h, :], t1[i].ap[0:96, :], t2[i].ap[0:96, :], ALU.add, [t1[i].b, t2[i].b])
                p_ = pC[1]
                proj(p_, 3176, 96)
                tt("dve", bs, bs.ap, p_.ap[0:96, 0:T], s96.ap, ALU.mult, [p_.b, s96.b])
                for h in range(8):
                    i = rr[0] % 2
                    rr[0] += 1
                    a_ = pA[i]
                    for k in range(8):
                        mm(a_, a_.ap[0:96, 0:T], wAb.ap[:, k, 3080:3176], hb.ap[:, k, :], [wAb.b, hb.b], k == 0, False)
                    mm(a_, a_.ap[0:96, 0:T], wukb.ap[:, 0, 96 * h:96 * h + 96], ckvn.ap, [wukb.b, ckvn.b], False, True)
                    tt("dve", t1[i], t1[i].ap[0:96, :], a_.ap[0:96, 0:T], c96.ap, ALU.mult, [a_.b, c96.b])
                    tt("pool", mk_st, mk_st.ap[:, h, :], t1[i].ap[0:96, :], bs.ap, ALU.add, [t1[i].b, bs.b])
                for c in range(3):
                    p_ = pB[c % 2]
                    mm(p_, p_.ap[:, 0:512], ckvn.ap[:, c * 128:(c + 1) * 128], wuvb.ap[:, 0, :], [wuvb.b, ckvn.b], True, True)
                    copy("act", mv_st, mv_st.ap[:, c, :, 0:64], p_.ap[:, 0:512].rearrange("p (h d) -> p h d", d=64), [p_.b])
                o = outs
                em.dma("pool", "st_q", o["aq"].ap[:, :, n0:n0 + T].rearrange("n p t -> p n t"), aq_st.ap, reads=[aq_st.b], writes=[o["aq"].bufs[m]], nslots=4)
                em.dma("pool", "st_q", o["iq"].ap[:, :, n0:n0 + T].rearrange("n p t -> p n t"), iq_st.ap, reads=[iq_st.b], writes=[o["iq"].bufs[m]], nslots=4)
                em.dma("pool", "st_q", o["mq"].ap[:, :, n0:n0 + T].rearrange("n p t -> p n t"), mq_st.ap, reads=[mq_st.b], writes=[o["mq"].bufs[m]], nslots=4)
                for h in range(8):
                    em.dma("pool", "st_q", o["mk"][h].ap[:, n0:n0 + T], mk_st.ap[:, h, :], reads=[mk_st.b], writes=[o["mk"][h].bufs[m]], nslots=4)
                em.dma("pool", "st_k", o["ak"].ap[:, n0:n0 + T], ak_st.ap, reads=[ak_st.b], writes=[o["ak"].bufs[m]], nslots=4)
                em.dma("pool", "st_k", o["ik"].ap[:, n0:n0 + T], ik_st.ap, reads=[ik_st.b], writes=[o["ik"].bufs[m]], nslots=4)
                em.dma("pool", "st_k", o["av"].ap[n0:n0 + T, :].rearrange("(c p) f -> p c f", p=128), av_st.ap, reads=[av_st.b], writes=[o["av"].bufs[m]], nslots=4)
                for (v_, h0_, h1_) in o["mvc"]:
                    em.dma("pool", "st_k", v_.ap[n0:n0 + T, :].rearrange("(c p) (h d) -> p c h d", p=128, d=65),
                           mv_st.ap[:, :, h0_:h1_, :], reads=[mv_st.b], writes=[v_.bufs[m]], nslots=4)
                em.dma("pool", "st_k", o["iw"].ap[:, 3 * m:3 * m + 3, :], iw_st.ap, reads=[iw_st.b], writes=[o["iw"].bufs[m]], nslots=4)
            em.barrier()

    def finalize_heads(accs, o_st, hlist, accsb, rrow, bc_ps, bc_sb, neg1):
        for r, h in enumerate(hlist):
            a_ = accs[r]
            sb_ = accsb[r % len(accsb)]
            copy("act", sb_, sb_.ap[0:65, :], a_.ap[0:65, 0:T], [a_.b])
            act(rrow, rrow.ap[64:65, :], sb_.ap[64:65, :], AF.Ln, [sb_.b])
            act(rrow, rrow.ap[64:65, :], rrow.ap[64:65, :], AF.Exp, [rrow.b], scale=-1.0)
            mm(bc_ps, bc_ps.ap[0:64, 0:T], ones_f.ap[64:65, 0:64], rrow.ap[64:65, :], [ones_f.b, rrow.b], True, True)
            copy("act", bc_sb, bc_sb.ap, bc_ps.ap[0:64, 0:T], [bc_ps.b])
            tt("pool", o_st, o_st.ap[:, h, :], sb_.ap[0:64, :], bc_sb.ap, ALU.mult, [sb_.b, bc_sb.b])

    def stage_dsa(ins, oa_d, idxm_d, iw_d):
        with ExitStack() as es:
            def sb(name, shape, dt):
                return TL(es.enter_context(nc.sbuf_tensor(uq(name), list(shape), dt)).ap(), name)

            def ps(name):
                return TL(es.enter_context(nc.psum_tensor(uq(name), [128, 512], F32)).ap(), name)

            idxm = sb("idxm_sb", [128, 3, 2, 3, 128], F32)
            em.dma("sp", "ld_small", idxm.ap, idxm_d.ap, reads=[idxm_d.bufs[0]], writes=[idxm.b])
            ik_sb = sb("ik_sb", [128, 2, NT], BF16)
            ak_sb = sb("ak_sb", [128, 2, NT], BF16)
            av_sb = sb("av_sb", [128, 2, NBK, 130], BF16)
            iw_sb = sb("iw_sb", [128, NBK, 8], F32)
            aw_sb = sb("aw_sb", [128, NBK, 8], F32)
            sg_sb = sb("sg_sb", [128, NBK, 8], F32)
            for j in range(2):
                em.dma("sp", "ld_k", ik_sb.ap[:, j, :], ins["ik"].ap[j], reads=ins["ik"].bufs, writes=[ik_sb.b], nslots=4)
                em.dma("sp", "ld_k", ak_sb.ap[:, j, :], ins["ak"].ap[j], reads=ins["ak"].bufs, writes=[ak_sb.b], nslots=4)
                for i0 in range(0, NBK, 11):
                    em.dma("sp", "ld_k", av_sb.ap[:, j, i0:i0 + 11, :],
                           ins["av"].ap[j, i0 * 128:(i0 + 11) * 128, :].rearrange("(i p) c -> p i c", p=128),
                           reads=ins["av"].bufs, writes=[av_sb.b], nslots=4)
            em.dma("sp", "ld_k", iw_sb.ap, iw_d.ap, reads=iw_d.bufs, writes=[iw_sb.b], nslots=4)
            act(aw_sb, aw_sb.ap, iw_sb.ap, AF.Abs, [iw_sb.b])
            act(sg_sb, sg_sb.ap, iw_sb.ap, AF.Sign, [iw_sb.b])
            scs = [sb("sc0", [128, 2 * NT], F32), sb("sc1", [128, 2 * NT], F32)]
            junk = sb("junk", [128, 2 * NT], U8)
            maskT = sb("maskT", [128, NBLK, T], FP8)
            iqz = [sb("iqz0", [128, 4, T], BF16), sb("iqz1", [128, 4, T], BF16)]
            aqz = [sb("aqz0", [128, 4, T], BF16), sb("aqz1", [128, 4, T], BF16)]
            for z_ in (iqz, aqz):
                em.op("dve", lambda e: e.memset(z_[0].ap[64:128, :, :], 0.0), writes=[z_[0].b])
                em.op("dve", lambda e: e.memset(z_[1].ap[0:64, :, :], 0.0), writes=[z_[1].b])
            dg = [sb("dg0", [128, 8, 128], BF16), sb("dg1", [128, 8, 128], BF16)]
            rl = [sb("rl%d" % i, [128, 512], BF16) for i in range(4)]
            mch = [sb("mch0", [128, 1024], BF16), sb("mch1", [128, 1024], BF16)]
            pt = [sb("pt%d" % i, [128, T], BF16) for i in range(4)]
            accsb = [sb("accsb0", [65, T], F32), sb("accsb1", [65, T], F32)]
            neg1 = sb("neg1", [65, T], F32)
            em.op("pool", lambda e: e.memset(neg1.ap, -1.0), writes=[neg1.b])
            thrs = [sb("thr0", [128, 1], F32), sb("thr1", [128, 1], F32)]
            cnts = [sb("cnt0", [128, 1], F32), sb("cnt1", [128, 1], F32)]
            dds = [sb("dd0", [128, 1], F32), sb("dd1", [128, 1], F32)]
            rrow = sb("rrow", [65, T], F32)
            bc_sb = sb("bc_sb", [64, T], F32)
            o_st = sb("o_st", [64, 8, T], BF16)
            psx = [ps("psx0"), ps("psx1")]
            accs = [ps("acc%d" % r) for r in range(4)]
            bc_ps = ps("bc_ps")
            ptr = TL(es.enter_context(nc.psum_tensor(uq("ptr"), [128, 1024], BF16)).ap(), "ptr")
            xi = [0]
            ci = [0]
            mi = [0]
            LA = 2

            def idx_block(m, c):
                nI = 3 * m + 3
                ncol = 2 * nI * 128
                ib = 3 * m + c
                sc = scs[ib % 2]
                scv = sc.ap[:, 0:ncol].rearrange("p (j i t) -> p j i t", j=2, t=128)
                dg_ = dg[ib % 2]
                for h in range(8):
                    act(dg_, dg_.ap[:, h, :], ident_b.ap, AF.Identity, [ident_b.b, sg_sb.b], scale=sg_sb.ap[:, ib, h:h + 1])
                steps = []
                for j in range(2):
                    for i0 in range(0, nI, 4):
                        for h in range(8):
                            steps.append((j, i0, h))
                held = {}
                for si in range(len(steps) + LA):
                    if si < len(steps):
                        j, i0, h = steps[si]
                        cw = min(4, nI - i0) * 128
                        r, hf = h // 2, h % 2
                        p_ = psx[xi[0] % 2]
                        r_ = rl[xi[0] % 4]
                        xi[0] += 1
                        mm(p_, p_.ap[:, 0:cw], iqz[hf].ap[:, r, c * 128:(c + 1) * 128],
                           ik_sb.ap[:, j, i0 * 128:i0 * 128 + cw], [iqz[hf].b, ik_sb.b], True, True)
                        act(r_, r_.ap[:, 0:cw], p_.ap[:, 0:cw], AF.Relu, [p_.b, aw_sb.b], scale=aw_sb.ap[:, ib, h:h + 1])
                        held[si] = r_
                    if si >= LA:
                        j, i0, h = steps[si - LA]
                        cw = min(4, nI - i0) * 128
                        col0 = (j * nI + i0) * 128
                        r_ = held.pop(si - LA)
                        if h == 0:
                            ci[0] += 1
                        a_ = accs[2 + ci[0] % 2]
                        mm(a_, a_.ap[:, 0:cw], dg_.ap[:, h, :], r_.ap[:, 0:cw], [dg_.b, r_.b], h == 0, h == 7)
                        if h == 7:
                            copy("act", sc, sc.ap[:, col0:col0 + cw], a_.ap[:, 0:cw], [a_.b])

            def bisect_block(m, c):
                nI = 3 * m + 3
                ncol = 2 * nI * 128
                ib = 3 * m + c
                sc, thr, cnt, dd = scs[ib % 2], thrs[ib % 2], cnts[ib % 2], dds[ib % 2]
                scv = sc.ap[:, 0:ncol].rearrange("p (j i t) -> p j i t", j=2, t=128)
                tt("dve", sc, scv[:, :, nI - 3:nI, :], scv[:, :, nI - 3:nI, :], idxm.ap[:, c, :, :, :], ALU.add, [sc.b, idxm.b])
                em.op("dve", lambda e: e.memset(thr.ap, 3.0e-5), writes=[thr.b])
                for it in range(1, BIS_IT + 1):
                    step = BIS_R / (2 ** it)
                    ts("dve", junk, junk.ap[:, 0:ncol], sc.ap[:, 0:ncol], thr.ap[:, 0:1], ALU.is_ge, [sc.b, thr.b],
                       op1=ALU.add, accum=cnt.ap[:, 0:1], extra_w=[cnt.b])
                    ts("dve", dd, dd.ap, cnt.ap, 255.5, ALU.is_ge, [cnt.b], s2=2.0 * step, op1=ALU.mult)
                    dec = -step if it < BIS_IT else -2.0 * step
                    stt(thr, thr.ap, thr.ap, dec, dd.ap, ALU.add, ALU.add, [thr.b, dd.b])

            def mask_block(m, c):
                nI = 3 * m + 3
                ib = 3 * m + c
                sc, thr = scs[ib % 2], thrs[ib % 2]
                nkb = 2 * nI
                for k0 in range(0, nkb, 8):
                    nk = min(8, nkb - k0)
                    mc = mch[mi[0] % 2]
                    mi[0] += 1
                    ts("dve", mc, mc.ap[:, 0:nk * 128], sc.ap[:, k0 * 128:(k0 + nk) * 128], thr.ap[:, 0:1], ALU.is_ge, [sc.b, thr.b])
                    for q in range(nk):
                        em.op("pe", lambda e: e.transpose(out=ptr.ap[:, q * 128:(q + 1) * 128], in_=mc.ap[:, q * 128:(q + 1) * 128], identity=ident_b.ap),
                              reads=[mc.b, ident_b.b], writes=[ptr.b])
                    em.op("act", lambda e: e.activation(out=maskT.ap[:, k0:k0 + nk, c * 128:(c + 1) * 128],
                                                        in_=ptr.ap[:, 0:nk * 128].rearrange("p (k t) -> p k t", t=128),
                                                        func=AF.Identity, scale=240.0, bias=m240.ap[:, 0:1], saturate=False),
                          reads=[ptr.b, m240.b], writes=[maskT.b])

            def load_iq(m):
                n0 = m * T
                for hf_ in range(2):
                    pr_ = slice(64 * hf_, 64 * hf_ + 64)
                    em.dma("sp", "ld_q", iqz[hf_].ap[pr_, :, :], ins["iq"].ap[:, pr_, n0:n0 + T].rearrange("n p t -> p n t"),
                           reads=[ins["iq"].bufs[m]], writes=[iqz[hf_].b], nslots=4)

            def load_aq(m):
                n0 = m * T
                for hf_ in range(2):
                    pr_ = slice(64 * hf_, 64 * hf_ + 64)
                    em.dma("sp", "ld_q", aqz[hf_].ap[pr_, :, :], ins["aq"].ap[:, pr_, n0:n0 + T].rearrange("n p t -> p n t"),
                           reads=[ins["aq"].bufs[m]], writes=[aqz[hf_].b], nslots=4)

            def attention(m):
                n0 = m * T
                nI = 3 * m + 3
                nkb = 2 * nI
                for g in range(2):
                    for half in range(2):
                        rs_ = [2 * half, 2 * half + 1]
                        steps = [(kb, r) for kb in range(nkb) for r in rs_]
                        held = {}
                        for si in range(len(steps) + LA):
                            if si < len(steps):
                                kb, r = steps[si]
                                j, i = kb // nI, kb % nI
                                p_ = psx[xi[0] % 2]
                                e_ = pt[xi[0] % 4]
                                xi[0] += 1
                                mm(p_, p_.ap[:, 0:T], ak_sb.ap[:, j, i * 128:(i + 1) * 128], aqz[g].ap[:, r, :], [ak_sb.b, aqz[g].b], True, False)
                                mm(p_, p_.ap[:, 0:T], ident_b.ap, maskT.ap[:, kb, :], [ident_b.b, maskT.b], False, True)
                                act(e_, e_.ap, p_.ap[:, 0:T], AF.Exp, [p_.b], scale=0.125)
                                held[si] = e_
                            if si >= LA:
                                kb, r = steps[si - LA]
                                j, i = kb // nI, kb % nI
                                m_ = held.pop(si - LA)
                                a_ = accs[r % 2]
                                mm(a_, a_.ap[0:65, 0:T], av_sb.ap[:, j, i, 65 * g:65 * g + 65], m_.ap, [av_sb.b, m_.b], kb == 0, kb == nkb - 1)
                        finalize_heads(accs[0:2], o_st, [4 * g + r for r in rs_], accsb, rrow, bc_ps, bc_sb, neg1)
                em.dma("pool", "st_o", oa_d.ap[:, :, n0:n0 + T].rearrange("n p t -> p n t"), o_st.ap, reads=[o_st.b], writes=[oa_d.bufs[m]])

            load_iq(0)
            idx_block(0, 0)
            bisect_block(0, 0)
            idx_block(0, 1)
            mask_block(0, 0)
            bisect_block(0, 1)
            idx_block(0, 2)
            mask_block(0, 1)
            bisect_block(0, 2)
            if NTL > 1:
                load_iq(1)
                idx_block(1, 0)
            mask_block(0, 2)
            load_aq(0)
            for m in range(NTL):
                if m + 1 < NTL:
                    bisect_block(m + 1, 0)
                    idx_block(m + 1, 1)
                    bisect_block(m + 1, 1)
                attention(m)
                if m + 1 < NTL:
                    load_aq(m + 1)
                    mask_block(m + 1, 0)
                    idx_block(m + 1, 2)
                    mask_block(m + 1, 1)
                    bisect_block(m + 1, 2)
                    if m + 2 < NTL:
                        load_iq(m + 2)
                        idx_block(m + 2, 0)
                    mask_block(m + 1, 2)
            em.barrier()

    def stage_mla(ins, ob_d, cmT_d):
        with ExitStack() as es:
            def sb(name, shape, dt):
                return TL(es.enter_context(nc.sbuf_tensor(uq(name), list(shape), dt)).ap(), name)

            def ps(name):
                return TL(es.enter_context(nc.psum_tensor(uq(name), [128, 512], F32)).ap(), name)

            cmT = sb("cmT_sb", [128, 6, T], BF16)
            em.dma("sp", "ld_small", cmT.ap, cmT_d.ap, reads=[cmT_d.bufs[0]], writes=[cmT.b])
            mk_sb = sb("mk_sb", [96, 2, 4, NT], BF16)
            mv_sb = sb("mv_sb", [128, 2, NBK, 260], BF16)
            mq_sb = sb("mq_sb", [96, 4, T], BF16)
            pt = [sb("pt%d" % i, [128, T], BF16) for i in range(4)]
            pm = [sb("pm%d" % i, [128, T], BF16) for i in range(4)]
            accsb = [sb("accsb0", [65, T], F32), sb("accsb1", [65, T], F32)]
            neg1 = sb("neg1", [65, T], F32)
            em.op("pool", lambda e: e.memset(neg1.ap, -1.0), writes=[neg1.b])
            rrow = sb("rrow", [65, T], F32)
            bc_sb = sb("bc_sb", [64, T], F32)
            o_st = sb("o_st", [64, 8, T], BF16)
            psx = [ps("psx0"), ps("psx1"), ps("psx2")]
            accs = [ps("acc%d" % r) for r in range(4)]
            bc_ps = ps("bc_ps")
            xi = [0]
            LA = 2
            for hh in range(2):
                for j in range(2):
                    for r in range(4):
                        mkv = ins["mk"][4 * hh + r]
                        em.dma("sp", "ld_k", mk_sb.ap[:, j, r, :], mkv.ap[j], reads=mkv.bufs, writes=[mk_sb.b], nslots=4)
                    for (mvv, r0_, nr_) in ins["mvg"][hh]:
                        for i0 in range(0, NBK, 11):
                            em.dma("sp", "ld_k", mv_sb.ap[:, j, i0:i0 + 11, 65 * r0_:65 * (r0_ + nr_)],
                                   mvv.ap[j, i0 * 128:(i0 + 11) * 128, :].rearrange("(i p) c -> p i c", p=128),
                                   reads=mvv.bufs, writes=[mv_sb.b], nslots=4)
                for m in range(NTL):
                    n0 = m * T
                    nI = 3 * m + 3
                    nkb = 2 * nI
                    em.dma("sp", "ld_q", mq_sb.ap, ins["mq"].ap[4 * hh:4 * hh + 4, :, n0:n0 + T].rearrange("n p t -> p n t"),
                           reads=[ins["mq"].bufs[m]], writes=[mq_sb.b], nslots=4)
                    steps = [(kb, r) for kb in range(nkb) for r in range(4)]
                    held = {}
                    for si in range(len(steps) + LA):
                        if si < len(steps):
                            kb, r = steps[si]
                            j, i = kb // nI, kb % nI
                            zone = i >= nI - 3
                            p_ = psx[xi[0] % 3]
                            e_ = pt[xi[0] % 4]
                            m_ = pm[xi[0] % 4]
                            xi[0] += 1
                            mm(p_, p_.ap[:, 0:T], mk_sb.ap[:, j, r, i * 128:(i + 1) * 128], mq_sb.ap[:, r, :], [mk_sb.b, mq_sb.b], True, True)
                            act(e_, e_.ap, p_.ap[:, 0:T], AF.Exp, [p_.b], scale=float(96 ** -0.5))
                            if zone:
                                z = j * 3 + (i - (nI - 3))
                                tt("pool" if (xi[0] % 2) else "dve", m_, m_.ap, e_.ap, cmT.ap[:, z, :], ALU.mult, [e_.b, cmT.b])
                                held[si] = m_
                            else:
                                held[si] = e_
                        if si >= LA:
                            kb, r = steps[si - LA]
                            j, i = kb // nI, kb % nI
                            src = held.pop(si - LA)
                            mm(accs[r], accs[r].ap[0:65, 0:T], mv_sb.ap[:, j, i, 65 * r:65 * r + 65], src.ap, [mv_sb.b, src.b], kb == 0, kb == nkb - 1)
                    finalize_heads(accs, o_st, [r for r in range(4)], accsb, rrow, bc_ps, bc_sb, neg1)
                    em.dma("pool", "st_o", ob_d.ap[4 * hh:4 * hh + 4, :, n0:n0 + T].rearrange("n p t -> p n t"), o_st.ap[:, 0:4, :],
                           reads=[o_st.b], writes=[ob_d.bufs[m]])
                em.barrier()

    def stage_merge(h1_d, oa_d, ob_d, wg_d, wba_d, wbb_d, wout_d, lnp, li, hout_d):
        with ExitStack() as es:
            def sb(name, shape, dt):
                return TL(es.enter_context(nc.sbuf_tensor(uq(name), list(shape), dt)).ap(), name)

            def ps(name):
                return TL(es.enter_context(nc.psum_tensor(uq(name), [128, 512], F32)).ap(), name)

            wgb = sb("wgb", [128, 8, 2048], BF16)
            wbab = sb("wbab", [128, 4, D], BF16)
            wbbb = sb("wbbb", [128, 4, D], BF16)
            woutb = sb("woutb", [128, 8, D], BF16)
            stg = [sb("stg0", [128, 704], F32), sb("stg1", [128, 704], F32), sb("stg2", [128, 704], F32)]
            load_w(es, wg_d, wgb, 8, 2048, stg)
            load_w(es, wba_d, wbab, 4, D, stg)
            load_w(es, wbb_d, wbbb, 4, D, stg)
            load_w(es, wout_d, woutb, 8, D, stg)
            hin = sb("hin", [128, 8, T], F32)
            hb = sb("hb", [128, 8, T], BF16)
            oa = sb("oa", [128, 4, T], BF16)
            ob = sb("ob", [128, 4, T], BF16)
            mixed = sb("mixed", [128, 8, T], BF16)
            xr = [sb("xr%d" % j, [128, T], F32) for j in range(8)]
            sa = [sb("sa0", [128, T], F32), sb("sa1", [128, T], F32)]
            sbb = [sb("sbb0", [128, T], F32), sb("sbb1", [128, T], F32)]
            sq = [sb("sq0", [128, T], F32), sb("sq1", [128, T], F32)]
            tmp = [sb("tmp0", [128, T], F32), sb("tmp1", [128, T], F32)]
            st = {n: sb("ln_" + n, [128, T], F32) for n in ("mean", "msq", "var", "rstd", "bt")}
            pga, pgb, pba, pbb = ps("pga"), ps("pgb"), ps("pba"), ps("pbb")
            py = [ps("py0"), ps("py1")]
            ps1 = ps("ps1")
            ps2 = ps("ps2")
            for m in range(NTL):
                n0 = m * T
                em.dma("sp", "ld_h", hin.ap, h1_d.ap[:, n0:n0 + T].rearrange("(k p) t -> p k t", p=128), reads=[h1_d.bufs[m]], writes=[hin.b])
                for two in range(2):
                    pr_ = slice(64 * two, 64 * two + 64)
                    em.dma("sp", "ld_o", oa.ap[pr_, :, :], oa_d.ap[:, :, n0:n0 + T].rearrange("(q w) p t -> w p q t", w=2)[two],
                           reads=[oa_d.bufs[m]], writes=[oa.b], nslots=4)
                    em.dma("sp", "ld_o", ob.ap[pr_, :, :], ob_d.ap[:, :, n0:n0 + T].rearrange("(q w) p t -> w p q t", w=2)[two],
                           reads=[ob_d.bufs[m]], writes=[ob.b], nslots=4)
                copy("dve", hb, hb.ap[:, 0:4, :], hin.ap[:, 0:4, :], [hin.b])
                copy("pool", hb, hb.ap[:, 4:8, :], hin.ap[:, 4:8, :], [hin.b])
                for j in range(8):
                    cs = slice(j * 128, (j + 1) * 128)
                    for k in range(8):
                        mm(pga, pga.ap[:, 0:T], wgb.ap[:, k, cs], hb.ap[:, k, :], [wgb.b, hb.b], k == 0, k == 7)
                    for k in range(8):
                        mm(pgb, pgb.ap[:, 0:T], wgb.ap[:, k, 1024 + j * 128:1024 + (j + 1) * 128], hb.ap[:, k, :], [wgb.b, hb.b], k == 0, k == 7)
                    for k in range(4):
                        mm(pba, pba.ap[:, 0:T], wbab.ap[:, k, cs], oa.ap[:, k, :], [wbab.b, oa.b], k == 0, k == 3)
                    for k in range(4):
                        mm(pbb, pbb.ap[:, 0:T], wbbb.ap[:, k, cs], ob.ap[:, k, :], [wbbb.b, ob.b], k == 0, k == 3)
                    s1_, s2_ = sa[j % 2], sbb[j % 2]
                    act(s1_, s1_.ap, pga.ap[:, 0:T], AF.Sigmoid, [pga.b])
                    act(s2_, s2_.ap, pgb.ap[:, 0:T], AF.Sigmoid, [pgb.b])
                    tt("dve", s1_, s1_.ap, s1_.ap, pba.ap[:, 0:T], ALU.mult, [s1_.b, pba.b])
                    tt("dve", s2_, s2_.ap, s2_.ap, pbb.ap[:, 0:T], ALU.mult, [s2_.b, pbb.b])
                    tt("pool", mixed, mixed.ap[:, j, :], s1_.ap, s2_.ap, ALU.add, [s1_.b, s2_.b])
                for j in range(8):
                    y_ = py[j % 2]
                    for k in range(8):
                        mm(y_, y_.ap[:, 0:T], woutb.ap[:, k, j * 128:(j + 1) * 128], mixed.ap[:, k, :], [woutb.b, mixed.b], k == 0, k == 7)
                    act(xr[j], xr[j].ap, hin.ap[:, j, :], AF.Copy, [hin.b], scale=ALPHA)
                    stt(xr[j], xr[j].ap, y_.ap[:, 0:T], 1.0, xr[j].ap, ALU.mult, ALU.add, [y_.b, xr[j].b])
                layer_norm(xr, lnp, li, ps1, ps2, sq, tmp, st)
                for j in range(8):
                    em.dma("pool", "st_h", hout_d.ap[j * 128:(j + 1) * 128, n0:n0 + T], xr[j].ap,
                           reads=[xr[j].b], writes=[hout_d.bufs[m]], nslots=4)
            em.barrier()

    class V:
        def __init__(self, ap, bufs):
            self.ap = ap
            self.bufs = bufs

    def internal(name, shape, dt, n=NTL):
        return dram(name, shape, dt, "Internal", n)

    FMR = [224, 224, 192, 192, 192]
    TMC = [195, 195, 195, 65]
    MKLOC = [(0, 128), (1, 128), (2, 0), (2, 96), (3, 0), (3, 96), (4, 0), (4, 96)]
    MVLOC = [(0, 130), (1, 0), (1, 65), (1, 130), (2, 0), (2, 65), (2, 130), (3, 0)]

    def exchange(L, fm, tm):
        fa, ta = [None] * len(fm), [None] * len(tm)

        def xf(i):
            A = internal("fmA%d_%d" % (L, i), [2 * FMR[i], NT], BF16, 1)
            em.coll(fm[i].ap, A.ap, reads=fm[i].bufs, writes=A.bufs)
            fa[i] = V(A.ap.rearrange("(j r) t -> j r t", j=2), A.bufs)

        def xt(i):
            A = internal("tmA%d_%d" % (L, i), [2 * NT, TMC[i]], BF16, 1)
            em.coll(tm[i].ap, A.ap, reads=tm[i].bufs, writes=A.bufs)
            ta[i] = V(A.ap.rearrange("(j r) c -> j r c", j=2), A.bufs)

        xf(0)
        xf(1)
        xt(0)
        for i in range(2, len(fm)):
            xf(i)
        for i in range(1, len(tm)):
            xt(i)
        return {"ak": V(fa[0].ap[:, 0:128, :], fa[0].bufs), "ik": V(fa[1].ap[:, 0:128, :], fa[1].bufs),
                "mk": [V(fa[c].ap[:, r:r + 96, :], fa[c].bufs) for (c, r) in MKLOC],
                "av": V(ta[0].ap[:, :, 0:130], ta[0].bufs),
                "mvg": [[(V(ta[0].ap[:, :, 130:195], ta[0].bufs), 0, 1), (V(ta[1].ap[:, :, 0:195], ta[1].bufs), 1, 3)],
                        [(V(ta[2].ap[:, :, 0:195], ta[2].bufs), 0, 3), (V(ta[3].ap[:, :, 0:65], ta[3].bufs), 3, 1)]]}

    tabs = (din("c128", [128, NT], F32, 1), din("s128", [128, NT], F32, 1),
            din("c96", [96, NT], F32, 1), din("s96", [96, NT], F32, 1))
    idxm_d = din("idxm", [128, 3, 2, 3, 128], F32, 1)
    cmT_d = din("cmT", [128, 6, T], BF16, 1)
    hin = din("xT", [D, NT], F32)
    for L in range(DEPTH):
        lnp = small_in("lnp_%d" % L, [128, 48])
        nrm = small_in("nrm_%d" % L, [128, 3])
        w = dict(w13a=din("w13a_%d" % L, [D, 2 * DFF], F32, 1), w2a=din("w2a_%d" % L, [DFF, D], F32, 1),
                 wA=din("wA_%d" % L, [D, NA], F32, 1), wuq=din("wuq_%d" % L, [256, 1536], F32, 1),
                 wuk=din("wuk_%d" % L, [128, 768], F32, 1), wuv=din("wuv_%d" % L, [128, 512], F32, 1),
                 wg=din("wg_%d" % L, [D, 2048], F32, 1), wba=din("wba_%d" % L, [512, D], F32, 1),
                 wbb=din("wbb_%d" % L, [512, D], F32, 1), wout=din("wout_%d" % L, [D, D], F32, 1),
                 w13b=din("w13b_%d" % L, [D, 2 * DFF], F32, 1), w2b=din("w2b_%d" % L, [DFF, D], F32, 1))
        h1 = internal("h1_%d" % L, [D, NT], F32)
        stage_ffn(hin, h1, w["w13a"], w["w2a"], lnp, 0)
        fm = [internal("fm%d_%d" % (L, i), [r_, NT], BF16) for i, r_ in enumerate(FMR)]
        tm = [internal("tm%d_%d" % (L, i), [NT, c_], BF16) for i, c_ in enumerate(TMC)]
        q = {"aq": internal("aq%d" % L, [4, 128, NT], BF16), "iq": internal("iq%d" % L, [4, 128, NT], BF16),
             "mq": internal("mq%d" % L, [8, 96, NT], BF16)}
        iw = internal("iw%d" % L, [128, NBK, 8], F32)
        outs = dict(q)
        outs.update({"ak": V(fm[0].ap[0:128, :], fm[0].bufs), "ik": V(fm[1].ap[0:128, :], fm[1].bufs),
                     "mk": [V(fm[c].ap[r:r + 96, :], fm[c].bufs) for (c, r) in MKLOC],
                     "av": V(tm[0].ap[:, 0:130], tm[0].bufs),
                     "mvc": [(V(tm[0].ap[:, 130:195], tm[0].bufs), 0, 1), (V(tm[1].ap, tm[1].bufs), 1, 4),
                             (V(tm[2].ap, tm[2].bufs), 4, 7), (V(tm[3].ap, tm[3].bufs), 7, 8)],
                     "iw": iw})
        stage_proj(h1, w["wA"], w["wuq"], w["wuk"], w["wuv"], nrm, tabs, outs)
        ins = dict(q)
        ins.update(exchange(L, fm, tm))
        oa = internal("oa%d" % L, [8, 64, NT], BF16)
        ob = internal("ob%d" % L, [8, 64, NT], BF16)
        h2 = internal("h2_%d" % L, [D, NT], F32)
        stage_dsa(ins, oa, idxm_d, iw)
        stage_mla(ins, ob, cmT_d)
        stage_merge(h1, oa, ob, w["wg"], w["wba"], w["wbb"], w["wout"], lnp, 1, h2)
        h3 = dout("outT", [D, NT], F32) if L == DEPTH - 1 else internal("h3_%d" % L, [D, NT], F32)
        stage_ffn(h2, h3, w["w13b"], w["w2b"], lnp, 2)
        hin = h3
    em.barrier()
    gs.close()
    return nc, em


def _rope_perm64():
    p = np.arange(64)
    p[0:8] = np.arange(8, 16)
    p[8:16] = np.arange(0, 8)
    return p


def _prep_weights(inp, L):
    f = np.float32
    w_in = np.asarray(inp["w_in"][L], f)
    perm = _rope_perm64()
    a_q = w_in[:, 0:512].reshape(D, 8, 64)
    a_k = w_in[:, 512:640].reshape(D, 2, 64)
    a_v = w_in[:, 640:768]
    i_q = w_in[:, 768:1280].reshape(D, 8, 64)
    i_k = w_in[:, 1280:1344]
    i_w = w_in[:, 1344:1352]
    c_q = w_in[:, 1352:1608]
    c_kv = w_in[:, 1608:1736]
    k_r = w_in[:, 1736:1768]
    gates = w_in[:, 1768:3816]
    aq_pair = np.stack([np.concatenate([a_q[:, r], a_q[:, 4 + r]], 1) for r in range(4)], 1)
    aq_pairB = np.stack([np.concatenate([a_q[:, r][:, perm], a_q[:, 4 + r][:, perm]], 1) for r in range(4)], 1)
    akA = a_k.reshape(D, 128)
    akB = a_k[:, :, perm].reshape(D, 128)
    iqA = i_q.reshape(D, 512)
    iqB = i_q[:, :, perm].reshape(D, 512)
    ikA = np.concatenate([i_k, i_k], 1)
    ikB = np.concatenate([i_k[:, perm], i_k[:, perm]], 1)
    krA = np.zeros((D, 96), f)
    krA[:, 64:96] = k_r
    krB = np.zeros((D, 96), f)
    krB[:, 64:80] = k_r[:, 16:32]
    krB[:, 80:96] = k_r[:, 0:16]
    wA = np.concatenate([aq_pair.reshape(D, 512), aq_pairB.reshape(D, 512), akA, akB, a_v, iqA, iqB, ikA, ikB,
                         i_w, c_q, c_kv, krA, krB], 1)
    assert wA.shape[1] == NA, wA.shape
    w_uq = np.asarray(inp["mla_w_uq"][L], f).reshape(256, 8, 96)
    uqB = w_uq.copy()
    uqB[:, :, 64:80] = w_uq[:, :, 80:96]
    uqB[:, :, 80:96] = w_uq[:, :, 64:80]
    wuq = np.concatenate([w_uq.reshape(256, 768), uqB.reshape(256, 768)], 1)
    w_ukv = np.asarray(inp["mla_w_ukv"][L], f).reshape(128, 8, 128)
    wuk = np.zeros((128, 8, 96), f)
    wuk[:, :, 0:64] = w_ukv[:, :, 0:64]
    wuv = np.ascontiguousarray(w_ukv[:, :, 64:128]).reshape(128, 512)
    ln_g = np.asarray(inp["ln_g"][L], f)
    ln_b = np.asarray(inp["ln_b"][L], f)
    lnp = np.zeros((128, 48), f)
    for li in range(3):
        lnp[:, li * 16:li * 16 + 8] = ln_g[li].reshape(8, 128).T
        lnp[:, li * 16 + 8:li * 16 + 16] = ln_b[li].reshape(8, 128).T
    nrm = np.zeros((128, 3), f)
    nrm[:, 0:2] = np.asarray(inp["mla_q_norm"][L], f).reshape(2, 128).T
    nrm[:, 2] = np.asarray(inp["mla_kv_norm"][L], f)
    c = np.ascontiguousarray
    return {
        "w13a_%d" % L: c(np.asarray(inp["ffn1_w13"][L], f)), "w2a_%d" % L: c(np.asarray(inp["ffn1_w2"][L], f)),
        "wA_%d" % L: c(wA), "wuq_%d" % L: c(wuq), "wuk_%d" % L: c(wuk.reshape(128, 768)), "wuv_%d" % L: c(wuv),
        "wg_%d" % L: c(gates), "wba_%d" % L: c(np.asarray(inp["w_branch_a"][L], f)),
        "wbb_%d" % L: c(np.asarray(inp["w_branch_b"][L], f)), "wout_%d" % L: c(np.asarray(inp["w_out"][L], f)),
        "w13b_%d" % L: c(np.asarray(inp["ffn2_w13"][L], f)), "w2b_%d" % L: c(np.asarray(inp["ffn2_w2"][L], f)),
        "lnp_%d" % L: lnp, "nrm_%d" % L: nrm,
    }


def _tables(j):
    n = np.arange(NT)
    pos = ((2 * (n // 128) + j) * 128 + n % 128).astype(np.float32)
    inv16 = (THETA ** (-(np.arange(8, dtype=np.float32) * 2.0 / 16))).astype(np.float32)
    inv32 = (THETA ** (-(np.arange(16, dtype=np.float32) * 2.0 / 32))).astype(np.float32)
    a16 = pos[None, :] * inv16[:, None]
    a32 = pos[None, :] * inv32[:, None]
    c64 = np.ones((64, NT), np.float32)
    s64 = np.zeros((64, NT), np.float32)
    c64[0:8] = np.cos(a16)
    c64[8:16] = np.cos(a16)
    s64[0:8] = -np.sin(a16)
    s64[8:16] = np.sin(a16)
    c96 = np.ones((96, NT), np.float32)
    s96 = np.zeros((96, NT), np.float32)
    c96[64:80] = np.cos(a32)
    c96[80:96] = np.cos(a32)
    s96[64:80] = -np.sin(a32)
    s96[80:96] = np.sin(a32)
    return {"c128": np.concatenate([c64, c64], 0), "s128": np.concatenate([s64, s64], 0), "c96": c96, "s96": s96}


def _masks(j):
    idxm = np.zeros((128, 3, 2, 3, 128), np.float32)
    cmT = np.zeros((128, 6, T), np.float32)
    t = np.arange(128)
    for c in range(3):
        qbz = 2 * c + j
        for jp in range(2):
            for cp in range(3):
                kbz = 2 * cp + jp
                if kbz < qbz:
                    vis = np.ones((128, 128), bool)
                elif kbz == qbz:
                    vis = t[None, :] <= t[:, None]
                else:
                    vis = np.zeros((128, 128), bool)
                idxm[:, c, jp, cp, :] = np.where(vis, 0.0, NEGM)
                cmT[:, jp * 3 + cp, c * 128:(c + 1) * 128] = vis.T.astype(np.float32)
    return {"idxm": idxm, "cmT": cmT.astype(ml_dtypes.bfloat16)}


_PROG = []


def _prog():
    if not _PROG:
        _PROG.append(build_program()[0])
    return _PROG[0]


def kernel(x, meta_tokens, ln_g, ln_b, ffn1_w13, ffn1_w2, w_in, mla_q_norm, mla_kv_norm,
           mla_w_uq, mla_w_ukv, w_branch_a, w_branch_b, w_out, ffn2_w13, ffn2_w2):
    inp = dict(x=x, meta_tokens=meta_tokens, ln_g=ln_g, ln_b=ln_b, ffn1_w13=ffn1_w13, ffn1_w2=ffn1_w2, w_in=w_in,
               mla_q_norm=mla_q_norm, mla_kv_norm=mla_kv_norm, mla_w_uq=mla_w_uq, mla_w_ukv=mla_w_ukv,
               w_branch_a=w_branch_a, w_branch_b=w_branch_b, w_out=w_out, ffn2_w13=ffn2_w13, ffn2_w2=ffn2_w2)
    x = np.asarray(x, np.float32)
    meta = np.asarray(meta_tokens, np.float32)
    W = {}
    for L in range(DEPTH):
        W.update(_prep_weights(inp, L))
    tabs = [_tables(j) for j in range(2)]
    msk = [_masks(j) for j in range(2)]
    maps = []
    for c in range(8):
        b, j = c // 2, c % 2
        h0 = np.zeros((NBLK * 128, D), np.float32)
        h0[0:N_META] = meta
        h0[N_META:N_META + SEQ] = x[b]
        mine = h0.reshape(NBLK, 128, D)[j::2].reshape(NT, D)
        m = dict(W)
        m.update(tabs[j])
        m.update(msk[j])
        m["xT"] = np.ascontiguousarray(mine.T)
        maps.append(m)
    res = run_bass_kernel_spmd(_prog(), maps, core_ids=list(range(8))).results
    out = np.zeros((BATCH, SEQ, D), np.float32)
    for b in range(BATCH):
        full = np.zeros((NBLK, 128, D), np.float32)
        for j in range(2):
            full[j::2] = np.asarray(res[2 * b + j]["outT"]).T.reshape(NBK, 128, D)
        out[b] = full.reshape(NBLK * 128, D)[N_META:N_META + SEQ]
    return out
```

```python
import numpy as np
import ml_dtypes
from contextlib import ExitStack
import concourse.bass as bass
import concourse.mybir as mybir
from concourse.bass_utils import run_bass_kernel_spmd

F32 = mybir.dt.float32
BF16 = mybir.dt.bfloat16
U8 = mybir.dt.uint8
FP8 = mybir.dt.float8e4
AF = mybir.ActivationFunctionType
ALU = mybir.AluOpType

P = 128
T = 384
NTL = 11
NBK = 33
NT = NBK * 128
NBLK = 66
D = 1024
DFF = 2816
N_META = 16
SEQ = 8192
BATCH = 4
DEPTH = 2
ALPHA = float((2 * DEPTH) ** 0.25)
THETA = 500000.0
NEGM = -1.0e4
NA = 3272
IDX_SCALE = float(64 ** -0.5 * 8 ** -0.5)
BIS_R = 16.0
BIS_IT = 17


class Buf:
    __slots__ = ("w", "r", "name")

    def __init__(self, name=""):
        self.w = {}
        self.r = {}
        self.name = name


class TL:
    __slots__ = ("ap", "b")

    def __init__(self, ap, name=""):
        self.ap = ap
        self.b = Buf(name)


class Em:
    def __init__(self, nc):
        self.nc = nc
        self.engs = {"pe": nc.tensor, "act": nc.scalar, "dve": nc.vector, "pool": nc.gpsimd, "sp": nc.sync}
        self.sems = {}
        self.cnt = {}
        self.seen = {k: {} for k in self.engs}
        self.grp = {}
        for k in self.engs:
            self.sems[k] = nc.alloc_semaphore("s_" + k)
            self.cnt[k] = 0
        self.nwait = 0
        self.nins = 0

    def _wait(self, eng, k, c):
        if self.seen[eng].get(k, 0) < c:
            self.engs[eng].wait_ge(self.sems[k], c)
            self.seen[eng][k] = c
            self.nwait += 1

    def _deps(self, eng, reads, writes):
        need = {}
        for b in reads:
            for k, c in b.w.items():
                if need.get(k, 0) < c:
                    need[k] = c
        for b in writes:
            for d in (b.w, b.r):
                for k, c in d.items():
                    if need.get(k, 0) < c:
                        need[k] = c
        for k, c in need.items():
            if k == eng and eng == "pe":
                continue
            self._wait(eng, k, c)

    def op(self, eng, fn, reads=(), writes=()):
        self._deps(eng, reads, writes)
        ins = fn(self.engs[eng])
        self.cnt[eng] += 1
        self.nins += 1
        ins.then_inc(self.sems[eng], 1)
        c = self.cnt[eng]
        for b in writes:
            b.w[eng] = c
            b.r = {}
        for b in reads:
            if b.r.get(eng, 0) < c:
                b.r[eng] = c
        return ins

    def dma(self, q, grp, out, in_, reads=(), writes=(), nslots=2):
        st = self.grp.setdefault(grp, [0, nslots])
        slot = st[0] % st[1]
        st[0] += 1
        key = (grp, slot)
        if key not in self.sems:
            self.sems[key] = self.nc.alloc_semaphore("d_%s_%d" % (grp, slot))
            self.cnt[key] = 0
        if self.cnt[key] > 0:
            self._wait(q, key, self.cnt[key])
        self._deps(q, reads, writes)
        ins = self.engs[q].dma_start(out=out, in_=in_)
        self.cnt[key] += 16
        self.nins += 1
        ins.then_inc(self.sems[key], 16)
        c = self.cnt[key]
        for b in writes:
            b.w[key] = c
            b.r = {}
        for b in reads:
            if b.r.get(key, 0) < c:
                b.r[key] = c
        return ins

    def coll(self, in_ap, out_ap, reads=(), writes=()):
        q, key = "pool", "cc"
        if key not in self.sems:
            self.sems[key] = self.nc.alloc_semaphore("s_cc")
            self.cnt[key] = 0
        if self.cnt[key] > 0:
            self._wait(q, key, self.cnt[key])
        self._deps(q, reads, writes)
        ins = self.engs[q].collective_compute("AllGather", ALU.bypass, replica_groups=[[0, 1], [2, 3], [4, 5], [6, 7]],
                                              ins=[in_ap], outs=[out_ap])
        self.cnt[key] += 1
        self.nins += 1
        ins.then_inc(self.sems[key])
        c = self.cnt[key]
        for b in writes:
            b.w[key] = c
            b.r = {}
        for b in reads:
            if b.r.get(key, 0) < c:
                b.r[key] = c
        return ins

    def barrier(self, engines=None):
        for e in (engines or list(self.engs)):
            for k, c in self.cnt.items():
                if k != e and c > 0:
                    self._wait(e, k, c)


class DT:
    def __init__(self, ap, n=NTL, name=""):
        self.ap = ap
        self.bufs = [Buf(name + str(i)) for i in range(n)]


def build_program():
    nc = bass.Bass("TRN2", target_bir_lowering=False)
    em = Em(nc)
    dts = {}

    def dram(name, shape, dt, kind, n=NTL):
        t = DT(nc.dram_tensor(name, list(shape), dt, kind=kind).ap(), n=n, name=name)
        dts[name] = t
        return t

    def din(name, shape, dt, n=NTL):
        return dram(name, shape, dt, "ExternalInput", n)

    def dout(name, shape, dt, n=NTL):
        return dram(name, shape, dt, "ExternalOutput", n)

    gs = ExitStack()
    uid = [0]

    def uq(name):
        uid[0] += 1
        return "%s__u%d" % (name, uid[0])

    def galloc(name, shape, dt):
        return TL(gs.enter_context(nc.sbuf_tensor(name, list(shape), dt)).ap(), name)

    ones_f = galloc("ones_f", [128, 128], F32)
    ident_b = galloc("ident_b", [128, 128], BF16)
    eps5 = galloc("eps5", [128, 1], F32)
    eps6 = galloc("eps6", [128, 1], F32)
    m240 = galloc("m240", [128, 1], F32)
    em.op("pool", lambda e: e.memset(ones_f.ap, 1.0), writes=[ones_f.b])
    em.op("pool", lambda e: e.memset(eps5.ap, 1e-5), writes=[eps5.b])
    em.op("pool", lambda e: e.memset(eps6.ap, 1e-6), writes=[eps6.b])
    em.op("pool", lambda e: e.memset(m240.ap, -240.0), writes=[m240.b])
    em.op("pool", lambda e: e.memset(ident_b.ap, 0.0), writes=[ident_b.b])
    em.op("pool", lambda e: e.affine_select(out=ident_b.ap, in_=ident_b.ap, pattern=[[-1, 128]],
                                            compare_op=ALU.not_equal, fill=1.0, base=0, channel_multiplier=1),
          reads=[ident_b.b], writes=[ident_b.b])

    def small_in(name, shape):
        d = din(name, shape, F32, n=1)
        t = galloc(name + "_sb", shape, F32)
        em.dma("sp", "ld_small", t.ap, d.ap, reads=[d.bufs[0]], writes=[t.b])
        return t

    def mm(out_tl, out_ap, lhsT, rhs, reads, start, stop):
        em.op("pe", lambda e: e.matmul(out_ap, lhsT=lhsT, rhs=rhs, start=start, stop=stop),
              reads=reads, writes=[out_tl.b])

    def act(out_tl, out_ap, in_ap, func, reads, scale=None, bias=None):
        kw = {}
        if scale is not None:
            kw["scale"] = scale
        if bias is not None:
            kw["bias"] = bias
        em.op("act", lambda e: e.activation(out=out_ap, in_=in_ap, func=func, **kw), reads=reads, writes=[out_tl.b])

    def tt(eng, out_tl, out_ap, in0, in1, op, reads):
        em.op(eng, lambda e: e.tensor_tensor(out=out_ap, in0=in0, in1=in1, op=op), reads=reads, writes=[out_tl.b])

    def ts(eng, out_tl, out_ap, in0, s1, op0, reads, s2=None, op1=None, accum=None, extra_w=()):
        kw = {}
        if op1 is not None:
            kw["op1"] = op1
        if accum is not None:
            kw["accum_out"] = accum
        em.op(eng, lambda e: e.tensor_scalar(out=out_ap, in0=in0, scalar1=s1, scalar2=s2, op0=op0, **kw),
              reads=reads, writes=[out_tl.b] + list(extra_w))

    def stt(out_tl, out_ap, in0, scalar, in1, op0, op1, reads):
        em.op("dve", lambda e: e.scalar_tensor_tensor(out=out_ap, in0=in0, scalar=scalar, in1=in1, op0=op0, op1=op1),
              reads=reads, writes=[out_tl.b])

    def copy(eng, out_tl, out_ap, in_ap, reads):
        if eng == "act":
            act(out_tl, out_ap, in_ap, AF.Copy, reads)
        else:
            em.op(eng, lambda e: e.tensor_copy(out=out_ap, in_=in_ap), reads=reads, writes=[out_tl.b])

    cast_rr = [0]

    def load_w(es_alloc, d, dst, kc, ncols, stg, kp=128):
        W = 704
        for k in range(kc):
            for c0 in range(0, ncols, W):
                cw = min(W, ncols - c0)
                s = stg[cast_rr[0] % len(stg)]
                eng = ("dve", "act")[cast_rr[0] % 2]
                cast_rr[0] += 1
                em.dma("sp", "ld_w", s.ap[0:kp, 0:cw], d.ap[k * kp:(k + 1) * kp, c0:c0 + cw],
                       reads=[d.bufs[0]], writes=[s.b], nslots=3)
                copy(eng, dst, dst.ap[0:kp, k, c0:c0 + cw], s.ap[0:kp, 0:cw], reads=[s.b])

    def ln_stats(xr, ps1, ps2, sq):
        for j in range(8):
            mm(ps1, ps1.ap[:, 0:T], ones_f.ap, xr[j].ap, [ones_f.b, xr[j].b], j == 0, j == 7)
        for j in range(8):
            s = sq[j % 2]
            act(s, s.ap, xr[j].ap, AF.Square, [xr[j].b])
            mm(ps2, ps2.ap[:, 0:T], ones_f.ap, s.ap, [ones_f.b, s.b], j == 0, j == 7)

    def ln_apply(xr, lnp, li, ps1, ps2, tmp, st):
        mean, msq, var, rstd, bt = st["mean"], st["msq"], st["var"], st["rstd"], st["bt"]
        act(mean, mean.ap, ps1.ap[:, 0:T], AF.Copy, [ps1.b], scale=1.0 / D)
        act(msq, msq.ap, mean.ap, AF.Square, [mean.b])
        stt(var, var.ap, ps2.ap[:, 0:T], 1.0 / D, msq.ap, ALU.mult, ALU.subtract, [ps2.b, msq.b])
        act(var, var.ap, var.ap, AF.Sqrt, [var.b, eps5.b], bias=eps5.ap[:, 0:1])
        em.op("dve", lambda e: e.reciprocal(out=rstd.ap, in_=var.ap), reads=[var.b], writes=[rstd.b])
        stt(bt, bt.ap, mean.ap, -1.0, rstd.ap, ALU.mult, ALU.mult, [mean.b, rstd.b])
        for j in range(8):
            t = tmp[j % 2]
            tt("dve", t, t.ap, xr[j].ap, rstd.ap, ALU.mult, [xr[j].b, rstd.b])
            tt("pool", t, t.ap, t.ap, bt.ap, ALU.add, [t.b, bt.b])
            act(xr[j], xr[j].ap, t.ap, AF.Identity, [t.b, lnp.b],
                scale=lnp.ap[:, li * 16 + j:li * 16 + j + 1], bias=lnp.ap[:, li * 16 + 8 + j:li * 16 + 8 + j + 1])

    def layer_norm(xr, lnp, li, ps1, ps2, sq, tmp, st):
        ln_stats(xr, ps1, ps2, sq)
        ln_apply(xr, lnp, li, ps1, ps2, tmp, st)

    def stage_ffn(hin_d, hout_d, w13_d, w2_d, lnp, li):
        with ExitStack() as es:
            def sb(name, shape, dt):
                return TL(es.enter_context(nc.sbuf_tensor(uq(name), list(shape), dt)).ap(), name)

            def ps(name):
                return TL(es.enter_context(nc.psum_tensor(uq(name), [128, 512], F32)).ap(), name)

            w13b = sb("w13b", [128, 8, 2 * DFF], BF16)
            w2b = sb("w2b", [128, 22, D], BF16)
            stg = [sb("stg0", [128, 704], F32), sb("stg1", [128, 704], F32), sb("stg2", [128, 704], F32)]
            w13p = [Buf("w13p%d" % p_) for p_ in range(8)]
            for p_ in (0, 4, 1, 5, 2, 6, 3, 7):
                for k in range(8):
                    s_ = stg[cast_rr[0] % len(stg)]
                    eng = ("dve", "act")[cast_rr[0] % 2]
                    cast_rr[0] += 1
                    em.dma("sp", "ld_w", s_.ap[:, 0:704], w13_d.ap[k * 128:(k + 1) * 128, p_ * 704:(p_ + 1) * 704],
                           reads=[w13_d.bufs[0]], writes=[s_.b], nslots=3)
                    em.op(eng, (lambda e: e.activation(out=w13b.ap[:, k, p_ * 704:(p_ + 1) * 704], in_=s_.ap[:, 0:704], func=AF.Copy)) if eng == "act"
                          else (lambda e: e.tensor_copy(out=w13b.ap[:, k, p_ * 704:(p_ + 1) * 704], in_=s_.ap[:, 0:704])),
                          reads=[s_.b], writes=[w13p[p_]])

            def w13r(col0):
                a_, b_ = col0 // 704, (col0 + 127) // 704
                return [w13p[a_]] if a_ == b_ else [w13p[a_], w13p[b_]]

            load_w(es, w2_d, w2b, 22, D, stg)
            X = [[sb("x%d_%d" % (b_, j), [128, T], F32) for j in range(8)] for b_ in range(2)]
            hb = sb("hb", [128, 8, T], BF16)
            G = sb("G", [128, 22, T], BF16)
            sg = [sb("sg0", [128, T], F32), sb("sg1", [128, T], F32)]
            sq = [sb("sq0", [128, T], F32), sb("sq1", [128, T], F32)]
            st = {n: sb("ln_" + n, [128, T], F32) for n in ("mean", "var", "bt")}
            st["msq"] = st["bt"]
            st["rstd"] = st["var"]
            tmpb = [sb("tmpb0", [128, T], F32), sb("tmpb1", [128, T], F32)]
            pg = [ps("pg0"), ps("pg1")]
            pu = [ps("pu0"), ps("pu1")]
            py = [ps("py0"), ps("py1")]
            ps1 = ps("ps1")
            ps2 = ps("ps2")

            def ph_a_hb(m):
                n0 = m * T
                em.dma("pool", "ld_hb", hb.ap, hin_d.ap[:, n0:n0 + T].rearrange("(k p) t -> p k t", p=128),
                       reads=[hin_d.bufs[m]], writes=[hb.b], nslots=2)

            def ph_a_x(m):
                n0 = m * T
                xb = X[m % 2]
                for j in range(8):
                    em.dma("sp", "ld_h", xb[j].ap, hin_d.ap[j * 128:(j + 1) * 128, n0:n0 + T],
                           reads=[hin_d.bufs[m]], writes=[xb[j].b], nslots=4)

            def ph_b(m):
                for c in range(22):
                    g_, u_ = pg[c % 2], pu[c % 2]
                    for k in range(8):
                        mm(g_, g_.ap[:, 0:T], w13b.ap[:, k, c * 128:(c + 1) * 128], hb.ap[:, k, :], w13r(c * 128) + [hb.b], k == 0, k == 7)
                    for k in range(8):
                        mm(u_, u_.ap[:, 0:T], w13b.ap[:, k, DFF + c * 128:DFF + (c + 1) * 128], hb.ap[:, k, :], w13r(DFF + c * 128) + [hb.b], k == 0, k == 7)
                    s_ = sg[c % 2]
                    act(s_, s_.ap, g_.ap[:, 0:T], AF.Silu, [g_.b])
                    tt("dve", G, G.ap[:, c, :], s_.ap, u_.ap[:, 0:T], ALU.mult, [s_.b, u_.b])

            def ph_c(m):
                xb = X[m % 2]
                for j in range(8):
                    y_ = py[j % 2]
                    for k in range(22):
                        mm(y_, y_.ap[:, 0:T], w2b.ap[:, k, j * 128:(j + 1) * 128], G.ap[:, k, :], [w2b.b, G.b], k == 0, k == 21)
                    act(xb[j], xb[j].ap, xb[j].ap, AF.Copy, [xb[j].b], scale=ALPHA)
                    stt(xb[j], xb[j].ap, y_.ap[:, 0:T], 0.5, xb[j].ap, ALU.mult, ALU.add, [y_.b, xb[j].b])

            def ph_d1(m):
                ln_stats(X[m % 2], ps1, ps2, sq)

            def ph_d2(m):
                n0 = m * T
                xb = X[m % 2]
                ln_apply(xb, lnp, li, ps1, ps2, tmpb, st)
                for j in range(8):
                    em.dma("sp", "st_h", hout_d.ap[j * 128:(j + 1) * 128, n0:n0 + T], xb[j].ap,
                           reads=[xb[j].b], writes=[hout_d.bufs[m]], nslots=4)

            ph_a_hb(0)
            ph_a_x(0)
            ph_b(0)
            for m in range(NTL):
                if m + 1 < NTL:
                    ph_a_hb(m + 1)
                ph_c(m)
                if m >= 1:
                    ph_d2(m - 1)
                if m + 1 < NTL:
                    ph_a_x(m + 1)
                    ph_b(m + 1)
                ph_d1(m)
            ph_d2(NTL - 1)
            em.barrier()

    def stage_proj(h1_d, wA_d, wuq_d, wuk_d, wuv_d, nrm, tabs, outs):
        c128_d, s128_d, c96_d, s96_d = tabs
        with ExitStack() as es:
            def sb(name, shape, dt):
                return TL(es.enter_context(nc.sbuf_tensor(uq(name), list(shape), dt)).ap(), name)

            def ps(name):
                return TL(es.enter_context(nc.psum_tensor(uq(name), [128, 512], F32)).ap(), name)

            wAb = sb("wAb", [128, 8, NA], BF16)
            wuqb = sb("wuqb", [128, 2, 1536], BF16)
            wukb = sb("wukb", [128, 1, 768], BF16)
            wuvb = sb("wuvb", [128, 1, 512], BF16)
            stg = [sb("stg0", [128, 704], F32), sb("stg1", [128, 704], F32), sb("stg2", [128, 704], F32)]
            load_w(es, wA_d, wAb, 8, NA, stg)
            load_w(es, wuq_d, wuqb, 2, 1536, stg)
            load_w(es, wuk_d, wukb, 1, 768, stg)
            load_w(es, wuv_d, wuvb, 1, 512, stg)
            hbs = [sb("hb0", [128, 8, T], BF16), sb("hb1", [128, 8, T], BF16)]
            cur = {}
            c128 = sb("c128", [128, T], F32)
            s128 = sb("s128", [128, T], F32)
            c96 = sb("c96", [96, T], F32)
            s96 = sb("s96", [96, T], F32)
            t1 = [sb("t1_0", [128, T], F32), sb("t1_1", [128, T], F32)]
            t2 = [sb("t2_0", [128, T], F32), sb("t2_1", [128, T], F32)]
            aq_st = sb("aq_st", [128, 4, T], BF16)
            iq_st = sb("iq_st", [128, 4, T], BF16)
            ak_st = sb("ak_st", [128, T], BF16)
            ik_st = sb("ik_st", [128, T], BF16)
            mq_st = sb("mq_st", [96, 8, T], BF16)
            mk_st = sb("mk_st", [96, 8, T], BF16)
            av_st = sb("av_st", [128, 3, 130], BF16)
            mv_st = sb("mv_st", [128, 3, 8, 65], BF16)
            iw_st = sb("iw_st", [128, 3, 8], F32)
            cq_sb = sb("cq_sb", [128, 2, T], F32)
            ckv_sb = sb("ckv_sb", [128, T], F32)
            sqs = [sb("sqs0", [128, T], F32), sb("sqs1", [128, T], F32)]
            rs = sb("rs", [128, T], F32)
            cqn = sb("cqn", [128, 2, T], BF16)
            ckvn = sb("ckvn", [128, T], BF16)
            bs = sb("bs", [96, T], F32)
            pA = [ps("pA0"), ps("pA1")]
            pB = [ps("pB0"), ps("pB1")]
            pC = [ps("pC0"), ps("pC1")]
            pS = ps("pS")
            em.op("pool", lambda e: e.memset(av_st.ap, 1.0), writes=[av_st.b])
            em.op("pool", lambda e: e.memset(mv_st.ap, 1.0), writes=[mv_st.b])
            rr = [0]

            def proj(out_ps, col0, M):
                hb = cur["hb"]
                for k in range(8):
                    mm(out_ps, out_ps.ap[0:M, 0:T], wAb.ap[:, k, col0:col0 + M], hb.ap[:, k, :], [wAb.b, hb.b], k == 0, k == 7)

            def roped(colA, colB, dst_tl, dst_ap):
                i = rr[0] % 2
                rr[0] += 1
                a_, b_ = pA[i], pB[i]
                proj(a_, colA, 128)
                proj(b_, colB, 128)
                tt("dve", t1[i], t1[i].ap, a_.ap[:, 0:T], c128.ap, ALU.mult, [a_.b, c128.b])
                tt("dve", t2[i], t2[i].ap, b_.ap[:, 0:T], s128.ap, ALU.mult, [b_.b, s128.b])
                tt("pool", dst_tl, dst_ap, t1[i].ap, t2[i].ap, ALU.add, [t1[i].b, t2[i].b])

            for m in range(NTL):
                n0 = m * T
                if m == 0:
                    em.dma("pool", "ld_hb", hbs[0].ap, h1_d.ap[:, 0:T].rearrange("(k p) t -> p k t", p=128),
                           reads=[h1_d.bufs[0]], writes=[hbs[0].b], nslots=2)
                if m + 1 < NTL:
                    em.dma("pool", "ld_hb", hbs[(m + 1) % 2].ap, h1_d.ap[:, n0 + T:n0 + 2 * T].rearrange("(k p) t -> p k t", p=128),
                           reads=[h1_d.bufs[m + 1]], writes=[hbs[(m + 1) % 2].b], nslots=2)
                hb = hbs[m % 2]
                cur["hb"] = hb
                for (tl_, d_) in ((c128, c128_d), (s128, s128_d), (c96, c96_d), (s96, s96_d)):
                    em.dma("sp", "ld_tab", tl_.ap, d_.ap[:, n0:n0 + T], reads=[d_.bufs[0]], writes=[tl_.b], nslots=4)
                for r in range(4):
                    roped(r * 128, 512 + r * 128, aq_st, aq_st.ap[:, r, :])
                roped(1024, 1152, ak_st, ak_st.ap)
                for r in range(4):
                    roped(1408 + r * 128, 1920 + r * 128, iq_st, iq_st.ap[:, r, :])
                roped(2432, 2560, ik_st, ik_st.ap)
                for c in range(3):
                    p_ = pC[c % 2]
                    for k in range(8):
                        mm(p_, p_.ap[:, 0:128], hb.ap[:, k, c * 128:(c + 1) * 128], wAb.ap[:, k, 1280:1408], [wAb.b, hb.b], k == 0, k == 7)
                    copy("act", av_st, av_st.ap[:, c, 0:64], p_.ap[:, 0:64], [p_.b])
                    copy("act", av_st, av_st.ap[:, c, 65:129], p_.ap[:, 64:128], [p_.b])
                    p2 = pC[(c + 1) % 2]
                    for k in range(8):
                        mm(p2, p2.ap[:, 0:8], hb.ap[:, k, c * 128:(c + 1) * 128], wAb.ap[:, k, 2688:2696], [wAb.b, hb.b], k == 0, k == 7)
                    act(iw_st, iw_st.ap[:, c, :], p2.ap[:, 0:8], AF.Copy, [p2.b], scale=IDX_SCALE)
                for c in range(2):
                    p_ = pC[c % 2]
                    proj(p_, 2696 + c * 128, 128)
                    copy("act", cq_sb, cq_sb.ap[:, c, :], p_.ap[:, 0:T], [p_.b])
                    act(sqs[c], sqs[c].ap, p_.ap[:, 0:T], AF.Square, [p_.b])
                for c in range(2):
                    mm(pS, pS.ap[:, 0:T], ones_f.ap, sqs[c].ap, [ones_f.b, sqs[c].b], c == 0, c == 1)
                act(rs, rs.ap, pS.ap[:, 0:T], AF.Sqrt, [pS.b, eps6.b], scale=1.0 / 256, bias=eps6.ap[:, 0:1])
                em.op("dve", lambda e: e.reciprocal(out=rs.ap, in_=rs.ap), reads=[rs.b], writes=[rs.b])
                for c in range(2):
                    stt(cqn, cqn.ap[:, c, :], cq_sb.ap[:, c, :], nrm.ap[:, c:c + 1], rs.ap, ALU.mult, ALU.mult, [cq_sb.b, nrm.b, rs.b])
                p_ = pC[0]
                proj(p_, 2952, 128)
                copy("act", ckv_sb, ckv_sb.ap, p_.ap[:, 0:T], [p_.b])
                act(sqs[0], sqs[0].ap, p_.ap[:, 0:T], AF.Square, [p_.b])
                mm(pS, pS.ap[:, 0:T], ones_f.ap, sqs[0].ap, [ones_f.b, sqs[0].b], True, True)
                act(rs, rs.ap, pS.ap[:, 0:T], AF.Sqrt, [pS.b, eps6.b], scale=1.0 / 128, bias=eps6.ap[:, 0:1])
                em.op("dve", lambda e: e.reciprocal(out=rs.ap, in_=rs.ap), reads=[rs.b], writes=[rs.b])
                stt(ckvn, ckvn.ap, ckv_sb.ap, nrm.ap[:, 2:3], rs.ap, ALU.mult, ALU.mult, [ckv_sb.b, nrm.b, rs.b])
                for h in range(8):
                    i = rr[0] % 2
                    rr[0] += 1
                    a_, b_ = pA[i], pB[i]
                    for k in range(2):
                        mm(a_, a_.ap[0:96, 0:T], wuqb.ap[:, k, 96 * h:96 * h + 96], cqn.ap[:, k, :], [wuqb.b, cqn.b], k == 0, k == 1)
                    for k in range(2):
                        mm(b_, b_.ap[0:96, 0:T], wuqb.ap[:, k, 768 + 96 * h:768 + 96 * h + 96], cqn.ap[:, k, :], [wuqb.b, cqn.b], k == 0, k == 1)
                    tt("dve", t1[i], t1[i].ap[0:96, :], a_.ap[0:96, 0:T], c96.ap, ALU.mult, [a_.b, c96.b])
                    tt("dve", t2[i], t2[i].ap[0:96, :], b_.ap[0:96, 0:T], s96.ap, ALU.mult, [b_.b, s96.b])
                    tt("pool", mq_st, mq_st.ap[:, h, :], t1[i].ap[0:96, :], t2[i].ap[0:96, :], ALU.add, [t1[i].b, t2[i].b])
                p_ = pC[1]
                proj(p_, 3176, 96)
                tt("dve", bs, bs.ap, p_.ap[0:96, 0:T], s96.ap, ALU.mult, [p_.b, s96.b])
                for h in range(8):
                    i = rr[0] % 2
                    rr[0] += 1
                    a_ = pA[i]
                    for k in range(8):
                        mm(a_, a_.ap[0:96, 0:T], wAb.ap[:, k, 3080:3176], hb.ap[:, k, :], [wAb.b, hb.b], k == 0, False)
                    mm(a_, a_.ap[0:96, 0:T], wukb.ap[:, 0, 96 * h:96 * h + 96], ckvn.ap, [wukb.b, ckvn.b], False, True)
                    tt("dve", t1[i], t1[i].ap[0:96, :], a_.ap[0:96, 0:T], c96.ap, ALU.mult, [a_.b, c96.b])
                    tt("pool", mk_st, mk_st.ap[:, h, :], t1[i].ap[0:96, :], bs.ap, ALU.add, [t1[i].b, bs.b])
                for c in range(3):
                    p_ = pB[c % 2]
                    mm(p_, p_.ap[:, 0:512], ckvn.ap[:, c * 128:(c + 1) * 128], wuvb.ap[:, 0, :], [wuvb.b, ckvn.b], True, True)
                    copy("act", mv_st, mv_st.ap[:, c, :, 0:64], p_.ap[:, 0:512].rearrange("p (h d) -> p h d", d=64), [p_.b])
                o = outs
                em.dma("pool", "st_q", o["aq"].ap[:, :, n0:n0 + T].rearrange("n p t -> p n t"), aq_st.ap, reads=[aq_st.b], writes=[o["aq"].bufs[m]], nslots=4)
                em.dma("pool", "st_q", o["iq"].ap[:, :, n0:n0 + T].rearrange("n p t -> p n t"), iq_st.ap, reads=[iq_st.b], writes=[o["iq"].bufs[m]], nslots=4)
                em.dma("pool", "st_q", o["mq"].ap[:, :, n0:n0 + T].rearrange("n p t -> p n t"), mq_st.ap, reads=[mq_st.b], writes=[o["mq"].bufs[m]], nslots=4)
                for h in range(8):
                    em.dma("pool", "st_q", o["mk"][h].ap[:, n0:n0 + T], mk_st.ap[:, h, :], reads=[mk_st.b], writes=[o["mk"][h].bufs[m]], nslots=4)
                em.dma("pool", "st_k", o["ak"].ap[:, n0:n0 + T], ak_st.ap, reads=[ak_st.b], writes=[o["ak"].bufs[m]], nslots=4)
                em.dma("pool", "st_k", o["ik"].ap[:, n0:n0 + T], ik_st.ap, reads=[ik_st.b], writes=[o["ik"].bufs[m]], nslots=4)
                em.dma("pool", "st_k", o["av"].ap[n0:n0 + T, :].rearrange("(c p) f -> p c f", p=128), av_st.ap, reads=[av_st.b], writes=[o["av"].bufs[m]], nslots=4)
                for (v_, h0_, h1_) in o["mvc"]:
                    em.dma("pool", "st_k", v_.ap[n0:n0 + T, :].rearrange("(c p) (h d) -> p c h d", p=128, d=65),
                           mv_st.ap[:, :, h0_:h1_, :], reads=[mv_st.b], writes=[v_.bufs[m]], nslots=4)
                em.dma("pool", "st_k", o["iw"].ap[:, 3 * m:3 * m + 3, :], iw_st.ap, reads=[iw_st.b], writes=[o["iw"].bufs[m]], nslots=4)
            em.barrier()

    def finalize_heads(accs, o_st, hlist, accsb, rrow, bc_ps, bc_sb, neg1):
        for r, h in enumerate(hlist):
            a_ = accs[r]
            sb_ = accsb[r % len(accsb)]
            copy("act", sb_, sb_.ap[0:65, :], a_.ap[0:65, 0:T], [a_.b])
            act(rrow, rrow.ap[64:65, :], sb_.ap[64:65, :], AF.Ln, [sb_.b])
            act(rrow, rrow.ap[64:65, :], rrow.ap[64:65, :], AF.Exp, [rrow.b], scale=-1.0)
            mm(bc_ps, bc_ps.ap[0:64, 0:T], ones_f.ap[64:65, 0:64], rrow.ap[64:65, :], [ones_f.b, rrow.b], True, True)
            copy("act", bc_sb, bc_sb.ap, bc_ps.ap[0:64, 0:T], [bc_ps.b])
            tt("pool", o_st, o_st.ap[:, h, :], sb_.ap[0:64, :], bc_sb.ap, ALU.mult, [sb_.b, bc_sb.b])

    def stage_dsa(ins, oa_d, idxm_d, iw_d):
        with ExitStack() as es:
            def sb(name, shape, dt):
                return TL(es.enter_context(nc.sbuf_tensor(uq(name), list(shape), dt)).ap(), name)

            def ps(name):
                return TL(es.enter_context(nc.psum_tensor(uq(name), [128, 512], F32)).ap(), name)

            idxm = sb("idxm_sb", [128, 3, 2, 3, 128], F32)
            em.dma("sp", "ld_small", idxm.ap, idxm_d.ap, reads=[idxm_d.bufs[0]], writes=[idxm.b])
            ik_sb = sb("ik_sb", [128, 2, NT], BF16)
            ak_sb = sb("ak_sb", [128, 2, NT], BF16)
            av_sb = sb("av_sb", [128, 2, NBK, 130], BF16)
            iw_sb = sb("iw_sb", [128, NBK, 8], F32)
            aw_sb = sb("aw_sb", [128, NBK, 8], F32)
            sg_sb = sb("sg_sb", [128, NBK, 8], F32)
            for j in range(2):
                em.dma("sp", "ld_k", ik_sb.ap[:, j, :], ins["ik"].ap[j], reads=ins["ik"].bufs, writes=[ik_sb.b], nslots=4)
                em.dma("sp", "ld_k", ak_sb.ap[:, j, :], ins["ak"].ap[j], reads=ins["ak"].bufs, writes=[ak_sb.b], nslots=4)
                for i0 in range(0, NBK, 11):
                    em.dma("sp", "ld_k", av_sb.ap[:, j, i0:i0 + 11, :],
                           ins["av"].ap[j, i0 * 128:(i0 + 11) * 128, :].rearrange("(i p) c -> p i c", p=128),
                           reads=ins["av"].bufs, writes=[av_sb.b], nslots=4)
            em.dma("sp", "ld_k", iw_sb.ap, iw_d.ap, reads=iw_d.bufs, writes=[iw_sb.b], nslots=4)
            act(aw_sb, aw_sb.ap, iw_sb.ap, AF.Abs, [iw_sb.b])
            act(sg_sb, sg_sb.ap, iw_sb.ap, AF.Sign, [iw_sb.b])
            scs = [sb("sc0", [128, 2 * NT], F32), sb("sc1", [128, 2 * NT], F32)]
            junk = sb("junk", [128, 2 * NT], U8)
            maskT = sb("maskT", [128, NBLK, T], FP8)
            iqz = [sb("iqz0", [128, 4, T], BF16), sb("iqz1", [128, 4, T], BF16)]
            aqz = [sb("aqz0", [128, 4, T], BF16), sb("aqz1", [128, 4, T], BF16)]
            for z_ in (iqz, aqz):
                em.op("dve", lambda e: e.memset(z_[0].ap[64:128, :, :], 0.0), writes=[z_[0].b])
                em.op("dve", lambda e: e.memset(z_[1].ap[0:64, :, :], 0.0), writes=[z_[1].b])
            dg = [sb("dg0", [128, 8, 128], BF16), sb("dg1", [128, 8, 128], BF16)]
            rl = [sb("rl%d" % i, [128, 512], BF16) for i in range(4)]
            mch = [sb("mch0", [128, 1024], BF16), sb("mch1", [128, 1024], BF16)]
            pt = [sb("pt%d" % i, [128, T], BF16) for i in range(4)]
            accsb = [sb("accsb0", [65, T], F32), sb("accsb1", [65, T], F32)]
            neg1 = sb("neg1", [65, T], F32)
            em.op("pool", lambda e: e.memset(neg1.ap, -1.0), writes=[neg1.b])
            thrs = [sb("thr0", [128, 1], F32), sb("thr1", [128, 1], F32)]
            cnts = [sb("cnt0", [128, 1], F32), sb("cnt1", [128, 1], F32)]
            dds = [sb("dd0", [128, 1], F32), sb("dd1", [128, 1], F32)]
            rrow = sb("rrow", [65, T], F32)
            bc_sb = sb("bc_sb", [64, T], F32)
            o_st = sb("o_st", [64, 8, T], BF16)
            psx = [ps("psx0"), ps("psx1")]
            accs = [ps("acc%d" % r) for r in range(4)]
            bc_ps = ps("bc_ps")
            ptr = TL(es.enter_context(nc.psum_tensor(uq("ptr"), [128, 1024], BF16)).ap(), "ptr")
            xi = [0]
            ci = [0]
            mi = [0]
            LA = 2

            def idx_block(m, c):
                nI = 3 * m + 3
                ncol = 2 * nI * 128
                ib = 3 * m + c
                sc = scs[ib % 2]
                scv = sc.ap[:, 0:ncol].rearrange("p (j i t) -> p j i t", j=2, t=128)
                dg_ = dg[ib % 2]
                for h in range(8):
                    act(dg_, dg_.ap[:, h, :], ident_b.ap, AF.Identity, [ident_b.b, sg_sb.b], scale=sg_sb.ap[:, ib, h:h + 1])
                steps = []
                for j in range(2):
                    for i0 in range(0, nI, 4):
                        for h in range(8):
                            steps.append((j, i0, h))
                held = {}
                for si in range(len(steps) + LA):
                    if si < len(steps):
                        j, i0, h = steps[si]
                        cw = min(4, nI - i0) * 128
                        r, hf = h // 2, h % 2
                        p_ = psx[xi[0] % 2]
                        r_ = rl[xi[0] % 4]
                        xi[0] += 1
                        mm(p_, p_.ap[:, 0:cw], iqz[hf].ap[:, r, c * 128:(c + 1) * 128],
                           ik_sb.ap[:, j, i0 * 128:i0 * 128 + cw], [iqz[hf].b, ik_sb.b], True, True)
                        act(r_, r_.ap[:, 0:cw], p_.ap[:, 0:cw], AF.Relu, [p_.b, aw_sb.b], scale=aw_sb.ap[:, ib, h:h + 1])
                        held[si] = r_
                    if si >= LA:
                        j, i0, h = steps[si - LA]
                        cw = min(4, nI - i0) * 128
                        col0 = (j * nI + i0) * 128
                        r_ = held.pop(si - LA)
                        if h == 0:
                            ci[0] += 1
                        a_ = accs[2 + ci[0] % 2]
                        mm(a_, a_.ap[:, 0:cw], dg_.ap[:, h, :], r_.ap[:, 0:cw], [dg_.b, r_.b], h == 0, h == 7)
                        if h == 7:
                            copy("act", sc, sc.ap[:, col0:col0 + cw], a_.ap[:, 0:cw], [a_.b])

            def bisect_block(m, c):
                nI = 3 * m + 3
                ncol = 2 * nI * 128
                ib = 3 * m + c
                sc, thr, cnt, dd = scs[ib % 2], thrs[ib % 2], cnts[ib % 2], dds[ib % 2]
                scv = sc.ap[:, 0:ncol].rearrange("p (j i t) -> p j i t", j=2, t=128)
                tt("dve", sc, scv[:, :, nI - 3:nI, :], scv[:, :, nI - 3:nI, :], idxm.ap[:, c, :, :, :], ALU.add, [sc.b, idxm.b])
                em.op("dve", lambda e: e.memset(thr.ap, 3.0e-5), writes=[thr.b])
                for it in range(1, BIS_IT + 1):
                    step = BIS_R / (2 ** it)
                    ts("dve", junk, junk.ap[:, 0:ncol], sc.ap[:, 0:ncol], thr.ap[:, 0:1], ALU.is_ge, [sc.b, thr.b],
                       op1=ALU.add, accum=cnt.ap[:, 0:1], extra_w=[cnt.b])
                    ts("dve", dd, dd.ap, cnt.ap, 255.5, ALU.is_ge, [cnt.b], s2=2.0 * step, op1=ALU.mult)
                    dec = -step if it < BIS_IT else -2.0 * step
                    stt(thr, thr.ap, thr.ap, dec, dd.ap, ALU.add, ALU.add, [thr.b, dd.b])

            def mask_block(m, c):
                nI = 3 * m + 3
                ib = 3 * m + c
                sc, thr = scs[ib % 2], thrs[ib % 2]
                nkb = 2 * nI
                for k0 in range(0, nkb, 8):
                    nk = min(8, nkb - k0)
                    mc = mch[mi[0] % 2]
                    mi[0] += 1
                    ts("dve", mc, mc.ap[:, 0:nk * 128], sc.ap[:, k0 * 128:(k0 + nk) * 128], thr.ap[:, 0:1], ALU.is_ge, [sc.b, thr.b])
                    for q in range(nk):
                        em.op("pe", lambda e: e.transpose(out=ptr.ap[:, q * 128:(q + 1) * 128], in_=mc.ap[:, q * 128:(q + 1) * 128], identity=ident_b.ap),
                              reads=[mc.b, ident_b.b], writes=[ptr.b])
                    em.op("act", lambda e: e.activation(out=maskT.ap[:, k0:k0 + nk, c * 128:(c + 1) * 128],
                                                        in_=ptr.ap[:, 0:nk * 128].rearrange("p (k t) -> p k t", t=128),
                                                        func=AF.Identity, scale=240.0, bias=m240.ap[:, 0:1], saturate=False),
                          reads=[ptr.b, m240.b], writes=[maskT.b])

            def load_iq(m):
                n0 = m * T
                for hf_ in range(2):
                    pr_ = slice(64 * hf_, 64 * hf_ + 64)
                    em.dma("sp", "ld_q", iqz[hf_].ap[pr_, :, :], ins["iq"].ap[:, pr_, n0:n0 + T].rearrange("n p t -> p n t"),
                           reads=[ins["iq"].bufs[m]], writes=[iqz[hf_].b], nslots=4)

            def load_aq(m):
                n0 = m * T
                for hf_ in range(2):
                    pr_ = slice(64 * hf_, 64 * hf_ + 64)
                    em.dma("sp", "ld_q", aqz[hf_].ap[pr_, :, :], ins["aq"].ap[:, pr_, n0:n0 + T].rearrange("n p t -> p n t"),
                           reads=[ins["aq"].bufs[m]], writes=[aqz[hf_].b], nslots=4)

            def attention(m):
                n0 = m * T
                nI = 3 * m + 3
                nkb = 2 * nI
                for g in range(2):
                    for half in range(2):
                        rs_ = [2 * half, 2 * half + 1]
                        steps = [(kb, r) for kb in range(nkb) for r in rs_]
                        held = {}
                        for si in range(len(steps) + LA):
                            if si < len(steps):
                                kb, r = steps[si]
                                j, i = kb // nI, kb % nI
                                p_ = psx[xi[0] % 2]
                                e_ = pt[xi[0] % 4]
                                xi[0] += 1
                                mm(p_, p_.ap[:, 0:T], ak_sb.ap[:, j, i * 128:(i + 1) * 128], aqz[g].ap[:, r, :], [ak_sb.b, aqz[g].b], True, False)
                                mm(p_, p_.ap[:, 0:T], ident_b.ap, maskT.ap[:, kb, :], [ident_b.b, maskT.b], False, True)
                                act(e_, e_.ap, p_.ap[:, 0:T], AF.Exp, [p_.b], scale=0.125)
                                held[si] = e_
                            if si >= LA:
                                kb, r = steps[si - LA]
                                j, i = kb // nI, kb % nI
                                m_ = held.pop(si - LA)
                                a_ = accs[r % 2]
                                mm(a_, a_.ap[0:65, 0:T], av_sb.ap[:, j, i, 65 * g:65 * g + 65], m_.ap, [av_sb.b, m_.b], kb == 0, kb == nkb - 1)
                        finalize_heads(accs[0:2], o_st, [4 * g + r for r in rs_], accsb, rrow, bc_ps, bc_sb, neg1)
                em.dma("pool", "st_o", oa_d.ap[:, :, n0:n0 + T].rearrange("n p t -> p n t"), o_st.ap, reads=[o_st.b], writes=[oa_d.bufs[m]])

            load_iq(0)
            idx_block(0, 0)
            bisect_block(0, 0)
            idx_block(0, 1)
            mask_block(0, 0)
            bisect_block(0, 1)
            idx_block(0, 2)
            mask_block(0, 1)
            bisect_block(0, 2)
            if NTL > 1:
                load_iq(1)
                idx_block(1, 0)
            mask_block(0, 2)
            load_aq(0)
            for m in range(NTL):
                if m + 1 < NTL:
                    bisect_block(m + 1, 0)
                    idx_block(m + 1, 1)
                    bisect_block(m + 1, 1)
                attention(m)
                if m + 1 < NTL:
                    load_aq(m + 1)
                    mask_block(m + 1, 0)
                    idx_block(m + 1, 2)
                    mask_block(m + 1, 1)
                    bisect_block(m + 1, 2)
                    if m + 2 < NTL:
                        load_iq(m + 2)
                        idx_block(m + 2, 0)
                    mask_block(m + 1, 2)
            em.barrier()

    def stage_mla(ins, ob_d, cmT_d):
        with ExitStack() as es:
            def sb(name, shape, dt):
                return TL(es.enter_context(nc.sbuf_tensor(uq(name), list(shape), dt)).ap(), name)

            def ps(name):
                return TL(es.enter_context(nc.psum_tensor(uq(name), [128, 512], F32)).ap(), name)

            cmT = sb("cmT_sb", [128, 6, T], BF16)
            em.dma("sp", "ld_small", cmT.ap, cmT_d.ap, reads=[cmT_d.bufs[0]], writes=[cmT.b])
            mk_sb = sb("mk_sb", [96, 2, 4, NT], BF16)
            mv_sb = sb("mv_sb", [128, 2, NBK, 260], BF16)
            mq_sb = sb("mq_sb", [96, 4, T], BF16)
            pt = [sb("pt%d" % i, [128, T], BF16) for i in range(4)]
            pm = [sb("pm%d" % i, [128, T], BF16) for i in range(4)]
            accsb = [sb("accsb0", [65, T], F32), sb("accsb1", [65, T], F32)]
            neg1 = sb("neg1", [65, T], F32)
            em.op("pool", lambda e: e.memset(neg1.ap, -1.0), writes=[neg1.b])
            rrow = sb("rrow", [65, T], F32)
            bc_sb = sb("bc_sb", [64, T], F32)
            o_st = sb("o_st", [64, 8, T], BF16)
            psx = [ps("psx0"), ps("psx1"), ps("psx2")]
            accs = [ps("acc%d" % r) for r in range(4)]
            bc_ps = ps("bc_ps")
            xi = [0]
            LA = 2
            for hh in range(2):
                for j in range(2):
                    for r in range(4):
                        mkv = ins["mk"][4 * hh + r]
                        em.dma("sp", "ld_k", mk_sb.ap[:, j, r, :], mkv.ap[j], reads=mkv.bufs, writes=[mk_sb.b], nslots=4)
                    for (mvv, r0_, nr_) in ins["mvg"][hh]:
                        for i0 in range(0, NBK, 11):
                            em.dma("sp", "ld_k", mv_sb.ap[:, j, i0:i0 + 11, 65 * r0_:65 * (r0_ + nr_)],
                                   mvv.ap[j, i0 * 128:(i0 + 11) * 128, :].rearrange("(i p) c -> p i c", p=128),
                                   reads=mvv.bufs, writes=[mv_sb.b], nslots=4)
                for m in range(NTL):
                    n0 = m * T
                    nI = 3 * m + 3
                    nkb = 2 * nI
                    em.dma("sp", "ld_q", mq_sb.ap, ins["mq"].ap[4 * hh:4 * hh + 4, :, n0:n0 + T].rearrange("n p t -> p n t"),
                           reads=[ins["mq"].bufs[m]], writes=[mq_sb.b], nslots=4)
                    steps = [(kb, r) for kb in range(nkb) for r in range(4)]
                    held = {}
                    for si in range(len(steps) + LA):
                        if si < len(steps):
                            kb, r = steps[si]
                            j, i = kb // nI, kb % nI
                            zone = i >= nI - 3
                            p_ = psx[xi[0] % 3]
                            e_ = pt[xi[0] % 4]
                            m_ = pm[xi[0] % 4]
                            xi[0] += 1
                            mm(p_, p_.ap[:, 0:T], mk_sb.ap[:, j, r, i * 128:(i + 1) * 128], mq_sb.ap[:, r, :], [mk_sb.b, mq_sb.b], True, True)
                            act(e_, e_.ap, p_.ap[:, 0:T], AF.Exp, [p_.b], scale=float(96 ** -0.5))
                            if zone:
                                z = j * 3 + (i - (nI - 3))
                                tt("pool" if (xi[0] % 2) else "dve", m_, m_.ap, e_.ap, cmT.ap[:, z, :], ALU.mult, [e_.b, cmT.b])
                                held[si] = m_
                            else:
                                held[si] = e_
                        if si >= LA:
                            kb, r = steps[si - LA]
                            j, i = kb // nI, kb % nI
                            src = held.pop(si - LA)
                            mm(accs[r], accs[r].ap[0:65, 0:T], mv_sb.ap[:, j, i, 65 * r:65 * r + 65], src.ap, [mv_sb.b, src.b], kb == 0, kb == nkb - 1)
                    finalize_heads(accs, o_st, [r for r in range(4)], accsb, rrow, bc_ps, bc_sb, neg1)
                    em.dma("pool", "st_o", ob_d.ap[4 * hh:4 * hh + 4, :, n0:n0 + T].rearrange("n p t -> p n t"), o_st.ap[:, 0:4, :],
                           reads=[o_st.b], writes=[ob_d.bufs[m]])
                em.barrier()

    def stage_merge(h1_d, oa_d, ob_d, wg_d, wba_d, wbb_d, wout_d, lnp, li, hout_d):
        with ExitStack() as es:
            def sb(name, shape, dt):
                return TL(es.enter_context(nc.sbuf_tensor(uq(name), list(shape), dt)).ap(), name)

            def ps(name):
                return TL(es.enter_context(nc.psum_tensor(uq(name), [128, 512], F32)).ap(), name)

            wgb = sb("wgb", [128, 8, 2048], BF16)
            wbab = sb("wbab", [128, 4, D], BF16)
            wbbb = sb("wbbb", [128, 4, D], BF16)
            woutb = sb("woutb", [128, 8, D], BF16)
            stg = [sb("stg0", [128, 704], F32), sb("stg1", [128, 704], F32), sb("stg2", [128, 704], F32)]
            load_w(es, wg_d, wgb, 8, 2048, stg)
            load_w(es, wba_d, wbab, 4, D, stg)
            load_w(es, wbb_d, wbbb, 4, D, stg)
            load_w(es, wout_d, woutb, 8, D, stg)
            hin = sb("hin", [128, 8, T], F32)
            hb = sb("hb", [128, 8, T], BF16)
            oa = sb("oa", [128, 4, T], BF16)
            ob = sb("ob", [128, 4, T], BF16)
            mixed = sb("mixed", [128, 8, T], BF16)
            xr = [sb("xr%d" % j, [128, T], F32) for j in range(8)]
            sa = [sb("sa0", [128, T], F32), sb("sa1", [128, T], F32)]
            sbb = [sb("sbb0", [128, T], F32), sb("sbb1", [128, T], F32)]
            sq = [sb("sq0", [128, T], F32), sb("sq1", [128, T], F32)]
            tmp = [sb("tmp0", [128, T], F32), sb("tmp1", [128, T], F32)]
            st = {n: sb("ln_" + n, [128, T], F32) for n in ("mean", "msq", "var", "rstd", "bt")}
            pga, pgb, pba, pbb = ps("pga"), ps("pgb"), ps("pba"), ps("pbb")
            py = [ps("py0"), ps("py1")]
            ps1 = ps("ps1")
            ps2 = ps("ps2")
            for m in range(NTL):
                n0 = m * T
                em.dma("sp", "ld_h", hin.ap, h1_d.ap[:, n0:n0 + T].rearrange("(k p) t -> p k t", p=128), reads=[h1_d.bufs[m]], writes=[hin.b])
                for two in range(2):
                    pr_ = slice(64 * two, 64 * two + 64)
                    em.dma("sp", "ld_o", oa.ap[pr_, :, :], oa_d.ap[:, :, n0:n0 + T].rearrange("(q w) p t -> w p q t", w=2)[two],
                           reads=[oa_d.bufs[m]], writes=[oa.b], nslots=4)
                    em.dma("sp", "ld_o", ob.ap[pr_, :, :], ob_d.ap[:, :, n0:n0 + T].rearrange("(q w) p t -> w p q t", w=2)[two],
                           reads=[ob_d.bufs[m]], writes=[ob.b], nslots=4)
                copy("dve", hb, hb.ap[:, 0:4, :], hin.ap[:, 0:4, :], [hin.b])
                copy("pool", hb, hb.ap[:, 4:8, :], hin.ap[:, 4:8, :], [hin.b])
                for j in range(8):
                    cs = slice(j * 128, (j + 1) * 128)
                    for k in range(8):
                        mm(pga, pga.ap[:, 0:T], wgb.ap[:, k, cs], hb.ap[:, k, :], [wgb.b, hb.b], k == 0, k == 7)
                    for k in range(8):
                        mm(pgb, pgb.ap[:, 0:T], wgb.ap[:, k, 1024 + j * 128:1024 + (j + 1) * 128], hb.ap[:, k, :], [wgb.b, hb.b], k == 0, k == 7)
                    for k in range(4):
                        mm(pba, pba.ap[:, 0:T], wbab.ap[:, k, cs], oa.ap[:, k, :], [wbab.b, oa.b], k == 0, k == 3)
                    for k in range(4):
                        mm(pbb, pbb.ap[:, 0:T], wbbb.ap[:, k, cs], ob.ap[:, k, :], [wbbb.b, ob.b], k == 0, k == 3)
                    s1_, s2_ = sa[j % 2], sbb[j % 2]
                    act(s1_, s1_.ap, pga.ap[:, 0:T], AF.Sigmoid, [pga.b])
                    act(s2_, s2_.ap, pgb.ap[:, 0:T], AF.Sigmoid, [pgb.b])
                    tt("dve", s1_, s1_.ap, s1_.ap, pba.ap[:, 0:T], ALU.mult, [s1_.b, pba.b])
                    tt("dve", s2_, s2_.ap, s2_.ap, pbb.ap[:, 0:T], ALU.mult, [s2_.b, pbb.b])
                    tt("pool", mixed, mixed.ap[:, j, :], s1_.ap, s2_.ap, ALU.add, [s1_.b, s2_.b])
                for j in range(8):
                    y_ = py[j % 2]
                    for k in range(8):
                        mm(y_, y_.ap[:, 0:T], woutb.ap[:, k, j * 128:(j + 1) * 128], mixed.ap[:, k, :], [woutb.b, mixed.b], k == 0, k == 7)
                    act(xr[j], xr[j].ap, hin.ap[:, j, :], AF.Copy, [hin.b], scale=ALPHA)
                    stt(xr[j], xr[j].ap, y_.ap[:, 0:T], 1.0, xr[j].ap, ALU.mult, ALU.add, [y_.b, xr[j].b])
                layer_norm(xr, lnp, li, ps1, ps2, sq, tmp, st)
                for j in range(8):
                    em.dma("pool", "st_h", hout_d.ap[j * 128:(j + 1) * 128, n0:n0 + T], xr[j].ap,
                           reads=[xr[j].b], writes=[hout_d.bufs[m]], nslots=4)
            em.barrier()

    class V:
        def __init__(self, ap, bufs):
            self.ap = ap
            self.bufs = bufs

    def internal(name, shape, dt, n=NTL):
        return dram(name, shape, dt, "Internal", n)

    FMR = [224, 224, 192, 192, 192]
    TMC = [195, 195, 195, 65]
    MKLOC = [(0, 128), (1, 128), (2, 0), (2, 96), (3, 0), (3, 96), (4, 0), (4, 96)]
    MVLOC = [(0, 130), (1, 0), (1, 65), (1, 130), (2, 0), (2, 65), (2, 130), (3, 0)]

    def exchange(L, fm, tm):
        fa, ta = [None] * len(fm), [None] * len(tm)

        def xf(i):
            A = internal("fmA%d_%d" % (L, i), [2 * FMR[i], NT], BF16, 1)
            em.coll(fm[i].ap, A.ap, reads=fm[i].bufs, writes=A.bufs)
            fa[i] = V(A.ap.rearrange("(j r) t -> j r t", j=2), A.bufs)

        def xt(i):
            A = internal("tmA%d_%d" % (L, i), [2 * NT, TMC[i]], BF16, 1)
            em.coll(tm[i].ap, A.ap, reads=tm[i].bufs, writes=A.bufs)
            ta[i] = V(A.ap.rearrange("(j r) c -> j r c", j=2), A.bufs)

        xf(0)
        xf(1)
        xt(0)
        for i in range(2, len(fm)):
            xf(i)
        for i in range(1, len(tm)):
            xt(i)
        return {"ak": V(fa[0].ap[:, 0:128, :], fa[0].bufs), "ik": V(fa[1].ap[:, 0:128, :], fa[1].bufs),
                "mk": [V(fa[c].ap[:, r:r + 96, :], fa[c].bufs) for (c, r) in MKLOC],
                "av": V(ta[0].ap[:, :, 0:130], ta[0].bufs),
                "mvg": [[(V(ta[0].ap[:, :, 130:195], ta[0].bufs), 0, 1), (V(ta[1].ap[:, :, 0:195], ta[1].bufs), 1, 3)],
                        [(V(ta[2].ap[:, :, 0:195], ta[2].bufs), 0, 3), (V(ta[3].ap[:, :, 0:65], ta[3].bufs), 3, 1)]]}

    tabs = (din("c128", [128, NT], F32, 1), din("s128", [128, NT], F32, 1),
            din("c96", [96, NT], F32, 1), din("s96", [96, NT], F32, 1))
    idxm_d = din("idxm", [128, 3, 2, 3, 128], F32, 1)
    cmT_d = din("cmT", [128, 6, T], BF16, 1)
    hin = din("xT", [D, NT], F32)
    for L in range(DEPTH):
        lnp = small_in("lnp_%d" % L, [128, 48])
        nrm = small_in("nrm_%d" % L, [128, 3])
        w = dict(w13a=din("w13a_%d" % L, [D, 2 * DFF], F32, 1), w2a=din("w2a_%d" % L, [DFF, D], F32, 1),
                 wA=din("wA_%d" % L, [D, NA], F32, 1), wuq=din("wuq_%d" % L, [256, 1536], F32, 1),
                 wuk=din("wuk_%d" % L, [128, 768], F32, 1), wuv=din("wuv_%d" % L, [128, 512], F32, 1),
                 wg=din("wg_%d" % L, [D, 2048], F32, 1), wba=din("wba_%d" % L, [512, D], F32, 1),
                 wbb=din("wbb_%d" % L, [512, D], F32, 1), wout=din("wout_%d" % L, [D, D], F32, 1),
                 w13b=din("w13b_%d" % L, [D, 2 * DFF], F32, 1), w2b=din("w2b_%d" % L, [DFF, D], F32, 1))
        h1 = internal("h1_%d" % L, [D, NT], F32)
        stage_ffn(hin, h1, w["w13a"], w["w2a"], lnp, 0)
        fm = [internal("fm%d_%d" % (L, i), [r_, NT], BF16) for i, r_ in enumerate(FMR)]
        tm = [internal("tm%d_%d" % (L, i), [NT, c_], BF16) for i, c_ in enumerate(TMC)]
        q = {"aq": internal("aq%d" % L, [4, 128, NT], BF16), "iq": internal("iq%d" % L, [4, 128, NT], BF16),
             "mq": internal("mq%d" % L, [8, 96, NT], BF16)}
        iw = internal("iw%d" % L, [128, NBK, 8], F32)
        outs = dict(q)
        outs.update({"ak": V(fm[0].ap[0:128, :], fm[0].bufs), "ik": V(fm[1].ap[0:128, :], fm[1].bufs),
                     "mk": [V(fm[c].ap[r:r + 96, :], fm[c].bufs) for (c, r) in MKLOC],
                     "av": V(tm[0].ap[:, 0:130], tm[0].bufs),
                     "mvc": [(V(tm[0].ap[:, 130:195], tm[0].bufs), 0, 1), (V(tm[1].ap, tm[1].bufs), 1, 4),
                             (V(tm[2].ap, tm[2].bufs), 4, 7), (V(tm[3].ap, tm[3].bufs), 7, 8)],
                     "iw": iw})
        stage_proj(h1, w["wA"], w["wuq"], w["wuk"], w["wuv"], nrm, tabs, outs)
        ins = dict(q)
        ins.update(exchange(L, fm, tm))
        oa = internal("oa%d" % L, [8, 64, NT], BF16)
        ob = internal("ob%d" % L, [8, 64, NT], BF16)
        h2 = internal("h2_%d" % L, [D, NT], F32)
        stage_dsa(ins, oa, idxm_d, iw)
        stage_mla(ins, ob, cmT_d)
        stage_merge(h1, oa, ob, w["wg"], w["wba"], w["wbb"], w["wout"], lnp, 1, h2)
        h3 = dout("outT", [D, NT], F32) if L == DEPTH - 1 else internal("h3_%d" % L, [D, NT], F32)
        stage_ffn(h2, h3, w["w13b"], w["w2b"], lnp, 2)
        hin = h3
    em.barrier()
    gs.close()
    return nc, em


def _rope_perm64():
    p = np.arange(64)
    p[0:8] = np.arange(8, 16)
    p[8:16] = np.arange(0, 8)
    return p


def _prep_weights(inp, L):
    f = np.float32
    w_in = np.asarray(inp["w_in"][L], f)
    perm = _rope_perm64()
    a_q = w_in[:, 0:512].reshape(D, 8, 64)
    a_k = w_in[:, 512:640].reshape(D, 2, 64)
    a_v = w_in[:, 640:768]
    i_q = w_in[:, 768:1280].reshape(D, 8, 64)
    i_k = w_in[:, 1280:1344]
    i_w = w_in[:, 1344:1352]
    c_q = w_in[:, 1352:1608]
    c_kv = w_in[:, 1608:1736]
    k_r = w_in[:, 1736:1768]
    gates = w_in[:, 1768:3816]
    aq_pair = np.stack([np.concatenate([a_q[:, r], a_q[:, 4 + r]], 1) for r in range(4)], 1)
    aq_pairB = np.stack([np.concatenate([a_q[:, r][:, perm], a_q[:, 4 + r][:, perm]], 1) for r in range(4)], 1)
    akA = a_k.reshape(D, 128)
    akB = a_k[:, :, perm].reshape(D, 128)
    iqA = i_q.reshape(D, 512)
    iqB = i_q[:, :, perm].reshape(D, 512)
    ikA = np.concatenate([i_k, i_k], 1)
    ikB = np.concatenate([i_k[:, perm], i_k[:, perm]], 1)
    krA = np.zeros((D, 96), f)
    krA[:, 64:96] = k_r
    krB = np.zeros((D, 96), f)
    krB[:, 64:80] = k_r[:, 16:32]
    krB[:, 80:96] = k_r[:, 0:16]
    wA = np.concatenate([aq_pair.reshape(D, 512), aq_pairB.reshape(D, 512), akA, akB, a_v, iqA, iqB, ikA, ikB,
                         i_w, c_q, c_kv, krA, krB], 1)
    assert wA.shape[1] == NA, wA.shape
    w_uq = np.asarray(inp["mla_w_uq"][L], f).reshape(256, 8, 96)
    uqB = w_uq.copy()
    uqB[:, :, 64:80] = w_uq[:, :, 80:96]
    uqB[:, :, 80:96] = w_uq[:, :, 64:80]
    wuq = np.concatenate([w_uq.reshape(256, 768), uqB.reshape(256, 768)], 1)
    w_ukv = np.asarray(inp["mla_w_ukv"][L], f).reshape(128, 8, 128)
    wuk = np.zeros((128, 8, 96), f)
    wuk[:, :, 0:64] = w_ukv[:, :, 0:64]
    wuv = np.ascontiguousarray(w_ukv[:, :, 64:128]).reshape(128, 512)
    ln_g = np.asarray(inp["ln_g"][L], f)
    ln_b = np.asarray(inp["ln_b"][L], f)
    lnp = np.zeros((128, 48), f)
    for li in range(3):
        lnp[:, li * 16:li * 16 + 8] = ln_g[li].reshape(8, 128).T
        lnp[:, li * 16 + 8:li * 16 + 16] = ln_b[li].reshape(8, 128).T
    nrm = np.zeros((128, 3), f)
    nrm[:, 0:2] = np.asarray(inp["mla_q_norm"][L], f).reshape(2, 128).T
    nrm[:, 2] = np.asarray(inp["mla_kv_norm"][L], f)
    c = np.ascontiguousarray
    return {
        "w13a_%d" % L: c(np.asarray(inp["ffn1_w13"][L], f)), "w2a_%d" % L: c(np.asarray(inp["ffn1_w2"][L], f)),
        "wA_%d" % L: c(wA), "wuq_%d" % L: c(wuq), "wuk_%d" % L: c(wuk.reshape(128, 768)), "wuv_%d" % L: c(wuv),
        "wg_%d" % L: c(gates), "wba_%d" % L: c(np.asarray(inp["w_branch_a"][L], f)),
        "wbb_%d" % L: c(np.asarray(inp["w_branch_b"][L], f)), "wout_%d" % L: c(np.asarray(inp["w_out"][L], f)),
        "w13b_%d" % L: c(np.asarray(inp["ffn2_w13"][L], f)), "w2b_%d" % L: c(np.asarray(inp["ffn2_w2"][L], f)),
        "lnp_%d" % L: lnp, "nrm_%d" % L: nrm,
    }


def _tables(j):
    n = np.arange(NT)
    pos = ((2 * (n // 128) + j) * 128 + n % 128).astype(np.float32)
    inv16 = (THETA ** (-(np.arange(8, dtype=np.float32) * 2.0 / 16))).astype(np.float32)
    inv32 = (THETA ** (-(np.arange(16, dtype=np.float32) * 2.0 / 32))).astype(np.float32)
    a16 = pos[None, :] * inv16[:, None]
    a32 = pos[None, :] * inv32[:, None]
    c64 = np.ones((64, NT), np.float32)
    s64 = np.zeros((64, NT), np.float32)
    c64[0:8] = np.cos(a16)
    c64[8:16] = np.cos(a16)
    s64[0:8] = -np.sin(a16)
    s64[8:16] = np.sin(a16)
    c96 = np.ones((96, NT), np.float32)
    s96 = np.zeros((96, NT), np.float32)
    c96[64:80] = np.cos(a32)
    c96[80:96] = np.cos(a32)
    s96[64:80] = -np.sin(a32)
    s96[80:96] = np.sin(a32)
    return {"c128": np.concatenate([c64, c64], 0), "s128": np.concatenate([s64, s64], 0), "c96": c96, "s96": s96}


def _masks(j):
    idxm = np.zeros((128, 3, 2, 3, 128), np.float32)
    cmT = np.zeros((128, 6, T), np.float32)
    t = np.arange(128)
    for c in range(3):
        qbz = 2 * c + j
        for jp in range(2):
            for cp in range(3):
                kbz = 2 * cp + jp
                if kbz < qbz:
                    vis = np.ones((128, 128), bool)
                elif kbz == qbz:
                    vis = t[None, :] <= t[:, None]
                else:
                    vis = np.zeros((128, 128), bool)
                idxm[:, c, jp, cp, :] = np.where(vis, 0.0, NEGM)
                cmT[:, jp * 3 + cp, c * 128:(c + 1) * 128] = vis.T.astype(np.float32)
    return {"idxm": idxm, "cmT": cmT.astype(ml_dtypes.bfloat16)}


_PROG = []


def _prog():
    if not _PROG:
        _PROG.append(build_program()[0])
    return _PROG[0]


def kernel(x, meta_tokens, ln_g, ln_b, ffn1_w13, ffn1_w2, w_in, mla_q_norm, mla_kv_norm,
           mla_w_uq, mla_w_ukv, w_branch_a, w_branch_b, w_out, ffn2_w13, ffn2_w2):
    inp = dict(x=x, meta_tokens=meta_tokens, ln_g=ln_g, ln_b=ln_b, ffn1_w13=ffn1_w13, ffn1_w2=ffn1_w2, w_in=w_in,
               mla_q_norm=mla_q_norm, mla_kv_norm=mla_kv_norm, mla_w_uq=mla_w_uq, mla_w_ukv=mla_w_ukv,
               w_branch_a=w_branch_a, w_branch_b=w_branch_b, w_out=w_out, ffn2_w13=ffn2_w13, ffn2_w2=ffn2_w2)
    x = np.asarray(x, np.float32)
    meta = np.asarray(meta_tokens, np.float32)
    W = {}
    for L in range(DEPTH):
        W.update(_prep_weights(inp, L))
    tabs = [_tables(j) for j in range(2)]
    msk = [_masks(j) for j in range(2)]
    maps = []
    for c in range(8):
        b, j = c // 2, c % 2
        h0 = np.zeros((NBLK * 128, D), np.float32)
        h0[0:N_META] = meta
        h0[N_META:N_META + SEQ] = x[b]
        mine = h0.reshape(NBLK, 128, D)[j::2].reshape(NT, D)
        m = dict(W)
        m.update(tabs[j])
        m.update(msk[j])
        m["xT"] = np.ascontiguousarray(mine.T)
        maps.append(m)
    res = run_bass_kernel_spmd(_prog(), maps, core_ids=list(range(8))).results
    out = np.zeros((BATCH, SEQ, D), np.float32)
    for b in range(BATCH):
        full = np.zeros((NBLK, 128, D), np.float32)
        for j in range(2):
            full[j::2] = np.asarray(res[2 * b + j]["outT"]).T.reshape(NBK, 128, D)
        out[b] = full.reshape(NBLK * 128, D)[N_META:N_META + SEQ]
    return out
```
